# Optimizing a Trainium2 kernel written in Bass

```python
import math
import jax, jax.numpy as jnp
from jax import lax
import numpy as np

D_MODEL = 2048
BATCH = 32
SEQ = 256
DEPTH = 2
DEC_BATCH = 2
DEC_SEQ = 2048
PAST_LEN = 256

GRID_W = 64
D_MIX = D_MODEL
ATT_HEADS = 8
ATT_KV_HEADS = 2
GQA_GROUP = ATT_HEADS // ATT_KV_HEADS
HEAD_DIM = 128
ATT_WIDTH = ATT_HEADS * HEAD_DIM
KV_WIDTH = ATT_KV_HEADS * HEAD_DIM
WINDOW = 128
ATT_BLOCK = 128
ROPE_BASE = 10000.0
HG_HEADS = 4
HG_DK = 128
HG_DV = 128
HG_KW = HG_HEADS * HG_DK
HG_WIDTH = HG_HEADS * HG_DV
HG_CHUNK = 32
GATE_EPS = 1e-6
CV_WIDTH = D_MIX - ATT_WIDTH - HG_WIDTH
CONV_TAPS = 31
SPLITS = (ATT_WIDTH, KV_WIDTH, KV_WIDTH, ATT_WIDTH,
          HG_KW, HG_WIDTH, HG_KW, HG_KW, HG_WIDTH,
          CV_WIDTH, CV_WIDTH, CV_WIDTH)
SPLIT_POINTS = tuple(int(s) for s in np.cumsum(SPLITS)[:-1])
D_IN = int(sum(SPLITS))
ALPHA = float((2 * DEPTH) ** 0.25)
BETA = float((8 * DEPTH) ** -0.25)
NEG = -1e30

kernel_name = "hymba_hgrn2_swa_conformer_dit_step"

F32 = jnp.float32


def layer_norm(x, w, b, eps=1e-5):
    xf = x.astype(F32)
    mu = jnp.mean(xf, axis=-1, keepdims=True)
    var = jnp.mean(jnp.square(xf - mu), axis=-1, keepdims=True)
    return ((xf - mu) * lax.rsqrt(var + eps) * w.astype(F32) + b.astype(F32)).astype(x.dtype)


def rms_norm(x, w, eps=1e-6):
    xf = x.astype(F32)
    return xf * lax.rsqrt(jnp.mean(xf * xf, axis=-1, keepdims=True) + eps) * w.astype(F32)


def axial_rope(x):
    seq_len = x.shape[1]
    rows = seq_len // GRID_W
    row = jnp.broadcast_to(jnp.arange(rows)[:, None], (rows, GRID_W)).reshape(-1)
    col = jnp.broadcast_to(jnp.arange(GRID_W)[None, :], (rows, GRID_W)).reshape(-1)
    half = HEAD_DIM // 2
    nf = half // 2
    inv = ROPE_BASE ** (-jnp.arange(nf, dtype=F32) / nf)

    def rot(xp, pos):
        ang = pos.astype(F32)[:, None] * inv[None, :]
        cos = jnp.cos(ang)[None, :, None, :]
        sin = jnp.sin(ang)[None, :, None, :]
        x1 = xp[..., :nf].astype(F32)
        x2 = xp[..., nf:].astype(F32)
        return jnp.concatenate([x1 * cos - x2 * sin, x2 * cos + x1 * sin], axis=-1)

    return jnp.concatenate([rot(x[..., :half], row), rot(x[..., half:], col)], axis=-1).astype(x.dtype)


def sink_softmax(logits, sink):
    s = jnp.broadcast_to(sink[:, :, None, None], logits.shape[:-1] + (1,))
    p = jax.nn.softmax(jnp.concatenate([logits, s], axis=-1), axis=-1)
    return p[..., :-1]


def context_attention(q, k, v, sink):
    B, L, H, D = q.shape
    nq = L // ATT_BLOCK
    scale = D ** -0.5
    qb = q.reshape(B, nq, ATT_BLOCK, ATT_KV_HEADS, GQA_GROUP, D).transpose(1, 0, 2, 3, 4, 5)

    def one_block(qblk):
        s = jnp.einsum('bqkgd,bckd->bkgqc', qblk, k).astype(F32) * scale
        p = sink_softmax(s, sink).astype(v.dtype)
        return jnp.einsum('bkgqc,bckd->bqkgd', p, v)

    o = lax.map(one_block, qb)
    return o.transpose(1, 0, 2, 3, 4, 5).reshape(B, L, H * D)


def latent_attention(q, k, v, ck, cv, sink):
    B, S, H, D = q.shape
    nb = S // ATT_BLOCK
    scale = D ** -0.5
    qb = q.reshape(B, nb, ATT_BLOCK, ATT_KV_HEADS, GQA_GROUP, D)
    pad = ((0, 0), (ATT_BLOCK, ATT_BLOCK), (0, 0), (0, 0))
    kb = jnp.pad(k, pad).reshape(B, nb + 2, ATT_BLOCK, ATT_KV_HEADS, D)
    vb = jnp.pad(v, pad).reshape(B, nb + 2, ATT_BLOCK, ATT_KV_HEADS, D)
    kl = jnp.concatenate([kb[:, :nb], kb[:, 1:nb + 1], kb[:, 2:]], axis=2)
    vl = jnp.concatenate([vb[:, :nb], vb[:, 1:nb + 1], vb[:, 2:]], axis=2)
    qpos = jnp.arange(S).reshape(nb, ATT_BLOCK)
    kpos = jnp.arange(nb)[:, None] * ATT_BLOCK - ATT_BLOCK + jnp.arange(3 * ATT_BLOCK)[None, :]
    mask = ((jnp.abs(qpos[:, :, None] - kpos[:, None, :]) <= WINDOW)
            & (kpos[:, None, :] >= 0) & (kpos[:, None, :] < S))
    s_loc = jnp.einsum('bnqkgd,bnrkd->bnkgqr', qb, kl).astype(F32) * scale
    s_loc = jnp.where(mask[None, :, None, None, :, :], s_loc, NEG)
    s_ctx = jnp.einsum('bnqkgd,bckd->bnkgqc', qb, ck.astype(q.dtype)).astype(F32) * scale
    p = sink_softmax(jnp.concatenate([s_loc, s_ctx], axis=-1), sink).astype(v.dtype)
    r = 3 * ATT_BLOCK
    o = (jnp.einsum('bnkgqr,bnrkd->bnqkgd', p[..., :r], vl)
         + jnp.einsum('bnkgqc,bckd->bnqkgd', p[..., r:], cv.astype(v.dtype)))
    return o.reshape(B, S, H * D)


def hgrn2_chunk_scan(q, k, v, logf, s0):
    B, L, H, DK = q.shape
    DV = v.shape[-1]
    n = L // HG_CHUNK

    def chunks(t):
        return t.reshape(B, n, HG_CHUNK, H, t.shape[-1]).transpose(1, 0, 3, 2, 4)

    causal = jnp.tril(jnp.ones((HG_CHUNK, HG_CHUNK), dtype=bool))

    def step(S, inp):
        qc, kc, vc, gc = inp
        G = jnp.cumsum(gc, axis=2)
        inter = jnp.einsum('bhtd,bhde->bhte', qc * jnp.exp(G), S)
        diff = jnp.where(causal[None, None, :, :, None],
                         G[:, :, :, None, :] - G[:, :, None, :, :], NEG)
        A = jnp.einsum('bhtd,bhtsd,bhsd->bhts', qc, jnp.exp(diff), kc)
        intra = jnp.einsum('bhts,bhse->bhte', A, vc)
        g_last = G[:, :, -1:, :]
        S_new = (jnp.exp(g_last[:, :, 0, :])[..., None] * S
                 + jnp.einsum('bhsd,bhse->bhde', kc * jnp.exp(g_last - G), vc))
        return S_new, inter + intra

    s_fin, o = lax.scan(step, s0.astype(F32), (chunks(q), chunks(k), chunks(v), chunks(logf)))
    o = o.transpose(1, 0, 3, 2, 4).reshape(B, L, H, DV)
    return o, s_fin


def hgrn2_branch(hq, hi, hf_f, hf_b, lb_f, lb_b, norm_w, s0_f, s0_b):
    B, L, _ = hq.shape
    q = jax.nn.silu(hq.astype(F32)).reshape(B, L, HG_HEADS, HG_DK) * (HG_DK ** -0.5)
    v = hi.astype(F32).reshape(B, L, HG_HEADS, HG_DV)

    def gates(hf, lb):
        lb = lb.astype(F32).reshape(HG_HEADS, HG_DK)
        one_minus_f = (1.0 - lb) * jax.nn.sigmoid(-hf.astype(F32).reshape(B, L, HG_HEADS, HG_DK))
        logf = jnp.log1p(-jnp.minimum(one_minus_f, 1.0 - GATE_EPS))
        return one_minus_f, logf

    k_f, g_f = gates(hf_f, lb_f)
    k_b, g_b = gates(hf_b, lb_b)
    o_f, s_f = hgrn2_chunk_scan(q, k_f, v, g_f, s0_f)
    o_b, s_b = hgrn2_chunk_scan(q[:, ::-1], k_b[:, ::-1], v[:, ::-1], g_b[:, ::-1], s0_b)
    o = o_f + o_b[:, ::-1]
    o = rms_norm(o, norm_w.reshape(HG_HEADS, HG_DV))
    return o.reshape(B, L, HG_WIDTH).astype(hq.dtype), jnp.stack([s_f, s_b], axis=1)


def conv_branch(ha, hb, w, b, ln_w, ln_b):
    u = ha * jax.nn.sigmoid(hb)
    y = lax.conv_general_dilated(u, w[:, None, :].astype(u.dtype), window_strides=(1,),
                                 padding=[(CONV_TAPS // 2, CONV_TAPS // 2)],
                                 dimension_numbers=('NWC', 'WIO', 'NWC'),
                                 feature_group_count=CV_WIDTH) + b.astype(u.dtype)
    return jax.nn.silu(layer_norm(y, ln_w, ln_b))


def modulation(cond, ada_w_l, ada_b_l):
    m = jax.nn.silu(cond) @ ada_w_l + ada_b_l
    return jnp.split(m, 3, axis=-1)


def trunk_layer(x, shift, scale, gate, w_in_l, w_out_l, ln_w_l, ln_b_l, sink_l, lb_l, hg_norm_l,
                conv_w_l, conv_b_l, conv_ln_w_l, conv_ln_b_l, ctx):
    B, L, _ = x.shape
    h = x * (1 + scale) + shift
    z = h @ w_in_l
    aq, ak, av, ag, hq, hi, hff, hfb, hgate, ca, cb, cgate = jnp.split(z, SPLIT_POINTS, axis=-1)
    q = aq.reshape(B, L, ATT_HEADS, HEAD_DIM)
    k = ak.reshape(B, L, ATT_KV_HEADS, HEAD_DIM)
    v = av.reshape(B, L, ATT_KV_HEADS, HEAD_DIM)
    sink = sink_l.astype(F32).reshape(ATT_KV_HEADS, GQA_GROUP)
    if ctx is None:
        att = context_attention(q, k, v, sink)
        s0 = jnp.zeros((B, 2, HG_HEADS, HG_DK, HG_DV), F32)
    else:
        ctx_k, ctx_v, s0 = ctx
        att = latent_attention(axial_rope(q), axial_rope(k), v, ctx_k, ctx_v, sink)
    hg_out, s_fin = hgrn2_branch(hq, hi, hff, hfb, lb_l[0], lb_l[1], hg_norm_l, s0[:, 0], s0[:, 1])
    cv_out = conv_branch(ca, cb, conv_w_l, conv_b_l, conv_ln_w_l, conv_ln_b_l)
    mix = jnp.concatenate([att * jax.nn.silu(ag), hg_out * jax.nn.silu(hgate),
                           cv_out * jax.nn.silu(cgate)], axis=-1)
    out = mix @ w_out_l
    x_new = layer_norm(ALPHA * x + gate * out, ln_w_l, ln_b_l)
    return x_new, k, v, s_fin.astype(x.dtype)


def setup_inputs(seed: int = 0) -> dict:
    key = jax.random.key(seed)
    ks = jax.random.split(key, 20)

    def nrm(k, shape, s):
        return s * jax.random.normal(k, shape, F32)

    col_scales = (1.0, 1.0, BETA, 1.0, 1.0, BETA, 1.0, 1.0, 1.0, BETA, 1.0, 1.0)
    col_scale = jnp.concatenate([jnp.full((n,), s, F32) for n, s in zip(SPLITS, col_scales)])
    return {
        "x_prompt": nrm(ks[0], (BATCH, SEQ, D_MODEL), 1.0),
        "x_sample": nrm(ks[1], (DEC_BATCH, DEC_SEQ, D_MODEL), 1.0),
        "cache_k": nrm(ks[2], (DEC_BATCH, DEPTH, PAST_LEN, ATT_KV_HEADS, HEAD_DIM), 1.0),
        "cache_v": nrm(ks[3], (DEC_BATCH, DEPTH, PAST_LEN, ATT_KV_HEADS, HEAD_DIM), 1.0),
        "state_hgrn": nrm(ks[4], (DEC_BATCH, DEPTH, 2, HG_HEADS, HG_DK, HG_DV), 0.5),
        "c": nrm(ks[5], (DEC_BATCH, D_MODEL), 1.0),
        "c_ctx": nrm(ks[6], (D_MODEL,), 1.0),
        "ada_w": nrm(ks[7], (DEPTH, D_MODEL, 3 * D_MODEL), 0.5 * D_MODEL ** -0.5),
        "ada_b": nrm(ks[8], (DEPTH, 3 * D_MODEL), 0.02),
        "w_in": nrm(ks[9], (DEPTH, D_MODEL, D_IN), D_MODEL ** -0.5) * col_scale,
        "w_out": nrm(ks[10], (DEPTH, D_MIX, D_MODEL), BETA * D_MIX ** -0.5),
        "attn_sink": nrm(ks[11], (DEPTH, ATT_HEADS), 0.5),
        "hg_lower_bounds": nrm(ks[12], (2, DEPTH, HG_KW), 0.5),
        "hg_norm_w": 1.0 + nrm(ks[13], (DEPTH, HG_WIDTH), 0.02),
        "conv_w": nrm(ks[14], (DEPTH, CONV_TAPS, CV_WIDTH), CONV_TAPS ** -0.5),
        "conv_b": nrm(ks[15], (DEPTH, CV_WIDTH), 0.02),
        "conv_ln_w": 1.0 + nrm(ks[16], (DEPTH, CV_WIDTH), 0.02),
        "conv_ln_b": nrm(ks[17], (DEPTH, CV_WIDTH), 0.02),
        "ln_w": 1.0 + nrm(ks[18], (DEPTH, D_MODEL), 0.02),
        "ln_b": nrm(ks[19], (DEPTH, D_MODEL), 0.02),
    }


def reference(x_prompt, x_sample, cache_k, cache_v, state_hgrn, c, c_ctx, ada_w, ada_b, w_in, w_out,
              attn_sink, hg_lower_bounds, hg_norm_w, conv_w, conv_b, conv_ln_w, conv_ln_b, ln_w, ln_b):
    lbs = jax.nn.softmax(hg_lower_bounds.astype(F32), axis=1)
    lbs = jnp.cumsum(lbs, axis=1) - lbs[:, :1]

    y_prompt = x_prompt
    ks_, vs_, ss_ = [], [], []
    for l in range(DEPTH):
        shift, scale, gate = modulation(c_ctx, ada_w[l], ada_b[l])
        y_prompt, k_l, v_l, s_l = trunk_layer(
            y_prompt, shift, scale, gate, w_in[l], w_out[l], ln_w[l], ln_b[l], attn_sink[l], lbs[:, l],
            hg_norm_w[l], conv_w[l], conv_b[l], conv_ln_w[l], conv_ln_b[l], None)
        ks_.append(k_l)
        vs_.append(v_l)
        ss_.append(s_l)
    new_cache_k = jnp.stack(ks_, axis=1)
    new_cache_v = jnp.stack(vs_, axis=1)
    new_state_hgrn = jnp.stack(ss_, axis=1)

    y_sample = x_sample
    for l in range(DEPTH):
        shift, scale, gate = modulation(c, ada_w[l], ada_b[l])
        y_sample, _, _, _ = trunk_layer(
            y_sample, shift[:, None, :], scale[:, None, :], gate[:, None, :], w_in[l], w_out[l],
            ln_w[l], ln_b[l], attn_sink[l], lbs[:, l], hg_norm_w[l], conv_w[l], conv_b[l],
            conv_ln_w[l], conv_ln_b[l], (cache_k[:, l], cache_v[:, l], state_hgrn[:, l]))

    return (y_prompt, y_sample, new_cache_k, new_cache_v, new_state_hgrn)
```

```python
import math
import numpy as np
import concourse.bass as bass
import concourse.mybir as mybir
from concourse.bass_utils import run_bass_kernel_spmd

F32 = mybir.dt.float32
BF16 = mybir.dt.bfloat16
ALU = mybir.AluOpType
AF = mybir.ActivationFunctionType

PE, ACT, DVE, POOL, SP = "pe", "act", "dve", "pool", "sp"
SAME_ENG_SYNC = True

D = 2048
KC = 16
DIN = 6656
ALPHA = float(4 ** 0.25)
C_AQ, C_AK, C_AV, C_AG = 0, 1024, 1280, 1536
C_HQ, C_HI, C_HFF, C_HFB, C_HG = 2560, 3072, 3584, 4096, 4608
C_CA, C_CB, C_CG = 5120, 5632, 6144
SLABW = 512


class StopBuild(Exception):
    pass


class Prog:
    def __init__(self, nc):
        self.nc = nc
        self.eng = {PE: nc.tensor, ACT: nc.scalar, DVE: nc.vector, POOL: nc.gpsimd, SP: nc.sync}
        self.ops = {e: [] for e in self.eng}
        self.sems = {}
        self.cnt = {}
        self.seen = {e: {} for e in self.eng}
        self.last_w = {}
        self.readers = {}
        self.nops = 0

    def _sem(self, key):
        if key not in self.sems:
            name = "s_" + ("_".join(str(k) for k in key) if isinstance(key, tuple) else str(key))
            self.sems[key] = self.nc.alloc_semaphore(name)
            self.cnt[key] = 0
        return self.sems[key]

    def _deps(self, reads, writes):
        deps = {}

        def add(tok):
            k, v = tok
            if deps.get(k, 0) < v:
                deps[k] = v
        for r in reads:
            if r in self.last_w:
                add(self.last_w[r])
        for w in writes:
            if w in self.last_w:
                add(self.last_w[w])
            for tok in self.readers.get(w, ()):
                add(tok)
        return deps

    def _commit(self, tok, reads, writes):
        for r in reads:
            self.readers.setdefault(r, []).append(tok)
        for w in writes:
            self.last_w[w] = tok
            self.readers[w] = []

    def _waits(self, eng, deps, own_key, same_sync):
        waits = []
        for k, v in deps.items():
            if k == own_key and not same_sync:
                continue
            if self.seen[eng].get(k, 0) >= v:
                continue
            self.seen[eng][k] = v
            waits.append((self._sem(k), v))
        return waits

    def op(self, eng, fn, reads=(), writes=(), same_sync=None):
        if same_sync is None:
            same_sync = SAME_ENG_SYNC and eng != PE
        reads = list(reads)
        writes = list(writes)
        deps = self._deps(reads, writes)
        waits = self._waits(eng, deps, eng, same_sync)
        sem = self._sem(eng)
        self.cnt[eng] += 1
        tok = (eng, self.cnt[eng])
        self.ops[eng].append((waits, fn, (sem, 1)))
        self._commit(tok, reads, writes)
        self.nops += 1
        return tok

    def dma(self, queue, ch, fn, reads=(), writes=()):
        key = ("dma", ch)
        reads = list(reads)
        writes = list(writes)
        deps = self._deps(reads, writes)
        sem = self._sem(key)
        if self.cnt[key] > 0:
            v = 16 * self.cnt[key]
            if deps.get(key, 0) < v:
                deps[key] = v
        waits = self._waits(queue, deps, None, True)
        self.cnt[key] += 1
        tok = (key, 16 * self.cnt[key])
        self.ops[queue].append((waits, fn, (sem, 16)))
        self._commit(tok, reads, writes)
        self.nops += 1
        return tok

    def barrier(self, keep_prefix=("slab",), skip_ch_prefix=("w",)):
        toks = {}
        for k, c in self.cnt.items():
            if c == 0:
                continue
            if isinstance(k, tuple):
                if str(k[1]).startswith(skip_ch_prefix):
                    continue
                toks[k] = 16 * c
            else:
                toks[k] = c
        for e in self.eng:
            waits = self._waits(e, dict(toks), None, True)
            if waits:
                self.ops[e].append((waits, None, None))

        def keep(r):
            name = r[0] if isinstance(r, tuple) else r
            return str(name).startswith(keep_prefix)
        self.last_w = {r: t for r, t in self.last_w.items() if keep(r)}
        self.readers = {r: t for r, t in self.readers.items() if keep(r)}

    def finish(self):
        toks = {}
        for k, c in self.cnt.items():
            if c == 0:
                continue
            toks[k] = 16 * c if isinstance(k, tuple) else c
        for e in self.eng:
            waits = self._waits(e, dict(toks), None, True)
            if waits:
                self.ops[e].append((waits, None, None))

    def emit(self):
        nc = self.nc
        with nc.Block() as block:
            def mk(e):
                def body(engine):
                    for waits, fn, inc in self.ops[e]:
                        for sem, v in waits:
                            engine.wait_ge(sem, v)
                        if fn is not None:
                            ins = fn(engine)
                            if inc is not None:
                                ins.then_inc(inc[0], inc[1])
                return body
            block.sync(mk(SP))
            block.tensor(mk(PE))
            block.scalar(mk(ACT))
            block.vector(mk(DVE))
            block.gpsimd(mk(POOL))


def build(NB, dbg=False, layers=2, stop_after=None):
    T = NB * 256
    NT = T // 128
    NG = T // 512
    NC = T // 32
    NQB = NT
    nc = bass.Bass("TRN2", target_bir_lowering=False)
    P = Prog(nc)

    def din(name, shape, dt=F32):
        return nc.dram_tensor(name, list(shape), dt, kind="ExternalInput").ap()

    def dout(name, shape, dt=F32, internal=False):
        return nc.dram_tensor(name, list(shape), dt, kind=("Internal" if internal else "ExternalOutput")).ap()

    x_d = din("x", [T, D])
    cond_d = din("cond", [128, 16])
    adaw_d = din("ada_w", [2, D, 6144])
    adabfm_d = din("adab_fm", [128, 2, 32])
    adabg_d = din("adab_g", [2, 2048])
    win_d = din("w_in", [2, D, DIN])
    wout_d = din("w_out", [2, D, D])
    ck_d = din("ck", [2, 256, 256])
    cv_d = din("cv", [2, 256, 256])
    s0_d = din("s0", [2, 2, 4, 128, 128])
    sink_d = din("sinkrep", [128, 2, 8])
    lbraw_d = din("lbraw", [128, 2, 2, 4])
    hgnw_d = din("hgnw", [128, 2, 4])
    convw_d = din("convw", [128, 2, 4, 31])
    convp_d = din("convp", [128, 3, 2, 4])
    lnw_d = din("lnw", [2, 2048])
    lnb_d = din("lnb", [2, 2048])
    cos_d = din("cosT", [128, T])
    sin_d = din("sinT", [128, T])
    rm_d = din("Rm", [128, 128])
    amask_d = din("amask", [128, NQB, 2, 128])
    ctxones_d = din("ctxones", [128, 128])
    flags_d = din("flags", [128, 4, NB])
    tri_d = din("tri", [128, 2, 32])

    y_d = dout("y", [T, D])
    ok_d = dout("ok", [2, T, 256])
    ov_d = dout("ov", [2, T, 256])
    ost_d = dout("ost", [2, NB, 2, 4, 128, 128])
    x1_d = dout("x1s", [T, D], internal=not dbg)
    mix_d = dout("mixs", [16, 128, T], BF16, internal=not dbg)

    ARENA = 176 * 1024
    arena = nc.alloc_sbuf_tensor("arena", [128, ARENA // 4], F32)
    top = [0]
    maxtop = [0]

    def alloc(shape, dt=F32, parts=None):
        esz = 4 if dt == F32 else 2
        n = 1
        for s in shape[1:]:
            n *= s
        nb = (n * esz + 31) // 32 * 32
        off = top[0]
        top[0] += nb
        maxtop[0] = max(maxtop[0], top[0])
        assert top[0] <= ARENA, f"SBUF arena overflow {top[0]}"
        p = shape[0]
        ap = arena[0:p, off // 4:(off + nb) // 4]
        if dt != F32:
            ap = ap.bitcast(dt)
        ap = ap[:, 0:n]
        if len(shape) == 3:
            ap = ap.rearrange("p (a b) -> p a b", a=shape[1])
        elif len(shape) == 4:
            ap = ap.rearrange("p (a b c) -> p a b c", a=shape[1], b=shape[2])
        return ap

    ps = [nc.alloc_psum_tensor(f"ps{i}", [128, 512], F32) for i in range(8)]

    def PSR(i):
        return ("ps", i)

    def mm(out, lhsT, rhs, start, stop, reads, writes):
        P.op(PE, lambda e: e.matmul(out, lhsT, rhs, start=start, stop=stop), reads, writes)

    def tr(out, in_, ident, reads, writes):
        P.op(PE, lambda e: e.transpose(out, in_, ident), reads, writes)

    def act(out, in_, func, reads, writes, bias=None, scale=None):
        kw = {}
        if bias is not None:
            kw["bias"] = bias
        if scale is not None:
            kw["scale"] = scale
        P.op(ACT, lambda e: e.activation(out=out, in_=in_, func=func, **kw), reads, writes)

    def tt(out, in0, in1, op, reads, writes, eng=DVE):
        P.op(eng, lambda e: e.tensor_tensor(out=out, in0=in0, in1=in1, op=op), reads, writes)

    def ts(out, in0, s1, s2, op0, op1, reads, writes, eng=DVE):
        if op1 is None:
            P.op(eng, lambda e: e.tensor_scalar(out=out, in0=in0, scalar1=s1, scalar2=None, op0=op0), reads, writes)
        else:
            P.op(eng, lambda e: e.tensor_scalar(out=out, in0=in0, scalar1=s1, scalar2=s2, op0=op0, op1=op1), reads, writes)

    def stt(out, in0, scalar, in1, op0, op1, reads, writes):
        P.op(DVE, lambda e: e.scalar_tensor_tensor(out=out, in0=in0, scalar=scalar, in1=in1, op0=op0, op1=op1), reads, writes)

    def cp(out, in_, reads, writes, eng=DVE):
        P.op(eng, lambda e: e.tensor_copy(out, in_), reads, writes)

    def recip(out, in_, reads, writes):
        P.op(DVE, lambda e: e.reciprocal(out, in_), reads, writes)

    def memset(ap, val, writes, eng=DVE):
        P.op(eng, lambda e: e.memset(ap, val), (), writes)

    def ld(ch, out, in_, writes, reads=()):
        return P.dma(SP, ch, lambda e: e.dma_start(out=out, in_=in_), reads, writes)

    def st(ch, out, in_, reads, writes=()):
        return P.dma(SP, ch, lambda e: e.dma_start(out=out, in_=in_), reads, writes)

    identf = alloc([128, 128])
    identb = alloc([128, 128], BF16)
    ones512 = alloc([128, 128])
    ones1 = alloc([128, 128])
    onesb = alloc([128, 128], BF16)
    ctxones = alloc([128, 128], BF16)
    ctxof = alloc([128, 128])
    tri = alloc([128, 2, 32])
    rm = alloc([128, 128])
    zc = alloc([128, 1])
    flags = alloc([128, 4, NB])
    modfm = alloc([128, 2, 32])
    lbraw = alloc([128, 2, 2, 4])
    lbv = alloc([128, 2, 2, 4])
    hgnw = alloc([128, 2, 4])
    convw = alloc([128, 2, 4, 31])
    convp = alloc([128, 3, 2, 4])
    cond = alloc([128, 16])
    sc_bf = alloc([128, 16], BF16)
    adabfm = alloc([128, 2, 32])
    const_top = top[0]

    memset(identf, 0.0, ["identf"])
    P.op(POOL, lambda e: e.affine_select(out=identf, in_=identf, pattern=[[-1, 128]], compare_op=ALU.not_equal,
                                         fill=1.0, base=0, channel_multiplier=1), ["identf"], ["identf"])
    cp(identb, identf, ["identf"], ["identb"])
    memset(ones512, 1.0 / 512.0, ["ones512"])
    memset(ones1, 1.0, ["ones1"])
    memset(onesb, 1.0, ["onesb"])
    memset(zc, 0.0, ["zc"])
    ld("c0", ctxof, ctxones_d[:, :], ["ctxof"])
    cp(ctxones, ctxof, ["ctxof"], ["ctxones"])
    ld("c1", tri, tri_d[:, :, :], ["tri"])
    ld("c2", rm, rm_d[:, :], ["rm"])
    ld("c3", flags, flags_d[:, :, :], ["flags"])
    ld("c4", lbraw, lbraw_d[:, :, :, :], ["lbraw"])
    ld("c5", hgnw, hgnw_d[:, :, :], ["hgnw"])
    ld("c6", convw, convw_d[:, :, :, :], ["convw"])
    ld("c7", convp, convp_d[:, :, :, :], ["convp"])
    ld("c0", cond, cond_d[:, :], ["cond"])
    ld("c1", adabfm, adabfm_d[:, :, :], ["adabfm"])
    act(sc_bf, cond, AF.Silu, ["cond"], ["sc_bf"])

    slab = [alloc([128, 16, SLABW], BF16) for _ in range(2)]
    slab_specs = []

    def recorded_dma(i):
        b = i % 2
        for (src, off, n) in slab_specs[i]:
            dst = slab[b][:, :, off:off + n]
            P.dma(POOL, f"w{b}", lambda e, dst=dst, src=src: e.dma_start(
                out=dst, in_=src.rearrange("(kc p) n -> p kc n", p=128)), (), [("slab", b)])

    slab_state = {"next": 0}

    def acquire():
        i = slab_state["next"]
        if i == 0:
            recorded_dma(0)
        if i + 1 < len(slab_specs):
            recorded_dma(i + 1)
        slab_state["next"] = i + 1
        return slab[i % 2], ("slab", i % 2)

    def seg(w, c0, n, off):
        return (w[:, c0:c0 + n], off, n)

    for s in range(8):
        slab_specs.append([seg(adaw_d[0], s * 512, 512, 0)])
    for l in range(layers):
        w = win_d[l]
        for j in range(4):
            slab_specs.append([seg(w, C_CA + j * 128, 128, 0), seg(w, C_CB + j * 128, 128, 128)])
        slab_specs.append([seg(w, C_CG, 512, 0)])
        for h in range(4):
            slab_specs.append([seg(w, C_HI + h * 128, 128, 0)])
            slab_specs.append([seg(w, C_HQ + h * 128, 128, 0), seg(w, C_HFF + h * 128, 128, 128), seg(w, C_HFB + h * 128, 128, 256),
                               seg(w, C_HG + h * 128, 128, 384)])
        slab_specs.append([seg(w, C_AK, 512, 0)])
        for g in range(2):
            slab_specs.append([seg(w, C_AQ + g * 512, 512, 0)])
            slab_specs.append([seg(w, C_AG + g * 512, 512, 0)])
        for s in range(8, 12):
            slab_specs.append([seg(adaw_d[l], s * 512, 512, 0)])
        if l + 1 < layers:
            for s in range(8):
                slab_specs.append([seg(adaw_d[l + 1], s * 512, 512, 0)])

    bank_rr = [0]

    def nextbank(lo=0, hi=8):
        b = lo + bank_rr[0] % (hi - lo)
        bank_rr[0] += 1
        return b

    for l in range(1):
        for s in range(8):
            sb, sres = acquire()
            for j in range(4):
                cb = s * 4 + j
                for kc in range(16):
                    mm(ps[7][:, cb:cb + 1], sb[:, kc, j * 128:(j + 1) * 128], sc_bf[:, kc:kc + 1], kc == 0, kc == 15,
                       [sres, "sc_bf"], [PSR(7)])
        tt(modfm[:, l, :], ps[7][:, 0:32], adabfm[:, l, :], ALU.add, [PSR(7), "adabfm"], [("modfm", l)])
        ts(modfm[:, l, 16:32], modfm[:, l, 16:32], 1.0, None, ALU.add, None, [("modfm", l)], [("modfm", l)])

    def mod_gen(l2):
        for s in range(8):
            sb, sres = acquire()
            for j in range(4):
                for kc in range(16):
                    mm(ps[7][:, j:j + 1], sb[:, kc, j * 128:(j + 1) * 128], sc_bf[:, kc:kc + 1], kc == 0, kc == 15,
                       [sres, "sc_bf"], [PSR(7)])
            tt(modfm[:, l2, s * 4:(s + 1) * 4], ps[7][:, 0:4], adabfm[:, l2, s * 4:(s + 1) * 4], ALU.add, [PSR(7), "adabfm"], [("modfm", l2)])
            if s == 7:
                ts(modfm[:, l2, 16:32], modfm[:, l2, 16:32], 1.0, None, ALU.add, None, [("modfm", l2)], [("modfm", l2)])
            yield
    P.barrier()
    top[0] = const_top + 2 * (16 * SLABW * 2)

    hT = alloc([128, 16, T], BF16)
    layer_top = top[0]

    def chk(name):
        if stop_after == name:
            raise StopBuild()

    def layer_body(l):
        xin_d = x_d if l == 0 else x1_d
        xout_d = y_d if l == layers - 1 else x1_d
        top[0] = layer_top
        xb = [alloc([128, 2048]) for _ in range(3)]
        for tile in range(NT):
            xt = xb[tile % 3]
            xres = ("xb", tile % 3)
            ld(f"x{tile % 3}", xt, xin_d[tile * 128:(tile + 1) * 128, :], [xres])
            for j in range(4):
                bk = nextbank()
                for q in range(4):
                    kc = 4 * j + q
                    tr(ps[bk][:, q * 128:(q + 1) * 128], xt[:, kc * 128:(kc + 1) * 128], identf, [xres, "identf"], [PSR(bk)])
                for q in range(4):
                    kc = 4 * j + q
                    dst = hT[:, kc, tile * 128:(tile + 1) * 128]
                    src = ps[bk][:, q * 128:(q + 1) * 128]
                    if j % 2 == 0:
                        act(dst, src, AF.Identity, [PSR(bk), ("modfm", l)], [("hT", kc, tile)],
                            bias=modfm[:, l, kc:kc + 1], scale=modfm[:, l, 16 + kc:17 + kc])
                    else:
                        ts(dst, src, modfm[:, l, 16 + kc:17 + kc], modfm[:, l, kc:kc + 1], ALU.mult, ALU.add,
                           [PSR(bk), ("modfm", l)], [("hT", kc, tile)])
        P.barrier()
        top[0] = layer_top
        if stop_after == "H":
            raise StopBuild()

        def inproj_fm(sb, sres, c0, tg, bk):
            for kc in range(16):
                mm(ps[bk][:, :], sb[:, kc, c0:c0 + 128], hT[:, kc, tg * 512:(tg + 1) * 512], kc == 0, kc == 15,
                   [sres], [PSR(bk)])

        y_all = alloc([128, 4, T])
        conv_top = top[0]
        ca_t = [alloc([128, 512]) for _ in range(2)]
        u_pad = alloc([128, NB, 286], BF16)
        diagw = alloc([128, 31, 128], BF16)
        sigt = [alloc([128, 512]) for _ in range(2)]
        for j in range(4):
            sb, sres = acquire()
            tt(diagw, identb.unsqueeze(1).broadcast_to([128, 31, 128]),
               convw[:, l, j, :].unsqueeze(2).broadcast_to([128, 31, 128]), ALU.mult, ["identb", "convw"], ["diagw"])
            for tg in range(NG):
                bk = nextbank(0, 4)
                inproj_fm(sb, sres, 0, tg, bk)
                ca_ = ca_t[tg % 2]
                act(ca_, ps[bk][:, :], AF.Copy, [PSR(bk)], [("ca", tg % 2)])
                bk = nextbank(0, 4)
                inproj_fm(sb, sres, 128, tg, bk)
                sg = sigt[tg % 2]
                act(sg, ps[bk][:, :], AF.Sigmoid, [PSR(bk)], [("sig", tg % 2)])
                tt(u_pad[:, 2 * tg:2 * tg + 2, 15:271], ca_.rearrange("p (b t) -> p b t", b=2),
                   sg.rearrange("p (b t) -> p b t", b=2), ALU.mult, [("ca", tg % 2), ("sig", tg % 2)], ["u_pad"])
            memset(u_pad[:, 0, 0:15], 0.0, ["u_pad"])
            memset(u_pad[:, NB - 1, 271:286], 0.0, ["u_pad"])
            if NB > 1:
                tt(u_pad[:, 1:NB, 0:15], u_pad[:, 0:NB - 1, 256:271], flags[:, 2, 1:NB].unsqueeze(2).broadcast_to([128, NB - 1, 15]),
                   ALU.mult, ["u_pad", "flags"], ["u_pad"])
                tt(u_pad[:, 0:NB - 1, 271:286], u_pad[:, 1:NB, 15:30], flags[:, 3, 0:NB - 1].unsqueeze(2).broadcast_to([128, NB - 1, 15]),
                   ALU.mult, ["u_pad", "flags"], ["u_pad"])
            for tg in range(NG):
                bk = nextbank(4, 8)
                for tap in range(31):
                    mm(ps[bk][:, :], diagw[:, tap, :], u_pad[:, 2 * tg:2 * tg + 2, tap:tap + 256], tap == 0, tap == 30,
                       ["diagw", "u_pad"], [PSR(bk)])
                act(y_all[:, j, tg * 512:(tg + 1) * 512], ps[bk][:, :], AF.Identity, [PSR(bk), "convp"], [("y_all", j, tg)],
                    bias=convp[:, 0, l, j:j + 1])
        P.barrier()
        top[0] = conv_top
        ysq = [alloc([128, 512]) for _ in range(2)]
        mean_sb = alloc([128, 512])
        m2 = alloc([128, 512])
        rstd = alloc([128, 512])
        t1 = [alloc([128, 512]) for _ in range(2)]
        cgt = [alloc([128, 512], BF16) for _ in range(2)]
        mixst = [alloc([128, 4, 512], BF16) for _ in range(2)]
        sb, sres = acquire()
        for tg in range(NG):
            for j in range(4):
                q = ysq[j % 2]
                act(q, y_all[:, j, tg * 512:(tg + 1) * 512], AF.Square, [("y_all", j, tg)], [("ysq", j % 2)])
                mm(ps[4][:, :], ones512, y_all[:, j, tg * 512:(tg + 1) * 512], j == 0, j == 3, [("y_all", j, tg), "ones512"], [PSR(4)])
                mm(ps[5][:, :], ones512, q, j == 0, j == 3, [("ysq", j % 2), "ones512"], [PSR(5)])
            act(mean_sb, ps[4][:, :], AF.Copy, [PSR(4)], ["mean_sb"])
            tt(m2, mean_sb, mean_sb, ALU.mult, ["mean_sb"], ["m2"])
            tt(m2, ps[5][:, :], m2, ALU.subtract, [PSR(5), "m2"], ["m2"])
            act(rstd, m2, AF.Sqrt, ["m2"], ["rstd"], bias=1e-5)
            recip(rstd, rstd, ["rstd"], ["rstd"])
            ms = mixst[tg % 2]
            for j in range(4):
                bk = nextbank(0, 4)
                inproj_fm(sb, sres, j * 128, tg, bk)
                cg_ = cgt[j % 2]
                act(cg_, ps[bk][:, :], AF.Silu, [PSR(bk)], [("cgt", j % 2)])
                t = t1[j % 2]
                tt(t, y_all[:, j, tg * 512:(tg + 1) * 512], mean_sb, ALU.subtract, [("y_all", j, tg), "mean_sb"], [("t1", j % 2)])
                tt(t, t, rstd, ALU.mult, [("t1", j % 2), "rstd"], [("t1", j % 2)])
                act(t, t, AF.Silu, [("t1", j % 2), "convp"], [("t1", j % 2)], bias=convp[:, 2, l, j:j + 1], scale=convp[:, 1, l, j:j + 1])
                tt(ms[:, j, :], t, cg_, ALU.mult, [("t1", j % 2), ("cgt", j % 2)], [("mixst", tg % 2)])
            st(f"mx{tg % 2}", mix_d[12:16, :, tg * 512:(tg + 1) * 512].rearrange("k p t -> p k t"), ms, [("mixst", tg % 2)])
        P.barrier()
        top[0] = layer_top
        if stop_after == "conv":
            raise StopBuild()

        lbe = alloc([128, 2, 2, 4])
        lbs = alloc([128, 2, 4])
        act(lbe, lbraw, AF.Exp, ["lbraw"], ["lbe"])
        tt(lbs, lbe[:, :, 0, :], lbe[:, :, 1, :], ALU.add, ["lbe"], ["lbs"])
        recip(lbs, lbs, ["lbs"], ["lbs"])
        tt(lbs, lbs, lbe[:, :, 1, :], ALU.mult, ["lbs", "lbe"], ["lbs"])
        if l == 0:
            ts(lbs, lbs, 0.0, None, ALU.mult, None, ["lbs"], ["lbs"])
        ts(lbv[:, :, 0, :], lbs, -1.0, 1.0, ALU.mult, ALU.add, ["lbs"], ["lbv"])
        hg_top = top[0]
        LNCQ = math.log(128 ** -0.5)
        for h in range(4):
            top[0] = hg_top
            qsil = alloc([128, T], BF16)
            gsil = alloc([128, T], BF16)
            Vp = alloc([128, NB * 3, 128], BF16)
            o_acc = alloc([128, T])
            scr_top = top[0]
            vtmp = [alloc([128, 512], BF16) for _ in range(2)]
            top[0] = scr_top
            KVsb = [[alloc([128, 8, 128]) for _ in range(2)] for _ in range(2)]
            scr_end = top[0]
            sbB, sresB = acquire()
            for tg in range(NG):
                bk = nextbank(0, 4)
                inproj_fm(sbB, sresB, 0, tg, bk)
                vt_ = vtmp[tg % 2]
                act(vt_, ps[bk][:, :], AF.Copy, [PSR(bk)], [("vtmp", tg % 2)])
                bk2 = nextbank(4, 8)
                pb = ps[bk2][:, :].bitcast(BF16)
                for blk in range(2):
                    for cl in range(8):
                        cc = blk * 8 + cl
                        q_, slot_ = cl % 3, blk * 3 + cl // 3
                        tr(pb[32 * q_:32 * q_ + 32, slot_ * 128:(slot_ + 1) * 128], vt_[:, cc * 32:(cc + 1) * 32], identb,
                           [("vtmp", tg % 2), "identb"], [PSR(bk2)])
                for blk in range(2):
                    b_ = 2 * tg + blk
                    act(Vp[0:96, b_ * 3:b_ * 3 + 2, :], pb[0:96, blk * 384:blk * 384 + 256].rearrange("p (a b) -> p a b", a=2),
                        AF.Copy, [PSR(bk2)], ["Vp"])
                    act(Vp[0:64, b_ * 3 + 2, :], pb[0:64, blk * 384 + 256:blk * 384 + 384], AF.Copy, [PSR(bk2)], ["Vp"])
            sb, sres = acquire()
            for tg in range(NG):
                bk = nextbank(0, 4)
                inproj_fm(sb, sres, 0, tg, bk)
                act(qsil[:, tg * 512:(tg + 1) * 512], ps[bk][:, :], AF.Silu, [PSR(bk)], [("qsil", tg)])
                bk = nextbank(0, 4)
                inproj_fm(sb, sres, 384, tg, bk)
                act(gsil[:, tg * 512:(tg + 1) * 512], ps[bk][:, :], AF.Silu, [PSR(bk)], [("gsil", tg)])
            P.barrier()
            def mk():
                d_ = dict(kraw=alloc([128, 256]), G=alloc([128, 256]), tA=alloc([128, 256]),
                          tB=alloc([128, 256]), kt=alloc([128, 256], BF16),
                          kh=alloc([128, 256], BF16), Khp=alloc([128, 3, 128], BF16), mref=alloc([128, 8]),
                          refn=alloc([128, 8]), bend=alloc([128, 1]))
                d_["Lg"] = d_["tB"]
                return d_

            def mkp():
                return dict(qt=alloc([128, 256], BF16), Dn=alloc([128, 8]), tin=alloc([128, 1]))
            BUF = [[mk() for _ in range(2)] for _ in range(2)]
            PERS = [[mkp() for _ in range(3)] for _ in range(2)]
            Sp = [[alloc([128, 128]) for _ in range(2)] for _ in range(2)]
            Spb = [[alloc([128, 128], BF16) for _ in range(4)] for _ in range(2)]
            send = [[alloc([128, 128]) for _ in range(2)] for _ in range(2)]

            def prep(d, step):
                b = step if d == 0 else NB - 1 - step
                B_ = BUF[d][step % 2]
                tag = (d, step % 2)
                R = lambda nm: ((nm, d, step % 3) if nm in ("qt", "Dn", "tin") else ((("tB",) + tag) if nm == "Lg" else (nm,) + tag))
                Pp = PERS[d][step % 3]
                tk = slice(b * 256, (b + 1) * 256)
                pbk = 7 if d == 0 else 3
                for kc in range(16):
                    mm(ps[pbk][:, 0:256], sb[:, kc, 128 + 128 * d:256 + 128 * d], hT[:, kc, tk], kc == 0, kc == 15, [sres], [PSR(pbk)])
                sg = B_["tA"]
                act(sg, ps[pbk][:, 0:256], AF.Sigmoid, [PSR(pbk)], [R("tA")], scale=-1.0)
                yield
                ts(B_["kraw"], sg, lbv[:, d, 0, h:h + 1], None, ALU.mult, None, [R("tA"), "lbv"], [R("kraw")])
                ts(sg, B_["kraw"], -1.0, 1.0, ALU.mult, ALU.add, [R("kraw")], [R("tA")])
                ts(sg, sg, 1e-6, None, ALU.max, None, [R("tA")], [R("tA")])
                yield
                act(B_["Lg"], sg, AF.Ln, [R("tA")], [R("Lg")])
                yield
                G = B_["G"]
                P.op(DVE, lambda e, G=G, Lg=B_["Lg"]: e.tensor_tensor_scan(out=G, data0=Lg, data1=zc.broadcast_to([128, 256]), initial=0.0,
                                                                            op0=ALU.add, op1=ALU.add), [R("Lg"), "zc"], [R("G")])
                mref, refn = B_["mref"], B_["refn"]
                if d == 0:
                    cp(mref, G[:, 15:256:32], [R("G")], [R("mref")])
                    cp(refn[:, 0:7], mref[:, 1:8], [R("mref")], [R("refn")])
                    cp(refn[:, 7:8], G[:, 255:256], [R("G"), R("refn")], [R("refn")])
                    tt(Pp["tin"], mref[:, 0:1], flags[:, 0, b:b + 1], ALU.add, [R("mref"), "flags"], [R("tin")])
                else:
                    ts(B_["bend"], G[:, 255:256], -1.0, None, ALU.mult, None, [R("G")], [R("bend")])
                    tt(G, B_["Lg"], G, ALU.subtract, [R("Lg"), R("G")], [R("G")])
                    cp(mref, G[:, 16:256:32], [R("G")], [R("mref")])
                    cp(refn[:, 1:8], mref[:, 0:7], [R("mref")], [R("refn")])
                    cp(refn[:, 0:1], G[:, 0:1], [R("G"), R("refn")], [R("refn")])
                    tt(Pp["tin"], mref[:, 7:8], B_["bend"], ALU.subtract, [R("mref"), R("bend")], [R("tin")])
                    tt(Pp["tin"], Pp["tin"], flags[:, 1, b:b + 1], ALU.add, [R("tin"), "flags"], [R("tin")])
                tt(Pp["Dn"], refn, mref, ALU.subtract, [R("refn"), R("mref")], [R("Dn")])
                a_, b_ = B_["tA"], B_["tB"]
                g3 = G.rearrange("p (c j) -> p c j", j=32)
                tt(a_.rearrange("p (c j) -> p c j", j=32), g3, mref.unsqueeze(2).broadcast_to([128, 8, 32]),
                   ALU.subtract, [R("G"), R("mref")], [R("tA")])
                ts(a_, a_, 41.0, -41.0, ALU.min, ALU.max, [R("tA")], [R("tA")])
                yield
                act(Pp["tin"], Pp["tin"], AF.Exp, [R("tin")], [R("tin")])
                act(Pp["Dn"], Pp["Dn"], AF.Exp, [R("Dn")], [R("Dn")])
                act(b_, a_, AF.Exp, [R("tA")], [R("tB")], bias=LNCQ)
                yield
                tt(Pp["qt"], qsil[:, tk], b_, ALU.mult, [R("tB")], [R("qt")])
                yield
                act(b_, a_, AF.Exp, [R("tA")], [R("tB")], scale=-1.0)
                yield
                tt(B_["kt"], B_["kraw"], b_, ALU.mult, [R("kraw"), R("tB")], [R("kt")])
                tt(a_.rearrange("p (c j) -> p c j", j=32), g3, refn.unsqueeze(2).broadcast_to([128, 8, 32]),
                   ALU.subtract, [R("G"), R("refn")], [R("tA")])
                yield
                act(b_, a_, AF.Exp, [R("tA")], [R("tB")], scale=-1.0)
                yield
                tt(B_["kh"], B_["kraw"], b_, ALU.mult, [R("kraw"), R("tB")], [R("kh")])
                yield
                pb = ps[pbk][:, :].bitcast(BF16)
                for cc in range(8):
                    q_, slot_ = cc % 3, cc // 3
                    tr(pb[32 * q_:32 * q_ + 32, slot_ * 128:(slot_ + 1) * 128], B_["kh"][:, cc * 32:(cc + 1) * 32], identb,
                       [R("kh"), "identb"], [PSR(pbk)])
                act(B_["Khp"][0:96, 0:2, :], pb[0:96, 0:256].rearrange("p (a b) -> p a b", a=2), AF.Copy, [PSR(pbk)], [R("Khp")])
                act(B_["Khp"][0:64, 2, :], pb[0:64, 256:384], AF.Copy, [PSR(pbk)], [R("Khp")])
                yield

            for d in range(2):
                ld(f"s0{d}", send[d][0], s0_d[l, d, h, :, :], [("send", d, 0)])
            nsend = [1, 1]
            NI = NB * 8

            Asb = [[alloc([128, 3, 32], BF16) for _ in range(2)] for _ in range(2)]

            def batch(step):
                for d in range(2):
                    b = step if d == 0 else NB - 1 - step
                    B_ = BUF[d][step % 2]
                    tag = (d, step % 2)
                    for cl in range(8):
                        q_ = cl % 3
                        pr = slice(32 * q_, 32 * q_ + 32)
                        mm(ps[q_][:, (cl // 3) * 128:(cl // 3 + 1) * 128], B_["Khp"][pr, cl // 3, :], Vp[pr, b * 3 + cl // 3, :], True, True,
                           [("Khp",) + tag, "Vp"], [PSR(q_)])
                    for q_ in range(3):
                        n_ = 3 if q_ < 2 else 2
                        act(KVsb[d][step % 2][:, q_:8:3, :], ps[q_][:, 0:n_ * 128].rearrange("p (a b) -> p a b", a=n_), AF.Copy,
                            [PSR(q_)], [("KVsb",) + tag])
                    for cl in range(8):
                        pr = slice(32 * (cl % 3), 32 * (cl % 3) + 32)
                        cs = slice(cl * 32, (cl + 1) * 32)
                        aslot = ps[4][pr, (d * 3 + cl // 3) * 32:(d * 3 + cl // 3 + 1) * 32]
                        mm(aslot, B_["kt"][:, cs], PERS[d][step % 3]["qt"][:, cs], True, True, [("kt",) + tag, ("qt", d, step % 3)], [PSR(4)])
                for d in range(2):
                    tag = (d, step % 2)
                    A_ = Asb[d][step % 2]
                    tt(A_[0:64, 0:3, :], ps[4][0:64, d * 96:d * 96 + 96].rearrange("p (a b) -> p a b", a=3),
                       tri[0:64, d, :].unsqueeze(1).broadcast_to([64, 3, 32]), ALU.mult, [PSR(4), "tri"], [("Asb",) + tag])
                    tt(A_[64:96, 0:2, :], ps[4][64:96, d * 96:d * 96 + 64].rearrange("p (a b) -> p a b", a=2),
                       tri[64:96, d, :].unsqueeze(1).broadcast_to([32, 2, 32]), ALU.mult, [PSR(4), "tri"], [("Asb",) + tag])

            def run_interleaved(gens):
                gens = list(gens)
                waiting = []
                while gens or waiting:
                    if not gens:
                        gens, waiting = waiting, []
                    for g in list(gens):
                        try:
                            r = next(g)
                            if r == "WAITPREP":
                                gens.remove(g)
                                waiting.append(g)
                        except StopIteration:
                            gens.remove(g)

            NI = NB * 8

            def chain_gen(step, d):
                for ci in range(8):
                    i = step * 8 + ci
                    if True:
                        cl = ci if d == 0 else 7 - ci
                        b = step if d == 0 else NB - 1 - step
                        c = b * 8 + cl
                        B_ = BUF[d][step % 2]
                        Pp = PERS[d][step % 3]
                        ptag = (d, step % 3)
                        tag = (d, step % 2)
                        pr = slice(32 * (cl % 3), 32 * (cl % 3) + 32)
                        cs = slice(cl * 32, (cl + 1) * 32)
                        A = Asb[d][step % 2][pr, cl // 3, :]
                        kvs = KVsb[d][step % 2][:, cl, :]
                        grp = c // 16
                        ob = 5 + d
                        oslice = ps[ob][:, (c % 16) * 32:(c % 16 + 1) * 32]
                        mm(oslice, Spb[d][i % 4], Pp["qt"][:, cs], True, False, [("Spb", d, i % 4), ("qt",) + ptag], [PSR(ob)])
                        mm(oslice, Vp[pr, b * 3 + cl // 3, :], A, False, True, ["Vp", ("Asb",) + tag], [PSR(ob)])
                        boundary = (ci == 7)
                        last = (i == NI - 1)
                        nxt = (i + 1) % 2
                        dn = Pp["Dn"][:, cl:cl + 1]
                        if boundary:
                            sd_ = send[d][nsend[d] % 2]
                            sres_ = ("send", d, nsend[d] % 2)
                            nsend[d] += 1
                            stt(sd_, Sp[d][i % 2], dn, kvs, ALU.mult, ALU.add, [("Sp", d, i % 2), ("Dn",) + ptag, ("KVsb",) + tag], [sres_])
                            st(f"so{d}", ost_d[l, b, d, h, :, :], sd_, [sres_])
                            if not last:
                                ntag = (d, (step + 1) % 3)
                                ts(Sp[d][nxt], sd_, PERS[d][(step + 1) % 3]["tin"], None, ALU.mult, None, [sres_, ("tin",) + ntag], [("Sp", d, nxt)])
                        else:
                            stt(Sp[d][nxt], Sp[d][i % 2], dn, kvs, ALU.mult, ALU.add, [("Sp", d, i % 2), ("Dn",) + ptag, ("KVsb",) + tag], [("Sp", d, nxt)])
                        if not last:
                            act(Spb[d][(i + 1) % 4], Sp[d][nxt], AF.Copy, [("Sp", d, nxt)], [("Spb", d, (i + 1) % 4)])
                        gdone = (c % 16 == 15) if d == 0 else (c % 16 == 0)
                        if gdone:
                            fstep = grp * 16 + 15
                            bstep = NI - 1 - grp * 16
                            mine = fstep if d == 0 else bstep
                            other = bstep if d == 0 else fstep
                            first = mine < other or (mine == other and d == 0)
                            osl = o_acc[:, grp * 512:(grp + 1) * 512]
                            if first:
                                act(osl, ps[ob][:, :], AF.Copy, [PSR(ob)], [("oacc", grp)])
                            else:
                                tt(osl, ps[ob][:, :], osl, ALU.add, [PSR(ob), ("oacc", grp)], [("oacc", grp)])
                        yield

            g0 = [prep(0, 0), prep(1, 0)]
            if NB > 1:
                g0 += [prep(0, 1), prep(1, 1)]
            run_interleaved(g0)
            for d in range(2):
                ts(Sp[d][0], send[d][0], PERS[d][0]["tin"], None, ALU.mult, None, [("send", d, 0), ("tin", d, 0)], [("Sp", d, 0)])
                act(Spb[d][0], Sp[d][0], AF.Copy, [("Sp", d, 0)], [("Spb", d, 0)])
            batch(0)
            for step in range(NB):
                if step + 1 < NB:
                    batch(step + 1)
                gens = [chain_gen(step, 0), chain_gen(step, 1)]
                if step + 2 < NB:
                    gens += [prep(0, step + 2), prep(1, step + 2)]
                run_interleaved(gens)
            P.barrier()
            save_top = top[0]
            top[0] = scr_top
            mxs = [alloc([128, 512], BF16) for _ in range(2)]
            rA = [alloc([128, 512]) for _ in range(2)]
            rB = [alloc([128, 512]) for _ in range(2)]
            assert top[0] <= scr_end
            top[0] = save_top
            for tg in range(NG):
                sl = slice(tg * 512, (tg + 1) * 512)
                a = rA[tg % 2]
                b = rB[tg % 2]
                act(a, o_acc[:, sl], AF.Square, [("oacc", tg)], [("rA", tg % 2)])
                bk = 7
                mm(ps[bk][:, :], ones1, a, True, True, [("rA", tg % 2), "ones1"], [PSR(bk)])
                act(b, ps[bk][:, :], AF.Sqrt, [PSR(bk)], [("rB", tg % 2)], bias=1e-6, scale=1.0 / 128.0)
                recip(b, b, [("rB", tg % 2)], [("rB", tg % 2)])
                tt(a, o_acc[:, sl], b, ALU.mult, [("oacc", tg), ("rB", tg % 2)], [("rA", tg % 2)])
                stt(mxs[tg % 2], a, hgnw[:, l, h:h + 1], gsil[:, sl], ALU.mult, ALU.mult,
                    [("rA", tg % 2), "hgnw"], [("mxs", tg % 2)])
                st(f"mx{tg % 2}", mix_d[8 + h, :, sl], mxs[tg % 2], [("mxs", tg % 2)])
            P.barrier()
        top[0] = layer_top
        if stop_after == "hgrn":
            raise StopBuild()

        kr = alloc([128, 2, T], BF16)
        Vt = alloc([128, NT, 256], BF16)
        ckT = alloc([128, 2, 256], BF16)
        cvb = alloc([128, 2, 256], BF16)
        amask = alloc([128, NQB, 2, 128], BF16)
        esink = alloc([128, 8])
        qr = alloc([128, 4, T], BF16)
        att_top = top[0]
        ckf = alloc([128, 2, 256])
        cvf = alloc([128, 2, 256])
        cst = [alloc([128, 512]) for _ in range(2)]
        snt = [alloc([128, 512]) for _ in range(2)]
        zq = [alloc([128, 512]) for _ in range(2)]
        r1 = [alloc([128, 512]) for _ in range(2)]
        r2 = [alloc([128, 512]) for _ in range(2)]
        ktok = [alloc([128, 4, 128]) for _ in range(2)]
        vtok = [alloc([128, 256]) for _ in range(2)]
        top[0] = att_top
        PT = [alloc([128, 5, 512], BF16) for _ in range(2)]
        rec = [alloc([128, 512]) for _ in range(2)]
        gst = [alloc([128, 512], BF16) for _ in range(2)]

        for half in range(0, NQB, 8):
            hi_ = min(half + 8, NQB)
            dst = amask[:, half:hi_, :, :]
            src = amask_d[:, half:hi_, :, :]
            P.dma(POOL, "am", lambda e, dst=dst, src=src: e.dma_start(out=dst, in_=src), (), ["amask"])
        ld("c1", ckf, ck_d[l].rearrange("(kb p) c -> p kb c", p=128), ["ckf"])
        ld("c2", cvf, cv_d[l].rearrange("(kb p) c -> p kb c", p=128), ["cvf"])
        cp(cvb, cvf, ["cvf"], ["cvb"])
        ld("c3", esink, sink_d[:, l, :], ["esink"])
        act(esink, esink, AF.Exp, ["esink"], ["esink"])
        for kb in range(2):
            for g in range(2):
                bk = nextbank()
                tr(ps[bk][:, 0:128], ckf[:, kb, g * 128:(g + 1) * 128], identf, ["ckf", "identf"], [PSR(bk)])
                act(ckT[:, g, kb * 128:(kb + 1) * 128], ps[bk][:, 0:128], AF.Copy, [PSR(bk)], ["ckT"])

        chk('attn_a')
        ropecnt = [0]

        def rope(src_bank, dst, tg):
            i = ropecnt[0] % 2
            ropecnt[0] += 1
            sl = slice(tg * 512, (tg + 1) * 512)
            ld(f"cs{i}", cst[i], cos_d[:, sl], [("cst", i)])
            ld(f"sn{i}", snt[i], sin_d[:, sl], [("snt", i)])
            act(zq[i], ps[src_bank][:, :], AF.Copy, [PSR(src_bank)], [("zq", i)])
            bk2 = nextbank(4, 8)
            mm(ps[bk2][:, :], rm, zq[i], True, True, ["rm", ("zq", i)], [PSR(bk2)])
            tt(r1[i], zq[i], cst[i], ALU.mult, [("zq", i), ("cst", i)], [("r1", i)])
            tt(r2[i], ps[bk2][:, :], snt[i], ALU.mult, [PSR(bk2), ("snt", i)], [("r2", i)])
            tt(dst, r1[i], r2[i], ALU.add, [("r1", i), ("r2", i)], ["ropeout"])
            return i

        sb, sres = acquire()
        for kvh in range(2):
            for tg in range(NG):
                bk = nextbank(0, 4)
                inproj_fm(sb, sres, kvh * 128, tg, bk)
                i = rope(bk, kr[:, kvh, tg * 512:(tg + 1) * 512], tg)
                bk3 = nextbank(4, 8)
                for a in range(4):
                    tr(ps[bk3][:, a * 128:(a + 1) * 128], zq[i][:, a * 128:(a + 1) * 128], identf, [("zq", i), "identf"], [PSR(bk3)])
                kt_ = ktok[(kvh * NG + tg) % 2]
                kres = ("ktok", (kvh * NG + tg) % 2)
                act(kt_, ps[bk3][:, :].rearrange("p (a b) -> p a b", a=4), AF.Copy, [PSR(bk3)], [kres])
                st(f"ko{(kvh * NG + tg) % 2}", ok_d[l, tg * 512:(tg + 1) * 512, kvh * 128:(kvh + 1) * 128].rearrange("(a p) d -> p a d", p=128),
                   kt_, [kres])
        chk('attn_b')
        for tt_ in range(NT):
            bk = nextbank(0, 4)
            for kc in range(16):
                mm(ps[bk][:, 0:256], hT[:, kc, tt_ * 128:(tt_ + 1) * 128], sb[:, kc, 256:512], kc == 0, kc == 15, [sres], [PSR(bk)])
            vt_ = vtok[tt_ % 2]
            act(vt_, ps[bk][:, 0:256], AF.Copy, [PSR(bk)], [("vtok", tt_ % 2)])
            cp(Vt[:, tt_, :], vt_, [("vtok", tt_ % 2)], ["Vt"])
            st(f"vo{tt_ % 2}", ov_d[l, tt_ * 128:(tt_ + 1) * 128, :], vt_, [("vtok", tt_ % 2)])

        chk('attn_kv')
        SCALE = 128 ** -0.5
        P.barrier()
        for g in range(2):
            sb, sres = acquire()
            for hh in range(4):
                for tg in range(NG):
                    bk = nextbank(0, 4)
                    inproj_fm(sb, sres, hh * 128, tg, bk)
                    rope(bk, qr[:, hh, tg * 512:(tg + 1) * 512], tg)
            P.barrier()
            chk('attn_q')
            def slots_of(n):
                kbL = max(n - 1, 0)
                kbR = min(n + 1, NQB - 1)
                return [(kr[:, g, kbL * 128:(kbL + 1) * 128], Vt[:, kbL, g * 128:(g + 1) * 128], 0, onesb),
                        (kr[:, g, n * 128:(n + 1) * 128], Vt[:, n, g * 128:(g + 1) * 128], None, onesb),
                        (kr[:, g, kbR * 128:(kbR + 1) * 128], Vt[:, kbR, g * 128:(g + 1) * 128], 1, onesb),
                        (ckT[:, g, 0:128], cvb[:, 0, g * 128:(g + 1) * 128], None, ctxones),
                        (ckT[:, g, 128:256], cvb[:, 1, g * 128:(g + 1) * 128], None, ctxones)]

            def S_part(n):
                pt = PT[n % 2]
                pres = ("PT", n % 2)
                qsl = qr[:, :, n * 128:(n + 1) * 128]
                for si, (kT, vv, mside, onesl) in enumerate(slots_of(n)):
                    bk = nextbank(0, 4)
                    mm(ps[bk][:, :], kT, qsl, True, True, ["kr", "ckT", ("qrn", n)], [PSR(bk)])
                    act(pt[:, si, :], ps[bk][:, :], AF.Exp, [PSR(bk)], [pres], scale=SCALE)
                    if mside is not None:
                        v4 = pt[:, si, :].rearrange("p (h q) -> p h q", h=4)
                        tt(v4, v4, amask[:, n, mside, :].unsqueeze(1).broadcast_to([128, 4, 128]), ALU.mult, [pres, "amask"], [pres])

            def F_part(n):
                pt = PT[n % 2]
                pres = ("PT", n % 2)
                qsl = qr[:, :, n * 128:(n + 1) * 128]
                slots = slots_of(n)
                pvb = 4 + 2 * (n % 2)
                smb = pvb + 1
                for si, (kT, vv, mside, onesl) in enumerate(slots):
                    mm(ps[pvb][:, :], vv, pt[:, si, :], si == 0, si == 4, [pres, "Vt", "cvb"], [PSR(pvb)])
                for si, (kT, vv, mside, onesl) in enumerate(slots):
                    mm(ps[smb][:, :], onesl, pt[:, si, :], si == 0, si == 4, [pres, "onesb", "ctxones"], [PSR(smb)])
                rc = rec[n % 2]
                tt(rc.rearrange("p (h q) -> p h q", h=4), ps[smb][:, :].rearrange("p (h q) -> p h q", h=4),
                   esink[:, g * 4:(g + 1) * 4].unsqueeze(2).broadcast_to([128, 4, 128]), ALU.add, [PSR(smb), "esink"], [("rec", n % 2)])
                recip(rc, rc, [("rec", n % 2)], [("rec", n % 2)])
                tt(qsl, ps[pvb][:, :].rearrange("p (h q) -> p h q", h=4), rc.rearrange("p (h q) -> p h q", h=4), ALU.mult,
                   [PSR(pvb), ("rec", n % 2)], [("qrn", n)])

            S_part(0)
            for n in range(NQB):
                if n + 1 < NQB:
                    S_part(n + 1)
                F_part(n)
            P.barrier()
            chk('attn_s')
            sb, sres = acquire()
            for hh in range(4):
                for tg in range(NG):
                    bk = nextbank(0, 4)
                    inproj_fm(sb, sres, hh * 128, tg, bk)
                    gs_ = gst[tg % 2]
                    act(gs_, ps[bk][:, :], AF.Silu, [PSR(bk)], [("gst", tg % 2)])
                    qs_ = qr[:, hh, tg * 512:(tg + 1) * 512]
                    tt(qs_, qs_, gs_, ALU.mult, [("gst", tg % 2), "qr"], ["qr"])
            st("mx0", mix_d[g * 4:(g + 1) * 4, :, :].rearrange("k p t -> p k t"), qr, ["qr"])
            P.barrier()
        top[0] = layer_top
        if stop_after == "attn":
            raise StopBuild()

        top[0] = const_top + 2 * (16 * SLABW * 2)
        wo = alloc([128, 16, 2048], BF16)
        lnw_b = alloc([128, 2048])
        lnb_b = alloc([128, 2048])
        gate_b = alloc([128, 2048])
        screp = alloc([128, 16, 128], BF16)
        cp(screp, sc_bf.unsqueeze(2).broadcast_to([128, 16, 128]), ["sc_bf"], ["screp"])
        ld("c2", lnb_b, adabg_d[l:l + 1, :].partition_broadcast(128), ["lnb_b"])
        for s_ in range(4):
            dst = wo[:, :, s_ * 512:(s_ + 1) * 512]
            src = wout_d[l][:, s_ * 512:(s_ + 1) * 512]
            P.dma(POOL, f"wo{s_ % 2}", lambda e, dst=dst, src=src: e.dma_start(out=dst, in_=src.rearrange("(kc p) n -> p kc n", p=128)),
                  (), [("wo", s_)])
            sb, sres = acquire()
            bk = nextbank(0, 4)
            for kc in range(16):
                mm(ps[bk][:, :], screp[:, kc, :], sb[:, kc, 0:512], kc == 0, kc == 15, [sres, "screp"], [PSR(bk)])
            tt(gate_b[:, s_ * 512:(s_ + 1) * 512], ps[bk][:, :], lnb_b[:, s_ * 512:(s_ + 1) * 512], ALU.add,
               [PSR(bk), "lnb_b"], [("gate_b", s_)])
        ld("c0", lnw_b, lnw_d[l:l + 1, :].partition_broadcast(128), ["lnw_b"])
        ld("c1", lnb_b, lnb_d[l:l + 1, :].partition_broadcast(128), ["lnb_b"])
        mixt = [alloc([128, 16, 128], BF16) for _ in range(2)]
        xt2 = [alloc([128, 2048]) for _ in range(2)]
        tm = [alloc([128, 512]) for _ in range(2)]
        stats = [alloc([128, 24]) for _ in range(2)]
        mv = [alloc([128, 2]) for _ in range(2)]
        sdv = [alloc([128, 2]) for _ in range(2)]

        def tile_gen(tile_i):
            p_ = tile_i % 2
            mt = mixt[p_]
            mres = ("mixt", p_)
            ld(f"mt{p_}", mt, mix_d[:, :, tile_i * 128:(tile_i + 1) * 128].rearrange("k p t -> p k t"), [mres])
            xt = xt2[p_]
            yres = ("xt2", p_)
            ld(f"x{p_}", xt, xin_d[tile_i * 128:(tile_i + 1) * 128, :], [yres])
            yield
            st_, mv_, sd_ = stats[p_], mv[p_], sdv[p_]
            for cg in range(4):
                bk = cg + 4 * p_
                for kc in range(16):
                    mm(ps[bk][:, :], mt[:, kc, :], wo[:, kc, cg * 512:(cg + 1) * 512], kc == 0, kc == 15,
                       [mres, ("wo", cg)], [PSR(bk)])
                csl = slice(cg * 512, (cg + 1) * 512)
                tm_ = tm[cg % 2]
                tt(tm_, ps[bk][:, :], gate_b[:, csl], ALU.mult, [PSR(bk), ("gate_b", cg)], [("tm", cg % 2)])
                yield
                stt(xt[:, csl], xt[:, csl], ALPHA, tm_, ALU.mult, ALU.add, [yres, ("tm", cg % 2)], [yres])
                yield
                P.op(DVE, lambda e, o=st_[:, cg * 6:(cg + 1) * 6], i_=xt[:, csl]: e.bn_stats(out=o, in_=i_), [yres], [("stats", p_)])
                yield
            yield "EPI"
            P.op(DVE, lambda e: e.bn_aggr(out=mv_, in_=st_), [("stats", p_)], [("mv", p_)])
            yield
            act(sd_[:, 0:1], mv_[:, 1:2], AF.Sqrt, [("mv", p_)], [("sdv", p_)], bias=1e-5)
            yield
            recip(sd_[:, 0:1], sd_[:, 0:1], [("sdv", p_)], [("sdv", p_)])
            yield
            ts(sd_[:, 1:2], mv_[:, 0:1], sd_[:, 0:1], -1.0, ALU.mult, ALU.mult, [("mv", p_), ("sdv", p_)], [("sdv", p_)])
            yield
            act(xt, xt, AF.Identity, [yres, ("sdv", p_)], [yres], bias=sd_[:, 1:2], scale=sd_[:, 0:1])
            yield
            tt(xt, xt, lnw_b, ALU.mult, [yres, "lnw_b"], [yres])
            yield
            tt(xt, xt, lnb_b, ALU.add, [yres, "lnb_b"], [yres], eng=POOL)
            yield
            st(f"yo{p_}", xout_d[tile_i * 128:(tile_i + 1) * 128, :], xt, [yres])
            yield

        active = [tile_gen(0)]
        nxt_tile = 1
        mg = mod_gen(l + 1) if l + 1 < layers else None
        while active:
            for g_ in list(active):
                try:
                    r_ = next(g_)
                    if r_ == "EPI":
                        if mg is not None:
                            try:
                                next(mg)
                            except StopIteration:
                                mg = None
                        if nxt_tile < NT:
                            active.append(tile_gen(nxt_tile))
                            nxt_tile += 1
                except StopIteration:
                    active.remove(g_)
        if mg is not None:
            for _ in mg:
                pass
        P.barrier()

    try:
        for l in range(layers):
            layer_body(l)
    except StopBuild:
        pass
    P.finish()
    P.emit()
    nc._maxtop = maxtop[0]
    return nc


def _fm(v):
    v = np.asarray(v, np.float32)
    lead = v.shape[:-1]
    n = v.shape[-1] // 128
    v = v.reshape(lead + (n, 128))
    return np.ascontiguousarray(np.moveaxis(v, -1, 0))


def rope_tables(T, sample):
    d = np.arange(128)
    if not sample:
        return np.ones((128, T), np.float32), np.zeros((128, T), np.float32)
    t = np.arange(T)
    row = (t // 64).astype(np.float32)
    col = (t % 64).astype(np.float32)
    nf = 32
    inv = (np.float32(10000.0) ** (-np.arange(nf, dtype=np.float32) / np.float32(nf))).astype(np.float32)
    pos = np.where((d < 64)[:, None], row[None, :], col[None, :]).astype(np.float32)
    ang = (pos * inv[d % 32][:, None]).astype(np.float32)
    return np.cos(ang).astype(np.float32), np.sin(ang).astype(np.float32)


def rot_matrix():
    R = np.zeros((128, 128), np.float32)
    for m in range(128):
        q = (m % 64) // 32
        if q == 0:
            R[m, m + 32] = -1.0
        else:
            R[m, m - 32] = 1.0
    return np.ascontiguousarray(R.T)


def core_consts(NB, sample):
    T = NB * 256
    NQB = T // 128
    cosT, sinT = rope_tables(T, sample)
    amask = np.zeros((128, NQB, 2, 128), np.float32)
    j = np.arange(128)[:, None]
    i = np.arange(128)[None, :]
    low = (j >= i).astype(np.float32)
    up = (j <= i).astype(np.float32)
    for n in range(NQB):
        if sample:
            if n >= 1:
                amask[:, n, 0, :] = low
            if n <= NQB - 2:
                amask[:, n, 1, :] = up
        else:
            if n % 2 == 0:
                amask[:, n, 1, :] = 1.0
            else:
                amask[:, n, 0, :] = 1.0
    ctxones = np.full((128, 128), 1.0 if sample else 0.0, np.float32)
    flags = np.zeros((128, 4, NB), np.float32)
    if sample:
        flags[:, 2, 1:] = 1.0
        flags[:, 3, :NB - 1] = 1.0
    else:
        flags[:, 0, 1:] = -1e4
        flags[:, 1, :NB - 1] = -1e4
    tri = np.zeros((128, 2, 32), np.float32)
    s = (np.arange(128) % 32)[:, None]
    t = np.arange(32)[None, :]
    tri[:, 0, :] = (s <= t)
    tri[:, 1, :] = (s >= t)
    return dict(cosT=cosT, sinT=sinT, Rm=rot_matrix(), amask=amask, ctxones=ctxones, flags=flags, tri=tri)


def shared_inputs(ada_w, ada_b, w_in, w_out, attn_sink, hg_lower_bounds, hg_norm_w, conv_w, conv_b, conv_ln_w, conv_ln_b, ln_w, ln_b):
    f = lambda a: np.ascontiguousarray(np.asarray(a, np.float32))
    ada_b = f(ada_b)
    d = {}
    d["ada_w"] = f(ada_w)
    d["adab_fm"] = np.ascontiguousarray(_fm(ada_b[:, :4096]))
    d["adab_g"] = np.ascontiguousarray(ada_b[:, 4096:])
    d["w_in"] = f(w_in)
    d["w_out"] = f(w_out)
    d["sinkrep"] = np.ascontiguousarray(np.broadcast_to(f(attn_sink).reshape(1, 2, 8), (128, 2, 8)))
    d["lbraw"] = _fm(f(hg_lower_bounds).reshape(2, 2, 512))
    d["hgnw"] = _fm(hg_norm_w)
    d["convw"] = np.ascontiguousarray(np.transpose(_fm(conv_w), (0, 1, 3, 2)))
    d["convp"] = np.ascontiguousarray(np.stack([_fm(conv_b), _fm(conv_ln_w), _fm(conv_ln_b)], axis=1))
    d["lnw"] = f(ln_w)
    d["lnb"] = f(ln_b)
    return d


NBK = 8
_CACHE = {}


def kernel(x_prompt, x_sample, cache_k, cache_v, state_hgrn, c, c_ctx, ada_w, ada_b, w_in, w_out,
           attn_sink, hg_lower_bounds, hg_norm_w, conv_w, conv_b, conv_ln_w, conv_ln_b, ln_w, ln_b):
    f = lambda a: np.ascontiguousarray(np.asarray(a, np.float32))
    x_prompt = f(x_prompt)
    x_sample = f(x_sample)
    NB = NBK
    T = NB * 256
    shared = shared_inputs(ada_w, ada_b, w_in, w_out, attn_sink, hg_lower_bounds, hg_norm_w, conv_w, conv_b,
                           conv_ln_w, conv_ln_b, ln_w, ln_b)
    cs = core_consts(NB, True)
    cp_ = core_consts(NB, False)
    assign = {2: list(range(0, 6)), 3: list(range(6, 12)), 4: list(range(12, 17)), 5: list(range(17, 22)),
              6: list(range(22, 27)), 7: list(range(27, 32))}
    in_maps = []
    for core in range(8):
        m = dict(shared)
        if core < 2:
            m.update(cs)
            m["x"] = np.ascontiguousarray(x_sample[core])
            m["cond"] = np.ascontiguousarray(f(c)[core].reshape(16, 128).T)
            m["ck"] = np.ascontiguousarray(f(cache_k)[core].reshape(2, 256, 256))
            m["cv"] = np.ascontiguousarray(f(cache_v)[core].reshape(2, 256, 256))
            m["s0"] = np.ascontiguousarray(f(state_hgrn)[core])
        else:
            m.update(cp_)
            xs = np.zeros((T, D), np.float32)
            for i, s in enumerate(assign[core]):
                xs[i * 256:(i + 1) * 256] = x_prompt[s]
            m["x"] = xs
            m["cond"] = np.ascontiguousarray(f(c_ctx).reshape(16, 128).T)
            m["ck"] = np.zeros((2, 256, 256), np.float32)
            m["cv"] = np.zeros((2, 256, 256), np.float32)
            m["s0"] = np.zeros((2, 2, 4, 128, 128), np.float32)
        in_maps.append(m)
    if "nc" not in _CACHE:
        _CACHE["nc"] = build(NB)
    res = run_bass_kernel_spmd(_CACHE["nc"], in_maps, core_ids=list(range(8))).results
    y_prompt = np.zeros((32, 256, D), np.float32)
    y_sample = np.zeros((2, 2048, D), np.float32)
    nk = np.zeros((32, 2, 256, 2, 128), np.float32)
    nv = np.zeros((32, 2, 256, 2, 128), np.float32)
    nst = np.zeros((32, 2, 2, 4, 128, 128), np.float32)
    for core in range(8):
        r = res[core]
        if core < 2:
            y_sample[core] = r["y"]
        else:
            for i, s in enumerate(assign[core]):
                y_prompt[s] = r["y"][i * 256:(i + 1) * 256]
                nk[s] = r["ok"][:, i * 256:(i + 1) * 256, :].reshape(2, 256, 2, 128)
                nv[s] = r["ov"][:, i * 256:(i + 1) * 256, :].reshape(2, 256, 2, 128)
                nst[s] = r["ost"][:, i]
    return (y_prompt, y_sample, nk, nv, nst)
```

```python
import math
import numpy as np
import concourse.bass as bass
import concourse.mybir as mybir
from concourse.bass_utils import run_bass_kernel_spmd

F32 = mybir.dt.float32
BF16 = mybir.dt.bfloat16
ALU = mybir.AluOpType
AF = mybir.ActivationFunctionType

PE, ACT, DVE, POOL, SP = "pe", "act", "dve", "pool", "sp"
SAME_ENG_SYNC = True

D = 2048
KC = 16
DIN = 6656
ALPHA = float(4 ** 0.25)
C_AQ, C_AK, C_AV, C_AG = 0, 1024, 1280, 1536
C_HQ, C_HI, C_HFF, C_HFB, C_HG = 2560, 3072, 3584, 4096, 4608
C_CA, C_CB, C_CG = 5120, 5632, 6144
SLABW = 512


class StopBuild(Exception):
    pass


class Prog:
    def __init__(self, nc):
        self.nc = nc
        self.eng = {PE: nc.tensor, ACT: nc.scalar, DVE: nc.vector, POOL: nc.gpsimd, SP: nc.sync}
        self.ops = {e: [] for e in self.eng}
        self.sems = {}
        self.cnt = {}
        self.seen = {e: {} for e in self.eng}
        self.last_w = {}
        self.readers = {}
        self.nops = 0

    def _sem(self, key):
        if key not in self.sems:
            name = "s_" + ("_".join(str(k) for k in key) if isinstance(key, tuple) else str(key))
            self.sems[key] = self.nc.alloc_semaphore(name)
            self.cnt[key] = 0
        return self.sems[key]

    def _deps(self, reads, writes):
        deps = {}

        def add(tok):
            k, v = tok
            if deps.get(k, 0) < v:
                deps[k] = v
        for r in reads:
            if r in self.last_w:
                add(self.last_w[r])
        for w in writes:
            if w in self.last_w:
                add(self.last_w[w])
            for tok in self.readers.get(w, ()):
                add(tok)
        return deps

    def _commit(self, tok, reads, writes):
        for r in reads:
            self.readers.setdefault(r, []).append(tok)
        for w in writes:
            self.last_w[w] = tok
            self.readers[w] = []

    def _waits(self, eng, deps, own_key, same_sync):
        waits = []
        for k, v in deps.items():
            if k == own_key and not same_sync:
                continue
            if self.seen[eng].get(k, 0) >= v:
                continue
            self.seen[eng][k] = v
            waits.append((self._sem(k), v))
        return waits

    def op(self, eng, fn, reads=(), writes=(), same_sync=None):
        if same_sync is None:
            same_sync = SAME_ENG_SYNC and eng != PE
        reads = list(reads)
        writes = list(writes)
        deps = self._deps(reads, writes)
        waits = self._waits(eng, deps, eng, same_sync)
        sem = self._sem(eng)
        self.cnt[eng] += 1
        tok = (eng, self.cnt[eng])
        self.ops[eng].append((waits, fn, (sem, 1)))
        self._commit(tok, reads, writes)
        self.nops += 1
        return tok

    def dma(self, queue, ch, fn, reads=(), writes=()):
        key = ("dma", ch)
        reads = list(reads)
        writes = list(writes)
        deps = self._deps(reads, writes)
        sem = self._sem(key)
        if self.cnt[key] > 0:
            v = 16 * self.cnt[key]
            if deps.get(key, 0) < v:
                deps[key] = v
        waits = self._waits(queue, deps, None, True)
        self.cnt[key] += 1
        tok = (key, 16 * self.cnt[key])
        self.ops[queue].append((waits, fn, (sem, 16)))
        self._commit(tok, reads, writes)
        self.nops += 1
        return tok

    def barrier(self, keep_prefix=("slab",), skip_ch_prefix=("w",)):
        toks = {}
        for k, c in self.cnt.items():
            if c == 0:
                continue
            if isinstance(k, tuple):
                if str(k[1]).startswith(skip_ch_prefix):
                    continue
                toks[k] = 16 * c
            else:
                toks[k] = c
        for e in self.eng:
            waits = self._waits(e, dict(toks), None, True)
            if waits:
                self.ops[e].append((waits, None, None))

        def keep(r):
            name = r[0] if isinstance(r, tuple) else r
            return str(name).startswith(keep_prefix)
        self.last_w = {r: t for r, t in self.last_w.items() if keep(r)}
        self.readers = {r: t for r, t in self.readers.items() if keep(r)}

    def finish(self):
        toks = {}
        for k, c in self.cnt.items():
            if c == 0:
                continue
            toks[k] = 16 * c if isinstance(k, tuple) else c
        for e in self.eng:
            waits = self._waits(e, dict(toks), None, True)
            if waits:
                self.ops[e].append((waits, None, None))

    def emit(self):
        nc = self.nc
        with nc.Block() as block:
            def mk(e):
                def body(engine):
                    for waits, fn, inc in self.ops[e]:
                        for sem, v in waits:
                            engine.wait_ge(sem, v)
                        if fn is not None:
                            ins = fn(engine)
                            if inc is not None:
                                ins.then_inc(inc[0], inc[1])
                return body
            block.sync(mk(SP))
            block.tensor(mk(PE))
            block.scalar(mk(ACT))
            block.vector(mk(DVE))
            block.gpsimd(mk(POOL))


def build(NB, dbg=False, layers=2, stop_after=None):
    T = NB * 256
    NT = T // 128
    NG = T // 512
    NC = T // 32
    NQB = NT
    nc = bass.Bass("TRN2", target_bir_lowering=False)
    P = Prog(nc)

    def din(name, shape, dt=F32):
        return nc.dram_tensor(name, list(shape), dt, kind="ExternalInput").ap()

    def dout(name, shape, dt=F32, internal=False):
        return nc.dram_tensor(name, list(shape), dt, kind=("Internal" if internal else "ExternalOutput")).ap()

    x_d = din("x", [T, D])
    cond_d = din("cond", [128, 16])
    adaw_d = din("ada_w", [2, D, 6144])
    adabfm_d = din("adab_fm", [128, 2, 32])
    adabg_d = din("adab_g", [2, 2048])
    win_d = din("w_in", [2, D, DIN])
    wout_d = din("w_out", [2, D, D])
    ck_d = din("ck", [2, 256, 256])
    cv_d = din("cv", [2, 256, 256])
    s0_d = din("s0", [2, 2, 4, 128, 128])
    sink_d = din("sinkrep", [128, 2, 8])
    lbraw_d = din("lbraw", [128, 2, 2, 4])
    hgnw_d = din("hgnw", [128, 2, 4])
    convw_d = din("convw", [128, 2, 4, 31])
    convp_d = din("convp", [128, 3, 2, 4])
    lnw_d = din("lnw", [2, 2048])
    lnb_d = din("lnb", [2, 2048])
    cos_d = din("cosT", [128, T])
    sin_d = din("sinT", [128, T])
    rm_d = din("Rm", [128, 128])
    amask_d = din("amask", [128, NQB, 2, 128])
    ctxones_d = din("ctxones", [128, 128])
    flags_d = din("flags", [128, 4, NB])
    tri_d = din("tri", [128, 2, 32])

    y_d = dout("y", [T, D])
    ok_d = dout("ok", [2, T, 256])
    ov_d = dout("ov", [2, T, 256])
    ost_d = dout("ost", [2, NB, 2, 4, 128, 128])
    x1_d = dout("x1s", [T, D], internal=not dbg)
    mix_d = dout("mixs", [16, 128, T], BF16, internal=not dbg)

    ARENA = 176 * 1024
    arena = nc.alloc_sbuf_tensor("arena", [128, ARENA // 4], F32)
    top = [0]
    maxtop = [0]

    def alloc(shape, dt=F32, parts=None):
        esz = 4 if dt == F32 else 2
        n = 1
        for s in shape[1:]:
            n *= s
        nb = (n * esz + 31) // 32 * 32
        off = top[0]
        top[0] += nb
        maxtop[0] = max(maxtop[0], top[0])
        assert top[0] <= ARENA, f"SBUF arena overflow {top[0]}"
        p = shape[0]
        ap = arena[0:p, off // 4:(off + nb) // 4]
        if dt != F32:
            ap = ap.bitcast(dt)
        ap = ap[:, 0:n]
        if len(shape) == 3:
            ap = ap.rearrange("p (a b) -> p a b", a=shape[1])
        elif len(shape) == 4:
            ap = ap.rearrange("p (a b c) -> p a b c", a=shape[1], b=shape[2])
        return ap

    ps = [nc.alloc_psum_tensor(f"ps{i}", [128, 512], F32) for i in range(8)]

    def PSR(i):
        return ("ps", i)

    def mm(out, lhsT, rhs, start, stop, reads, writes):
        P.op(PE, lambda e: e.matmul(out, lhsT, rhs, start=start, stop=stop), reads, writes)

    def tr(out, in_, ident, reads, writes):
        P.op(PE, lambda e: e.transpose(out, in_, ident), reads, writes)

    def act(out, in_, func, reads, writes, bias=None, scale=None):
        kw = {}
        if bias is not None:
            kw["bias"] = bias
        if scale is not None:
            kw["scale"] = scale
        P.op(ACT, lambda e: e.activation(out=out, in_=in_, func=func, **kw), reads, writes)

    def tt(out, in0, in1, op, reads, writes, eng=DVE):
        P.op(eng, lambda e: e.tensor_tensor(out=out, in0=in0, in1=in1, op=op), reads, writes)

    def ts(out, in0, s1, s2, op0, op1, reads, writes, eng=DVE):
        if op1 is None:
            P.op(eng, lambda e: e.tensor_scalar(out=out, in0=in0, scalar1=s1, scalar2=None, op0=op0), reads, writes)
        else:
            P.op(eng, lambda e: e.tensor_scalar(out=out, in0=in0, scalar1=s1, scalar2=s2, op0=op0, op1=op1), reads, writes)

    def stt(out, in0, scalar, in1, op0, op1, reads, writes):
        P.op(DVE, lambda e: e.scalar_tensor_tensor(out=out, in0=in0, scalar=scalar, in1=in1, op0=op0, op1=op1), reads, writes)

    def cp(out, in_, reads, writes, eng=DVE):
        P.op(eng, lambda e: e.tensor_copy(out, in_), reads, writes)

    def recip(out, in_, reads, writes):
        P.op(DVE, lambda e: e.reciprocal(out, in_), reads, writes)

    def memset(ap, val, writes, eng=DVE):
        P.op(eng, lambda e: e.memset(ap, val), (), writes)

    def ld(ch, out, in_, writes, reads=()):
        return P.dma(SP, ch, lambda e: e.dma_start(out=out, in_=in_), reads, writes)

    def st(ch, out, in_, reads, writes=()):
        return P.dma(SP, ch, lambda e: e.dma_start(out=out, in_=in_), reads, writes)

    identf = alloc([128, 128])
    identb = alloc([128, 128], BF16)
    ones512 = alloc([128, 128])
    ones1 = alloc([128, 128])
    onesb = alloc([128, 128], BF16)
    ctxones = alloc([128, 128], BF16)
    ctxof = alloc([128, 128])
    tri = alloc([128, 2, 32])
    rm = alloc([128, 128])
    zc = alloc([128, 1])
    flags = alloc([128, 4, NB])
    modfm = alloc([128, 2, 32])
    lbraw = alloc([128, 2, 2, 4])
    lbv = alloc([128, 2, 2, 4])
    hgnw = alloc([128, 2, 4])
    convw = alloc([128, 2, 4, 31])
    convp = alloc([128, 3, 2, 4])
    cond = alloc([128, 16])
    sc_bf = alloc([128, 16], BF16)
    adabfm = alloc([128, 2, 32])
    const_top = top[0]

    memset(identf, 0.0, ["identf"])
    P.op(POOL, lambda e: e.affine_select(out=identf, in_=identf, pattern=[[-1, 128]], compare_op=ALU.not_equal,
                                         fill=1.0, base=0, channel_multiplier=1), ["identf"], ["identf"])
    cp(identb, identf, ["identf"], ["identb"])
    memset(ones512, 1.0 / 512.0, ["ones512"])
    memset(ones1, 1.0, ["ones1"])
    memset(onesb, 1.0, ["onesb"])
    memset(zc, 0.0, ["zc"])
    ld("c0", ctxof, ctxones_d[:, :], ["ctxof"])
    cp(ctxones, ctxof, ["ctxof"], ["ctxones"])
    ld("c1", tri, tri_d[:, :, :], ["tri"])
    ld("c2", rm, rm_d[:, :], ["rm"])
    ld("c3", flags, flags_d[:, :, :], ["flags"])
    ld("c4", lbraw, lbraw_d[:, :, :, :], ["lbraw"])
    ld("c5", hgnw, hgnw_d[:, :, :], ["hgnw"])
    ld("c6", convw, convw_d[:, :, :, :], ["convw"])
    ld("c7", convp, convp_d[:, :, :, :], ["convp"])
    ld("c0", cond, cond_d[:, :], ["cond"])
    ld("c1", adabfm, adabfm_d[:, :, :], ["adabfm"])
    act(sc_bf, cond, AF.Silu, ["cond"], ["sc_bf"])

    slab = [alloc([128, 16, SLABW], BF16) for _ in range(2)]
    slab_specs = []

    def recorded_dma(i):
        b = i % 2
        for (src, off, n) in slab_specs[i]:
            dst = slab[b][:, :, off:off + n]
            P.dma(POOL, f"w{b}", lambda e, dst=dst, src=src: e.dma_start(
                out=dst, in_=src.rearrange("(kc p) n -> p kc n", p=128)), (), [("slab", b)])

    slab_state = {"next": 0}

    def acquire():
        i = slab_state["next"]
        if i == 0:
            recorded_dma(0)
        if i + 1 < len(slab_specs):
            recorded_dma(i + 1)
        slab_state["next"] = i + 1
        return slab[i % 2], ("slab", i % 2)

    def seg(w, c0, n, off):
        return (w[:, c0:c0 + n], off, n)

    for s in range(8):
        slab_specs.append([seg(adaw_d[0], s * 512, 512, 0)])
    for l in range(layers):
        w = win_d[l]
        for j in range(4):
            slab_specs.append([seg(w, C_CA + j * 128, 128, 0), seg(w, C_CB + j * 128, 128, 128)])
        slab_specs.append([seg(w, C_CG, 512, 0)])
        for h in range(4):
            slab_specs.append([seg(w, C_HI + h * 128, 128, 0)])
            slab_specs.append([seg(w, C_HQ + h * 128, 128, 0), seg(w, C_HFF + h * 128, 128, 128), seg(w, C_HFB + h * 128, 128, 256),
                               seg(w, C_HG + h * 128, 128, 384)])
        slab_specs.append([seg(w, C_AK, 512, 0)])
        for g in range(2):
            slab_specs.append([seg(w, C_AQ + g * 512, 512, 0)])
            slab_specs.append([seg(w, C_AG + g * 512, 512, 0)])
        for s in range(8, 12):
            slab_specs.append([seg(adaw_d[l], s * 512, 512, 0)])
        if l + 1 < layers:
            for s in range(8):
                slab_specs.append([seg(adaw_d[l + 1], s * 512, 512, 0)])

    bank_rr = [0]

    def nextbank(lo=0, hi=8):
        b = lo + bank_rr[0] % (hi - lo)
        bank_rr[0] += 1
        return b

    for l in range(1):
        for s in range(8):
            sb, sres = acquire()
            for j in range(4):
                cb = s * 4 + j
                for kc in range(16):
                    mm(ps[7][:, cb:cb + 1], sb[:, kc, j * 128:(j + 1) * 128], sc_bf[:, kc:kc + 1], kc == 0, kc == 15,
                       [sres, "sc_bf"], [PSR(7)])
        tt(modfm[:, l, :], ps[7][:, 0:32], adabfm[:, l, :], ALU.add, [PSR(7), "adabfm"], [("modfm", l)])
        ts(modfm[:, l, 16:32], modfm[:, l, 16:32], 1.0, None, ALU.add, None, [("modfm", l)], [("modfm", l)])

    def mod_gen(l2):
        for s in range(8):
            sb, sres = acquire()
            for j in range(4):
                for kc in range(16):
                    mm(ps[7][:, j:j + 1], sb[:, kc, j * 128:(j + 1) * 128], sc_bf[:, kc:kc + 1], kc == 0, kc == 15,
                       [sres, "sc_bf"], [PSR(7)])
            tt(modfm[:, l2, s * 4:(s + 1) * 4], ps[7][:, 0:4], adabfm[:, l2, s * 4:(s + 1) * 4], ALU.add, [PSR(7), "adabfm"], [("modfm", l2)])
            if s == 7:
                ts(modfm[:, l2, 16:32], modfm[:, l2, 16:32], 1.0, None, ALU.add, None, [("modfm", l2)], [("modfm", l2)])
            yield
    P.barrier()
    top[0] = const_top + 2 * (16 * SLABW * 2)

    hT = alloc([128, 16, T], BF16)
    layer_top = top[0]

    def chk(name):
        if stop_after == name:
            raise StopBuild()

    def layer_body(l):
        xin_d = x_d if l == 0 else x1_d
        xout_d = y_d if l == layers - 1 else x1_d
        top[0] = layer_top
        xb = [alloc([128, 2048]) for _ in range(3)]
        for tile in range(NT):
            xt = xb[tile % 3]
            xres = ("xb", tile % 3)
            ld(f"x{tile % 3}", xt, xin_d[tile * 128:(tile + 1) * 128, :], [xres])
            for j in range(4):
                bk = nextbank()
                for q in range(4):
                    kc = 4 * j + q
                    tr(ps[bk][:, q * 128:(q + 1) * 128], xt[:, kc * 128:(kc + 1) * 128], identf, [xres, "identf"], [PSR(bk)])
                for q in range(4):
                    kc = 4 * j + q
                    dst = hT[:, kc, tile * 128:(tile + 1) * 128]
                    src = ps[bk][:, q * 128:(q + 1) * 128]
                    if j % 2 == 0:
                        act(dst, src, AF.Identity, [PSR(bk), ("modfm", l)], [("hT", kc, tile)],
                            bias=modfm[:, l, kc:kc + 1], scale=modfm[:, l, 16 + kc:17 + kc])
                    else:
                        ts(dst, src, modfm[:, l, 16 + kc:17 + kc], modfm[:, l, kc:kc + 1], ALU.mult, ALU.add,
                           [PSR(bk), ("modfm", l)], [("hT", kc, tile)])
        P.barrier()
        top[0] = layer_top
        if stop_after == "H":
            raise StopBuild()

        def inproj_fm(sb, sres, c0, tg, bk):
            for kc in range(16):
                mm(ps[bk][:, :], sb[:, kc, c0:c0 + 128], hT[:, kc, tg * 512:(tg + 1) * 512], kc == 0, kc == 15,
                   [sres], [PSR(bk)])

        y_all = alloc([128, 4, T])
        conv_top = top[0]
        ca_t = [alloc([128, 512]) for _ in range(2)]
        u_pad = alloc([128, NB, 286], BF16)
        diagw = alloc([128, 31, 128], BF16)
        sigt = [alloc([128, 512]) for _ in range(2)]
        for j in range(4):
            sb, sres = acquire()
            tt(diagw, identb.unsqueeze(1).broadcast_to([128, 31, 128]),
               convw[:, l, j, :].unsqueeze(2).broadcast_to([128, 31, 128]), ALU.mult, ["identb", "convw"], ["diagw"])
            for tg in range(NG):
                bk = nextbank(0, 4)
                inproj_fm(sb, sres, 0, tg, bk)
                ca_ = ca_t[tg % 2]
                act(ca_, ps[bk][:, :], AF.Copy, [PSR(bk)], [("ca", tg % 2)])
                bk = nextbank(0, 4)
                inproj_fm(sb, sres, 128, tg, bk)
                sg = sigt[tg % 2]
                act(sg, ps[bk][:, :], AF.Sigmoid, [PSR(bk)], [("sig", tg % 2)])
                tt(u_pad[:, 2 * tg:2 * tg + 2, 15:271], ca_.rearrange("p (b t) -> p b t", b=2),
                   sg.rearrange("p (b t) -> p b t", b=2), ALU.mult, [("ca", tg % 2), ("sig", tg % 2)], ["u_pad"])
            memset(u_pad[:, 0, 0:15], 0.0, ["u_pad"])
            memset(u_pad[:, NB - 1, 271:286], 0.0, ["u_pad"])
            if NB > 1:
                tt(u_pad[:, 1:NB, 0:15], u_pad[:, 0:NB - 1, 256:271], flags[:, 2, 1:NB].unsqueeze(2).broadcast_to([128, NB - 1, 15]),
                   ALU.mult, ["u_pad", "flags"], ["u_pad"])
                tt(u_pad[:, 0:NB - 1, 271:286], u_pad[:, 1:NB, 15:30], flags[:, 3, 0:NB - 1].unsqueeze(2).broadcast_to([128, NB - 1, 15]),
                   ALU.mult, ["u_pad", "flags"], ["u_pad"])
            for tg in range(NG):
                bk = nextbank(4, 8)
                for tap in range(31):
                    mm(ps[bk][:, :], diagw[:, tap, :], u_pad[:, 2 * tg:2 * tg + 2, tap:tap + 256], tap == 0, tap == 30,
                       ["diagw", "u_pad"], [PSR(bk)])
                act(y_all[:, j, tg * 512:(tg + 1) * 512], ps[bk][:, :], AF.Identity, [PSR(bk), "convp"], [("y_all", j, tg)],
                    bias=convp[:, 0, l, j:j + 1])
        P.barrier()
        top[0] = conv_top
        ysq = [alloc([128, 512]) for _ in range(2)]
        mean_sb = alloc([128, 512])
        m2 = alloc([128, 512])
        rstd = alloc([128, 512])
        t1 = [alloc([128, 512]) for _ in range(2)]
        cgt = [alloc([128, 512], BF16) for _ in range(2)]
        mixst = [alloc([128, 4, 512], BF16) for _ in range(2)]
        sb, sres = acquire()
        for tg in range(NG):
            for j in range(4):
                q = ysq[j % 2]
                act(q, y_all[:, j, tg * 512:(tg + 1) * 512], AF.Square, [("y_all", j, tg)], [("ysq", j % 2)])
                mm(ps[4][:, :], ones512, y_all[:, j, tg * 512:(tg + 1) * 512], j == 0, j == 3, [("y_all", j, tg), "ones512"], [PSR(4)])
                mm(ps[5][:, :], ones512, q, j == 0, j == 3, [("ysq", j % 2), "ones512"], [PSR(5)])
            act(mean_sb, ps[4][:, :], AF.Copy, [PSR(4)], ["mean_sb"])
            tt(m2, mean_sb, mean_sb, ALU.mult, ["mean_sb"], ["m2"])
            tt(m2, ps[5][:, :], m2, ALU.subtract, [PSR(5), "m2"], ["m2"])
            act(rstd, m2, AF.Sqrt, ["m2"], ["rstd"], bias=1e-5)
            recip(rstd, rstd, ["rstd"], ["rstd"])
            ms = mixst[tg % 2]
            for j in range(4):
                bk = nextbank(0, 4)
                inproj_fm(sb, sres, j * 128, tg, bk)
                cg_ = cgt[j % 2]
                act(cg_, ps[bk][:, :], AF.Silu, [PSR(bk)], [("cgt", j % 2)])
                t = t1[j % 2]
                tt(t, y_all[:, j, tg * 512:(tg + 1) * 512], mean_sb, ALU.subtract, [("y_all", j, tg), "mean_sb"], [("t1", j % 2)])
                tt(t, t, rstd, ALU.mult, [("t1", j % 2), "rstd"], [("t1", j % 2)])
                act(t, t, AF.Silu, [("t1", j % 2), "convp"], [("t1", j % 2)], bias=convp[:, 2, l, j:j + 1], scale=convp[:, 1, l, j:j + 1])
                tt(ms[:, j, :], t, cg_, ALU.mult, [("t1", j % 2), ("cgt", j % 2)], [("mixst", tg % 2)])
            st(f"mx{tg % 2}", mix_d[12:16, :, tg * 512:(tg + 1) * 512].rearrange("k p t -> p k t"), ms, [("mixst", tg % 2)])
        P.barrier()
        top[0] = layer_top
        if stop_after == "conv":
            raise StopBuild()

        lbe = alloc([128, 2, 2, 4])
        lbs = alloc([128, 2, 4])
        act(lbe, lbraw, AF.Exp, ["lbraw"], ["lbe"])
        tt(lbs, lbe[:, :, 0, :], lbe[:, :, 1, :], ALU.add, ["lbe"], ["lbs"])
        recip(lbs, lbs, ["lbs"], ["lbs"])
        tt(lbs, lbs, lbe[:, :, 1, :], ALU.mult, ["lbs", "lbe"], ["lbs"])
        if l == 0:
            ts(lbs, lbs, 0.0, None, ALU.mult, None, ["lbs"], ["lbs"])
        ts(lbv[:, :, 0, :], lbs, -1.0, 1.0, ALU.mult, ALU.add, ["lbs"], ["lbv"])
        hg_top = top[0]
        LNCQ = math.log(128 ** -0.5)
        for h in range(4):
            top[0] = hg_top
            qsil = alloc([128, T], BF16)
            gsil = alloc([128, T], BF16)
            Vp = alloc([128, NB * 3, 128], BF16)
            o_acc = alloc([128, T])
            vtmp = [alloc([128, 512], BF16) for _ in range(2)]
            scr_top = top[0]
            KVsb = [[alloc([128, 8, 128]) for _ in range(2)] for _ in range(2)]
            scr_end = top[0]
            sbB, sresB = acquire()
            for tg in range(NG):
                bk = nextbank(0, 4)
                inproj_fm(sbB, sresB, 0, tg, bk)
                vt_ = vtmp[tg % 2]
                act(vt_, ps[bk][:, :], AF.Copy, [PSR(bk)], [("vtmp", tg % 2)])
                bk2 = nextbank(4, 8)
                pb = ps[bk2][:, :].bitcast(BF16)
                for blk in range(2):
                    for cl in range(8):
                        cc = blk * 8 + cl
                        q_, slot_ = cl % 3, blk * 3 + cl // 3
                        tr(pb[32 * q_:32 * q_ + 32, slot_ * 128:(slot_ + 1) * 128], vt_[:, cc * 32:(cc + 1) * 32], identb,
                           [("vtmp", tg % 2), "identb"], [PSR(bk2)])
                for blk in range(2):
                    b_ = 2 * tg + blk
                    act(Vp[0:96, b_ * 3:b_ * 3 + 2, :], pb[0:96, blk * 384:blk * 384 + 256].rearrange("p (a b) -> p a b", a=2),
                        AF.Copy, [PSR(bk2)], ["Vp"])
                    act(Vp[0:64, b_ * 3 + 2, :], pb[0:64, blk * 384 + 256:blk * 384 + 384], AF.Copy, [PSR(bk2)], ["Vp"])
            sb, sres = acquire()
            for tg in range(NG):
                bk = nextbank(0, 4)
                inproj_fm(sb, sres, 0, tg, bk)
                act(qsil[:, tg * 512:(tg + 1) * 512], ps[bk][:, :], AF.Silu, [PSR(bk)], [("qsil", tg)])
                bk = nextbank(0, 4)
                inproj_fm(sb, sres, 384, tg, bk)
                act(gsil[:, tg * 512:(tg + 1) * 512], ps[bk][:, :], AF.Silu, [PSR(bk)], [("gsil", tg)])
            def mk():
                d_ = dict(kraw=alloc([128, 256]), G=alloc([128, 256]), tA=alloc([128, 256]),
                          tB=alloc([128, 256]), kt=alloc([128, 256], BF16),
                          kh=alloc([128, 256], BF16), Khp=alloc([128, 3, 128], BF16), mref=alloc([128, 8]),
                          refn=alloc([128, 8]), bend=alloc([128, 1]))
                d_["Lg"] = d_["tB"]
                return d_

            def mkp():
                return dict(qt=alloc([128, 256], BF16), Dn=alloc([128, 8]), tin=alloc([128, 1]))
            BUF = [[mk() for _ in range(2)] for _ in range(2)]
            PERS = [[mkp() for _ in range(3)] for _ in range(2)]
            Sp = [[alloc([128, 128]) for _ in range(2)] for _ in range(2)]
            Spb = [[alloc([128, 128], BF16) for _ in range(4)] for _ in range(2)]
            send = [[alloc([128, 128]) for _ in range(2)] for _ in range(2)]

            def prep(d, step):
                b = step if d == 0 else NB - 1 - step
                B_ = BUF[d][step % 2]
                tag = (d, step % 2)
                R = lambda nm: ((nm, d, step % 3) if nm in ("qt", "Dn", "tin") else ((("tB",) + tag) if nm == "Lg" else (nm,) + tag))
                Pp = PERS[d][step % 3]
                tk = slice(b * 256, (b + 1) * 256)
                pbk = 7 if d == 0 else 3
                for kc in range(16):
                    mm(ps[pbk][:, 0:256], sb[:, kc, 128 + 128 * d:256 + 128 * d], hT[:, kc, tk], kc == 0, kc == 15, [sres], [PSR(pbk)])
                sg = B_["tA"]
                act(sg, ps[pbk][:, 0:256], AF.Sigmoid, [PSR(pbk)], [R("tA")], scale=-1.0)
                yield
                ts(B_["kraw"], sg, lbv[:, d, 0, h:h + 1], None, ALU.mult, None, [R("tA"), "lbv"], [R("kraw")])
                ts(sg, B_["kraw"], -1.0, 1.0, ALU.mult, ALU.add, [R("kraw")], [R("tA")])
                ts(sg, sg, 1e-6, None, ALU.max, None, [R("tA")], [R("tA")])
                yield
                act(B_["Lg"], sg, AF.Ln, [R("tA")], [R("Lg")])
                yield
                G = B_["G"]
                P.op(DVE, lambda e, G=G, Lg=B_["Lg"]: e.tensor_tensor_scan(out=G, data0=Lg, data1=zc.broadcast_to([128, 256]), initial=0.0,
                                                                            op0=ALU.add, op1=ALU.add), [R("Lg"), "zc"], [R("G")])
                mref, refn = B_["mref"], B_["refn"]
                if d == 0:
                    cp(mref, G[:, 15:256:32], [R("G")], [R("mref")])
                    cp(refn[:, 0:7], mref[:, 1:8], [R("mref")], [R("refn")])
                    cp(refn[:, 7:8], G[:, 255:256], [R("G"), R("refn")], [R("refn")])
                    tt(Pp["tin"], mref[:, 0:1], flags[:, 0, b:b + 1], ALU.add, [R("mref"), "flags"], [R("tin")])
                else:
                    ts(B_["bend"], G[:, 255:256], -1.0, None, ALU.mult, None, [R("G")], [R("bend")])
                    tt(G, B_["Lg"], G, ALU.subtract, [R("Lg"), R("G")], [R("G")])
                    cp(mref, G[:, 16:256:32], [R("G")], [R("mref")])
                    cp(refn[:, 1:8], mref[:, 0:7], [R("mref")], [R("refn")])
                    cp(refn[:, 0:1], G[:, 0:1], [R("G"), R("refn")], [R("refn")])
                    tt(Pp["tin"], mref[:, 7:8], B_["bend"], ALU.subtract, [R("mref"), R("bend")], [R("tin")])
                    tt(Pp["tin"], Pp["tin"], flags[:, 1, b:b + 1], ALU.add, [R("tin"), "flags"], [R("tin")])
                tt(Pp["Dn"], refn, mref, ALU.subtract, [R("refn"), R("mref")], [R("Dn")])
                a_, b_ = B_["tA"], B_["tB"]
                g3 = G.rearrange("p (c j) -> p c j", j=32)
                tt(a_.rearrange("p (c j) -> p c j", j=32), g3, mref.unsqueeze(2).broadcast_to([128, 8, 32]),
                   ALU.subtract, [R("G"), R("mref")], [R("tA")])
                ts(a_, a_, 41.0, -41.0, ALU.min, ALU.max, [R("tA")], [R("tA")])
                yield
                act(Pp["tin"], Pp["tin"], AF.Exp, [R("tin")], [R("tin")])
                act(Pp["Dn"], Pp["Dn"], AF.Exp, [R("Dn")], [R("Dn")])
                act(b_, a_, AF.Exp, [R("tA")], [R("tB")], bias=LNCQ)
                yield
                tt(Pp["qt"], qsil[:, tk], b_, ALU.mult, [R("tB"), ("qsil", b // 2)], [R("qt")])
                yield
                act(b_, a_, AF.Exp, [R("tA")], [R("tB")], scale=-1.0)
                yield
                tt(B_["kt"], B_["kraw"], b_, ALU.mult, [R("kraw"), R("tB")], [R("kt")])
                tt(a_.rearrange("p (c j) -> p c j", j=32), g3, refn.unsqueeze(2).broadcast_to([128, 8, 32]),
                   ALU.subtract, [R("G"), R("refn")], [R("tA")])
                yield
                act(b_, a_, AF.Exp, [R("tA")], [R("tB")], scale=-1.0)
                yield
                tt(B_["kh"], B_["kraw"], b_, ALU.mult, [R("kraw"), R("tB")], [R("kh")])
                yield
                pb = ps[pbk][:, :].bitcast(BF16)
                for cc in range(8):
                    q_, slot_ = cc % 3, cc // 3
                    tr(pb[32 * q_:32 * q_ + 32, slot_ * 128:(slot_ + 1) * 128], B_["kh"][:, cc * 32:(cc + 1) * 32], identb,
                       [R("kh"), "identb"], [PSR(pbk)])
                act(B_["Khp"][0:96, 0:2, :], pb[0:96, 0:256].rearrange("p (a b) -> p a b", a=2), AF.Copy, [PSR(pbk)], [R("Khp")])
                act(B_["Khp"][0:64, 2, :], pb[0:64, 256:384], AF.Copy, [PSR(pbk)], [R("Khp")])
                yield

            for d in range(2):
                ld(f"s0{d}", send[d][0], s0_d[l, d, h, :, :], [("send", d, 0)])
            nsend = [1, 1]
            NI = NB * 8

            Asb = [[alloc([128, 3, 32], BF16) for _ in range(2)] for _ in range(2)]

            def batch(step):
                for d in range(2):
                    b = step if d == 0 else NB - 1 - step
                    B_ = BUF[d][step % 2]
                    tag = (d, step % 2)
                    for cl in range(8):
                        q_ = cl % 3
                        pr = slice(32 * q_, 32 * q_ + 32)
                        mm(ps[q_][:, (cl // 3) * 128:(cl // 3 + 1) * 128], B_["Khp"][pr, cl // 3, :], Vp[pr, b * 3 + cl // 3, :], True, True,
                           [("Khp",) + tag, "Vp"], [PSR(q_)])
                    for q_ in range(3):
                        n_ = 3 if q_ < 2 else 2
                        act(KVsb[d][step % 2][:, q_:8:3, :], ps[q_][:, 0:n_ * 128].rearrange("p (a b) -> p a b", a=n_), AF.Copy,
                            [PSR(q_)], [("KVsb",) + tag])
                    for cl in range(8):
                        pr = slice(32 * (cl % 3), 32 * (cl % 3) + 32)
                        cs = slice(cl * 32, (cl + 1) * 32)
                        aslot = ps[4][pr, (d * 3 + cl // 3) * 32:(d * 3 + cl // 3 + 1) * 32]
                        mm(aslot, B_["kt"][:, cs], PERS[d][step % 3]["qt"][:, cs], True, True, [("kt",) + tag, ("qt", d, step % 3)], [PSR(4)])
                for d in range(2):
                    tag = (d, step % 2)
                    A_ = Asb[d][step % 2]
                    tt(A_[0:64, 0:3, :], ps[4][0:64, d * 96:d * 96 + 96].rearrange("p (a b) -> p a b", a=3),
                       tri[0:64, d, :].unsqueeze(1).broadcast_to([64, 3, 32]), ALU.mult, [PSR(4), "tri"], [("Asb",) + tag])
                    tt(A_[64:96, 0:2, :], ps[4][64:96, d * 96:d * 96 + 64].rearrange("p (a b) -> p a b", a=2),
                       tri[64:96, d, :].unsqueeze(1).broadcast_to([32, 2, 32]), ALU.mult, [PSR(4), "tri"], [("Asb",) + tag])

            def run_interleaved(gens):
                gens = list(gens)
                waiting = []
                while gens or waiting:
                    if not gens:
                        gens, waiting = waiting, []
                    for g in list(gens):
                        try:
                            r = next(g)
                            if r == "WAITPREP":
                                gens.remove(g)
                                waiting.append(g)
                        except StopIteration:
                            gens.remove(g)

            NI = NB * 8

            def chain_gen(step, d):
                for ci in range(8):
                    i = step * 8 + ci
                    if True:
                        cl = ci if d == 0 else 7 - ci
                        b = step if d == 0 else NB - 1 - step
                        c = b * 8 + cl
                        B_ = BUF[d][step % 2]
                        Pp = PERS[d][step % 3]
                        ptag = (d, step % 3)
                        tag = (d, step % 2)
                        pr = slice(32 * (cl % 3), 32 * (cl % 3) + 32)
                        cs = slice(cl * 32, (cl + 1) * 32)
                        A = Asb[d][step % 2][pr, cl // 3, :]
                        kvs = KVsb[d][step % 2][:, cl, :]
                        grp = c // 16
                        ob = 5 + d
                        oslice = ps[ob][:, (c % 16) * 32:(c % 16 + 1) * 32]
                        mm(oslice, Spb[d][i % 4], Pp["qt"][:, cs], True, False, [("Spb", d, i % 4), ("qt",) + ptag], [PSR(ob)])
                        mm(oslice, Vp[pr, b * 3 + cl // 3, :], A, False, True, ["Vp", ("Asb",) + tag], [PSR(ob)])
                        boundary = (ci == 7)
                        last = (i == NI - 1)
                        nxt = (i + 1) % 2
                        dn = Pp["Dn"][:, cl:cl + 1]
                        if boundary:
                            sd_ = send[d][nsend[d] % 2]
                            sres_ = ("send", d, nsend[d] % 2)
                            nsend[d] += 1
                            stt(sd_, Sp[d][i % 2], dn, kvs, ALU.mult, ALU.add, [("Sp", d, i % 2), ("Dn",) + ptag, ("KVsb",) + tag], [sres_])
                            st(f"so{d}", ost_d[l, b, d, h, :, :], sd_, [sres_])
                            if not last:
                                ntag = (d, (step + 1) % 3)
                                ts(Sp[d][nxt], sd_, PERS[d][(step + 1) % 3]["tin"], None, ALU.mult, None, [sres_, ("tin",) + ntag], [("Sp", d, nxt)])
                        else:
                            stt(Sp[d][nxt], Sp[d][i % 2], dn, kvs, ALU.mult, ALU.add, [("Sp", d, i % 2), ("Dn",) + ptag, ("KVsb",) + tag], [("Sp", d, nxt)])
                        if not last:
                            act(Spb[d][(i + 1) % 4], Sp[d][nxt], AF.Copy, [("Sp", d, nxt)], [("Spb", d, (i + 1) % 4)])
                        gdone = (c % 16 == 15) if d == 0 else (c % 16 == 0)
                        if gdone:
                            fstep = grp * 16 + 15
                            bstep = NI - 1 - grp * 16
                            mine = fstep if d == 0 else bstep
                            other = bstep if d == 0 else fstep
                            first = mine < other or (mine == other and d == 0)
                            osl = o_acc[:, grp * 512:(grp + 1) * 512]
                            if first:
                                act(osl, ps[ob][:, :], AF.Copy, [PSR(ob)], [("oacc", grp)])
                            else:
                                tt(osl, ps[ob][:, :], osl, ALU.add, [PSR(ob), ("oacc", grp)], [("oacc", grp)])
                        yield

            g0 = [prep(0, 0), prep(1, 0)]
            if NB > 1:
                g0 += [prep(0, 1), prep(1, 1)]
            run_interleaved(g0)
            for d in range(2):
                ts(Sp[d][0], send[d][0], PERS[d][0]["tin"], None, ALU.mult, None, [("send", d, 0), ("tin", d, 0)], [("Sp", d, 0)])
                act(Spb[d][0], Sp[d][0], AF.Copy, [("Sp", d, 0)], [("Spb", d, 0)])
            batch(0)
            for step in range(NB):
                if step + 1 < NB:
                    batch(step + 1)
                gens = [chain_gen(step, 0), chain_gen(step, 1)]
                if step + 2 < NB:
                    gens += [prep(0, step + 2), prep(1, step + 2)]
                run_interleaved(gens)
            P.barrier()
            save_top = top[0]
            top[0] = scr_top
            mxs = [alloc([128, 512], BF16) for _ in range(2)]
            rA = [alloc([128, 512]) for _ in range(2)]
            rB = [alloc([128, 512]) for _ in range(2)]
            assert top[0] <= scr_end
            top[0] = save_top
            for tg in range(NG):
                sl = slice(tg * 512, (tg + 1) * 512)
                a = rA[tg % 2]
                b = rB[tg % 2]
                act(a, o_acc[:, sl], AF.Square, [("oacc", tg)], [("rA", tg % 2)])
                bk = 7
                mm(ps[bk][:, :], ones1, a, True, True, [("rA", tg % 2), "ones1"], [PSR(bk)])
                act(b, ps[bk][:, :], AF.Sqrt, [PSR(bk)], [("rB", tg % 2)], bias=1e-6, scale=1.0 / 128.0)
                recip(b, b, [("rB", tg % 2)], [("rB", tg % 2)])
                tt(a, o_acc[:, sl], b, ALU.mult, [("oacc", tg), ("rB", tg % 2)], [("rA", tg % 2)])
                stt(mxs[tg % 2], a, hgnw[:, l, h:h + 1], gsil[:, sl], ALU.mult, ALU.mult,
                    [("rA", tg % 2), "hgnw"], [("mxs", tg % 2)])
                st(f"mx{tg % 2}", mix_d[8 + h, :, sl], mxs[tg % 2], [("mxs", tg % 2)])
            P.barrier()
        top[0] = layer_top
        if stop_after == "hgrn":
            raise StopBuild()

        kr = alloc([128, 2, T], BF16)
        Vt = alloc([128, NT, 256], BF16)
        ckT = alloc([128, 2, 256], BF16)
        cvb = alloc([128, 2, 256], BF16)
        amask = alloc([128, NQB, 2, 128], BF16)
        esink = alloc([128, 8])
        qr = alloc([128, 4, T], BF16)
        att_top = top[0]
        ckf = alloc([128, 2, 256])
        cvf = alloc([128, 2, 256])
        cst = [alloc([128, 512]) for _ in range(2)]
        snt = [alloc([128, 512]) for _ in range(2)]
        zq = [alloc([128, 512]) for _ in range(2)]
        r1 = [alloc([128, 512]) for _ in range(2)]
        r2 = [alloc([128, 512]) for _ in range(2)]
        ktok = [alloc([128, 4, 128]) for _ in range(2)]
        vtok = [alloc([128, 256]) for _ in range(2)]
        top[0] = att_top
        PT = [alloc([128, 5, 512], BF16) for _ in range(2)]
        rec = [alloc([128, 512]) for _ in range(2)]
        gst = [alloc([128, 512], BF16) for _ in range(2)]

        for half in range(0, NQB, 8):
            hi_ = min(half + 8, NQB)
            dst = amask[:, half:hi_, :, :]
            src = amask_d[:, half:hi_, :, :]
            P.dma(POOL, "am", lambda e, dst=dst, src=src: e.dma_start(out=dst, in_=src), (), ["amask"])
        ld("c1", ckf, ck_d[l].rearrange("(kb p) c -> p kb c", p=128), ["ckf"])
        ld("c2", cvf, cv_d[l].rearrange("(kb p) c -> p kb c", p=128), ["cvf"])
        cp(cvb, cvf, ["cvf"], ["cvb"])
        ld("c3", esink, sink_d[:, l, :], ["esink"])
        act(esink, esink, AF.Exp, ["esink"], ["esink"])
        for kb in range(2):
            for g in range(2):
                bk = nextbank()
                tr(ps[bk][:, 0:128], ckf[:, kb, g * 128:(g + 1) * 128], identf, ["ckf", "identf"], [PSR(bk)])
                act(ckT[:, g, kb * 128:(kb + 1) * 128], ps[bk][:, 0:128], AF.Copy, [PSR(bk)], ["ckT"])

        chk('attn_a')
        ropecnt = [0]

        def rope(src_bank, dst, tg):
            i = ropecnt[0] % 2
            ropecnt[0] += 1
            sl = slice(tg * 512, (tg + 1) * 512)
            ld(f"cs{i}", cst[i], cos_d[:, sl], [("cst", i)])
            ld(f"sn{i}", snt[i], sin_d[:, sl], [("snt", i)])
            act(zq[i], ps[src_bank][:, :], AF.Copy, [PSR(src_bank)], [("zq", i)])
            bk2 = nextbank(4, 8)
            mm(ps[bk2][:, :], rm, zq[i], True, True, ["rm", ("zq", i)], [PSR(bk2)])
            tt(r1[i], zq[i], cst[i], ALU.mult, [("zq", i), ("cst", i)], [("r1", i)])
            tt(r2[i], ps[bk2][:, :], snt[i], ALU.mult, [PSR(bk2), ("snt", i)], [("r2", i)])
            tt(dst, r1[i], r2[i], ALU.add, [("r1", i), ("r2", i)], ["ropeout"])
            return i

        sb, sres = acquire()
        for kvh in range(2):
            for tg in range(NG):
                bk = nextbank(0, 4)
                inproj_fm(sb, sres, kvh * 128, tg, bk)
                i = rope(bk, kr[:, kvh, tg * 512:(tg + 1) * 512], tg)
                bk3 = nextbank(4, 8)
                for a in range(4):
                    tr(ps[bk3][:, a * 128:(a + 1) * 128], zq[i][:, a * 128:(a + 1) * 128], identf, [("zq", i), "identf"], [PSR(bk3)])
                kt_ = ktok[(kvh * NG + tg) % 2]
                kres = ("ktok", (kvh * NG + tg) % 2)
                act(kt_, ps[bk3][:, :].rearrange("p (a b) -> p a b", a=4), AF.Copy, [PSR(bk3)], [kres])
                st(f"ko{(kvh * NG + tg) % 2}", ok_d[l, tg * 512:(tg + 1) * 512, kvh * 128:(kvh + 1) * 128].rearrange("(a p) d -> p a d", p=128),
                   kt_, [kres])
        chk('attn_b')
        for tt_ in range(NT):
            bk = nextbank(0, 4)
            for kc in range(16):
                mm(ps[bk][:, 0:256], hT[:, kc, tt_ * 128:(tt_ + 1) * 128], sb[:, kc, 256:512], kc == 0, kc == 15, [sres], [PSR(bk)])
            vt_ = vtok[tt_ % 2]
            act(vt_, ps[bk][:, 0:256], AF.Copy, [PSR(bk)], [("vtok", tt_ % 2)])
            cp(Vt[:, tt_, :], vt_, [("vtok", tt_ % 2)], ["Vt"])
            st(f"vo{tt_ % 2}", ov_d[l, tt_ * 128:(tt_ + 1) * 128, :], vt_, [("vtok", tt_ % 2)])

        chk('attn_kv')
        SCALE = 128 ** -0.5
        P.barrier()
        for g in range(2):
            sb, sres = acquire()
            for hh in range(4):
                for tg in range(NG):
                    bk = nextbank(0, 4)
                    inproj_fm(sb, sres, hh * 128, tg, bk)
                    rope(bk, qr[:, hh, tg * 512:(tg + 1) * 512], tg)
            P.barrier()
            chk('attn_q')
            def slots_of(n):
                kbL = max(n - 1, 0)
                kbR = min(n + 1, NQB - 1)
                return [(kr[:, g, kbL * 128:(kbL + 1) * 128], Vt[:, kbL, g * 128:(g + 1) * 128], 0, onesb),
                        (kr[:, g, n * 128:(n + 1) * 128], Vt[:, n, g * 128:(g + 1) * 128], None, onesb),
                        (kr[:, g, kbR * 128:(kbR + 1) * 128], Vt[:, kbR, g * 128:(g + 1) * 128], 1, onesb),
                        (ckT[:, g, 0:128], cvb[:, 0, g * 128:(g + 1) * 128], None, ctxones),
                        (ckT[:, g, 128:256], cvb[:, 1, g * 128:(g + 1) * 128], None, ctxones)]

            def S_part(n):
                pt = PT[n % 2]
                pres = ("PT", n % 2)
                qsl = qr[:, :, n * 128:(n + 1) * 128]
                for si, (kT, vv, mside, onesl) in enumerate(slots_of(n)):
                    bk = nextbank(0, 4)
                    mm(ps[bk][:, :], kT, qsl, True, True, ["kr", "ckT", ("qrn", n)], [PSR(bk)])
                    act(pt[:, si, :], ps[bk][:, :], AF.Exp, [PSR(bk)], [pres], scale=SCALE)
                    if mside is not None:
                        v4 = pt[:, si, :].rearrange("p (h q) -> p h q", h=4)
                        tt(v4, v4, amask[:, n, mside, :].unsqueeze(1).broadcast_to([128, 4, 128]), ALU.mult, [pres, "amask"], [pres])

            def F_part(n):
                pt = PT[n % 2]
                pres = ("PT", n % 2)
                qsl = qr[:, :, n * 128:(n + 1) * 128]
                slots = slots_of(n)
                pvb = 4 + 2 * (n % 2)
                smb = pvb + 1
                for si, (kT, vv, mside, onesl) in enumerate(slots):
                    mm(ps[pvb][:, :], vv, pt[:, si, :], si == 0, si == 4, [pres, "Vt", "cvb"], [PSR(pvb)])
                for si, (kT, vv, mside, onesl) in enumerate(slots):
                    mm(ps[smb][:, :], onesl, pt[:, si, :], si == 0, si == 4, [pres, "onesb", "ctxones"], [PSR(smb)])
                rc = rec[n % 2]
                tt(rc.rearrange("p (h q) -> p h q", h=4), ps[smb][:, :].rearrange("p (h q) -> p h q", h=4),
                   esink[:, g * 4:(g + 1) * 4].unsqueeze(2).broadcast_to([128, 4, 128]), ALU.add, [PSR(smb), "esink"], [("rec", n % 2)])
                recip(rc, rc, [("rec", n % 2)], [("rec", n % 2)])
                tt(qsl, ps[pvb][:, :].rearrange("p (h q) -> p h q", h=4), rc.rearrange("p (h q) -> p h q", h=4), ALU.mult,
                   [PSR(pvb), ("rec", n % 2)], [("qrn", n)])

            S_part(0)
            for n in range(NQB):
                if n + 1 < NQB:
                    S_part(n + 1)
                F_part(n)
            P.barrier()
            chk('attn_s')
            sb, sres = acquire()
            for hh in range(4):
                for tg in range(NG):
                    bk = nextbank(0, 4)
                    inproj_fm(sb, sres, hh * 128, tg, bk)
                    gs_ = gst[tg % 2]
                    act(gs_, ps[bk][:, :], AF.Silu, [PSR(bk)], [("gst", tg % 2)])
                    qs_ = qr[:, hh, tg * 512:(tg + 1) * 512]
                    tt(qs_, qs_, gs_, ALU.mult, [("gst", tg % 2), "qr"], ["qr"])
            st("mx0", mix_d[g * 4:(g + 1) * 4, :, :].rearrange("k p t -> p k t"), qr, ["qr"])
            P.barrier()
        top[0] = layer_top
        if stop_after == "attn":
            raise StopBuild()

        top[0] = const_top + 2 * (16 * SLABW * 2)
        wo = alloc([128, 16, 2048], BF16)
        lnw_b = alloc([128, 2048])
        lnb_b = alloc([128, 2048])
        gate_b = alloc([128, 2048])
        screp = alloc([128, 16, 128], BF16)
        cp(screp, sc_bf.unsqueeze(2).broadcast_to([128, 16, 128]), ["sc_bf"], ["screp"])
        ld("c2", lnb_b, adabg_d[l:l + 1, :].partition_broadcast(128), ["lnb_b"])
        for s_ in range(4):
            dst = wo[:, :, s_ * 512:(s_ + 1) * 512]
            src = wout_d[l][:, s_ * 512:(s_ + 1) * 512]
            P.dma(POOL, f"wo{s_ % 2}", lambda e, dst=dst, src=src: e.dma_start(out=dst, in_=src.rearrange("(kc p) n -> p kc n", p=128)),
                  (), [("wo", s_)])
            sb, sres = acquire()
            bk = nextbank(0, 4)
            for kc in range(16):
                mm(ps[bk][:, :], screp[:, kc, :], sb[:, kc, 0:512], kc == 0, kc == 15, [sres, "screp"], [PSR(bk)])
            tt(gate_b[:, s_ * 512:(s_ + 1) * 512], ps[bk][:, :], lnb_b[:, s_ * 512:(s_ + 1) * 512], ALU.add,
               [PSR(bk), "lnb_b"], [("gate_b", s_)])
        ld("c0", lnw_b, lnw_d[l:l + 1, :].partition_broadcast(128), ["lnw_b"])
        ld("c1", lnb_b, lnb_d[l:l + 1, :].partition_broadcast(128), ["lnb_b"])
        mixt = [alloc([128, 16, 128], BF16) for _ in range(2)]
        xt2 = [alloc([128, 2048]) for _ in range(2)]
        tm = [alloc([128, 512]) for _ in range(2)]
        stats = [alloc([128, 24]) for _ in range(2)]
        mv = [alloc([128, 2]) for _ in range(2)]
        sdv = [alloc([128, 2]) for _ in range(2)]

        def tile_gen(tile_i):
            p_ = tile_i % 2
            mt = mixt[p_]
            mres = ("mixt", p_)
            ld(f"mt{p_}", mt, mix_d[:, :, tile_i * 128:(tile_i + 1) * 128].rearrange("k p t -> p k t"), [mres])
            xt = xt2[p_]
            yres = ("xt2", p_)
            ld(f"x{p_}", xt, xin_d[tile_i * 128:(tile_i + 1) * 128, :], [yres])
            yield
            st_, mv_, sd_ = stats[p_], mv[p_], sdv[p_]
            for cg in range(4):
                bk = cg + 4 * p_
                for kc in range(16):
                    mm(ps[bk][:, :], mt[:, kc, :], wo[:, kc, cg * 512:(cg + 1) * 512], kc == 0, kc == 15,
                       [mres, ("wo", cg)], [PSR(bk)])
                csl = slice(cg * 512, (cg + 1) * 512)
                tm_ = tm[cg % 2]
                tt(tm_, ps[bk][:, :], gate_b[:, csl], ALU.mult, [PSR(bk), ("gate_b", cg)], [("tm", cg % 2)])
                yield
                stt(xt[:, csl], xt[:, csl], ALPHA, tm_, ALU.mult, ALU.add, [yres, ("tm", cg % 2)], [yres])
                yield
                P.op(DVE, lambda e, o=st_[:, cg * 6:(cg + 1) * 6], i_=xt[:, csl]: e.bn_stats(out=o, in_=i_), [yres], [("stats", p_)])
                yield
            yield "EPI"
            P.op(DVE, lambda e: e.bn_aggr(out=mv_, in_=st_), [("stats", p_)], [("mv", p_)])
            yield
            act(sd_[:, 0:1], mv_[:, 1:2], AF.Sqrt, [("mv", p_)], [("sdv", p_)], bias=1e-5)
            yield
            recip(sd_[:, 0:1], sd_[:, 0:1], [("sdv", p_)], [("sdv", p_)])
            yield
            ts(sd_[:, 1:2], mv_[:, 0:1], sd_[:, 0:1], -1.0, ALU.mult, ALU.mult, [("mv", p_), ("sdv", p_)], [("sdv", p_)])
            yield
            act(xt, xt, AF.Identity, [yres, ("sdv", p_)], [yres], bias=sd_[:, 1:2], scale=sd_[:, 0:1])
            yield
            tt(xt, xt, lnw_b, ALU.mult, [yres, "lnw_b"], [yres])
            yield
            tt(xt, xt, lnb_b, ALU.add, [yres, "lnb_b"], [yres])
            yield
            st(f"yo{p_}", xout_d[tile_i * 128:(tile_i + 1) * 128, :], xt, [yres])
            yield

        active = [tile_gen(0)]
        nxt_tile = 1
        mg = mod_gen(l + 1) if l + 1 < layers else None
        while active:
            for g_ in list(active):
                try:
                    r_ = next(g_)
                    if r_ == "EPI":
                        if mg is not None:
                            try:
                                next(mg)
                            except StopIteration:
                                mg = None
                        if nxt_tile < NT:
                            active.append(tile_gen(nxt_tile))
                            nxt_tile += 1
                except StopIteration:
                    active.remove(g_)
        if mg is not None:
            for _ in mg:
                pass
        P.barrier()

    try:
        for l in range(layers):
            layer_body(l)
    except StopBuild:
        pass
    P.finish()
    P.emit()
    nc._maxtop = maxtop[0]
    return nc


def _fm(v):
    v = np.asarray(v, np.float32)
    lead = v.shape[:-1]
    n = v.shape[-1] // 128
    v = v.reshape(lead + (n, 128))
    return np.ascontiguousarray(np.moveaxis(v, -1, 0))


def rope_tables(T, sample):
    d = np.arange(128)
    if not sample:
        return np.ones((128, T), np.float32), np.zeros((128, T), np.float32)
    t = np.arange(T)
    row = (t // 64).astype(np.float32)
    col = (t % 64).astype(np.float32)
    nf = 32
    inv = (np.float32(10000.0) ** (-np.arange(nf, dtype=np.float32) / np.float32(nf))).astype(np.float32)
    pos = np.where((d < 64)[:, None], row[None, :], col[None, :]).astype(np.float32)
    ang = (pos * inv[d % 32][:, None]).astype(np.float32)
    return np.cos(ang).astype(np.float32), np.sin(ang).astype(np.float32)


def rot_matrix():
    R = np.zeros((128, 128), np.float32)
    for m in range(128):
        q = (m % 64) // 32
        if q == 0:
            R[m, m + 32] = -1.0
        else:
            R[m, m - 32] = 1.0
    return np.ascontiguousarray(R.T)


def core_consts(NB, sample):
    T = NB * 256
    NQB = T // 128
    cosT, sinT = rope_tables(T, sample)
    amask = np.zeros((128, NQB, 2, 128), np.float32)
    j = np.arange(128)[:, None]
    i = np.arange(128)[None, :]
    low = (j >= i).astype(np.float32)
    up = (j <= i).astype(np.float32)
    for n in range(NQB):
        if sample:
            if n >= 1:
                amask[:, n, 0, :] = low
            if n <= NQB - 2:
                amask[:, n, 1, :] = up
        else:
            if n % 2 == 0:
                amask[:, n, 1, :] = 1.0
            else:
                amask[:, n, 0, :] = 1.0
    ctxones = np.full((128, 128), 1.0 if sample else 0.0, np.float32)
    flags = np.zeros((128, 4, NB), np.float32)
    if sample:
        flags[:, 2, 1:] = 1.0
        flags[:, 3, :NB - 1] = 1.0
    else:
        flags[:, 0, 1:] = -1e4
        flags[:, 1, :NB - 1] = -1e4
    tri = np.zeros((128, 2, 32), np.float32)
    s = (np.arange(128) % 32)[:, None]
    t = np.arange(32)[None, :]
    tri[:, 0, :] = (s <= t)
    tri[:, 1, :] = (s >= t)
    return dict(cosT=cosT, sinT=sinT, Rm=rot_matrix(), amask=amask, ctxones=ctxones, flags=flags, tri=tri)


def shared_inputs(ada_w, ada_b, w_in, w_out, attn_sink, hg_lower_bounds, hg_norm_w, conv_w, conv_b, conv_ln_w, conv_ln_b, ln_w, ln_b):
    f = lambda a: np.ascontiguousarray(np.asarray(a, np.float32))
    ada_b = f(ada_b)
    d = {}
    d["ada_w"] = f(ada_w)
    d["adab_fm"] = np.ascontiguousarray(_fm(ada_b[:, :4096]))
    d["adab_g"] = np.ascontiguousarray(ada_b[:, 4096:])
    d["w_in"] = f(w_in)
    d["w_out"] = f(w_out)
    d["sinkrep"] = np.ascontiguousarray(np.broadcast_to(f(attn_sink).reshape(1, 2, 8), (128, 2, 8)))
    d["lbraw"] = _fm(f(hg_lower_bounds).reshape(2, 2, 512))
    d["hgnw"] = _fm(hg_norm_w)
    d["convw"] = np.ascontiguousarray(np.transpose(_fm(conv_w), (0, 1, 3, 2)))
    d["convp"] = np.ascontiguousarray(np.stack([_fm(conv_b), _fm(conv_ln_w), _fm(conv_ln_b)], axis=1))
    d["lnw"] = f(ln_w)
    d["lnb"] = f(ln_b)
    return d


NBK = 8
_CACHE = {}


def kernel(x_prompt, x_sample, cache_k, cache_v, state_hgrn, c, c_ctx, ada_w, ada_b, w_in, w_out,
           attn_sink, hg_lower_bounds, hg_norm_w, conv_w, conv_b, conv_ln_w, conv_ln_b, ln_w, ln_b):
    f = lambda a: np.ascontiguousarray(np.asarray(a, np.float32))
    x_prompt = f(x_prompt)
    x_sample = f(x_sample)
    NB = NBK
    T = NB * 256
    shared = shared_inputs(ada_w, ada_b, w_in, w_out, attn_sink, hg_lower_bounds, hg_norm_w, conv_w, conv_b,
                           conv_ln_w, conv_ln_b, ln_w, ln_b)
    cs = core_consts(NB, True)
    cp_ = core_consts(NB, False)
    assign = {2: list(range(0, 6)), 3: list(range(6, 12)), 4: list(range(12, 17)), 5: list(range(17, 22)),
              6: list(range(22, 27)), 7: list(range(27, 32))}
    in_maps = []
    for core in range(8):
        m = dict(shared)
        if core < 2:
            m.update(cs)
            m["x"] = np.ascontiguousarray(x_sample[core])
            m["cond"] = np.ascontiguousarray(f(c)[core].reshape(16, 128).T)
            m["ck"] = np.ascontiguousarray(f(cache_k)[core].reshape(2, 256, 256))
            m["cv"] = np.ascontiguousarray(f(cache_v)[core].reshape(2, 256, 256))
            m["s0"] = np.ascontiguousarray(f(state_hgrn)[core])
        else:
            m.update(cp_)
            xs = np.zeros((T, D), np.float32)
            for i, s in enumerate(assign[core]):
                xs[i * 256:(i + 1) * 256] = x_prompt[s]
            m["x"] = xs
            m["cond"] = np.ascontiguousarray(f(c_ctx).reshape(16, 128).T)
            m["ck"] = np.zeros((2, 256, 256), np.float32)
            m["cv"] = np.zeros((2, 256, 256), np.float32)
            m["s0"] = np.zeros((2, 2, 4, 128, 128), np.float32)
        in_maps.append(m)
    if "nc" not in _CACHE:
        _CACHE["nc"] = build(NB)
    res = run_bass_kernel_spmd(_CACHE["nc"], in_maps, core_ids=list(range(8))).results
    y_prompt = np.zeros((32, 256, D), np.float32)
    y_sample = np.zeros((2, 2048, D), np.float32)
    nk = np.zeros((32, 2, 256, 2, 128), np.float32)
    nv = np.zeros((32, 2, 256, 2, 128), np.float32)
    nst = np.zeros((32, 2, 2, 4, 128, 128), np.float32)
    for core in range(8):
        r = res[core]
        if core < 2:
            y_sample[core] = r["y"]
        else:
            for i, s in enumerate(assign[core]):
                y_prompt[s] = r["y"][i * 256:(i + 1) * 256]
                nk[s] = r["ok"][:, i * 256:(i + 1) * 256, :].reshape(2, 256, 2, 128)
                nv[s] = r["ov"][:, i * 256:(i + 1) * 256, :].reshape(2, 256, 2, 128)
                nst[s] = r["ost"][:, i]
    return (y_prompt, y_sample, nk, nv, nst)
```

```python
import math
import numpy as np
import concourse.bass as bass
import concourse.mybir as mybir
from concourse.bass_utils import run_bass_kernel_spmd

F32 = mybir.dt.float32
BF16 = mybir.dt.bfloat16
ALU = mybir.AluOpType
AF = mybir.ActivationFunctionType

PE, ACT, DVE, POOL, SP = "pe", "act", "dve", "pool", "sp"
SAME_ENG_SYNC = True

D = 2048
KC = 16
DIN = 6656
ALPHA = float(4 ** 0.25)
C_AQ, C_AK, C_AV, C_AG = 0, 1024, 1280, 1536
C_HQ, C_HI, C_HFF, C_HFB, C_HG = 2560, 3072, 3584, 4096, 4608
C_CA, C_CB, C_CG = 5120, 5632, 6144
SLABW = 512


class StopBuild(Exception):
    pass


class Prog:
    def __init__(self, nc):
        self.nc = nc
        self.eng = {PE: nc.tensor, ACT: nc.scalar, DVE: nc.vector, POOL: nc.gpsimd, SP: nc.sync}
        self.ops = {e: [] for e in self.eng}
        self.sems = {}
        self.cnt = {}
        self.seen = {e: {} for e in self.eng}
        self.last_w = {}
        self.readers = {}
        self.nops = 0

    def _sem(self, key):
        if key not in self.sems:
            name = "s_" + ("_".join(str(k) for k in key) if isinstance(key, tuple) else str(key))
            self.sems[key] = self.nc.alloc_semaphore(name)
            self.cnt[key] = 0
        return self.sems[key]

    def _deps(self, reads, writes):
        deps = {}

        def add(tok):
            k, v = tok
            if deps.get(k, 0) < v:
                deps[k] = v
        for r in reads:
            if r in self.last_w:
                add(self.last_w[r])
        for w in writes:
            if w in self.last_w:
                add(self.last_w[w])
            for tok in self.readers.get(w, ()):
                add(tok)
        return deps

    def _commit(self, tok, reads, writes):
        for r in reads:
            self.readers.setdefault(r, []).append(tok)
        for w in writes:
            self.last_w[w] = tok
            self.readers[w] = []

    def _waits(self, eng, deps, own_key, same_sync):
        waits = []
        for k, v in deps.items():
            if k == own_key and not same_sync:
                continue
            if self.seen[eng].get(k, 0) >= v:
                continue
            self.seen[eng][k] = v
            waits.append((self._sem(k), v))
        return waits

    def op(self, eng, fn, reads=(), writes=(), same_sync=None):
        if same_sync is None:
            same_sync = SAME_ENG_SYNC and eng != PE
        reads = list(reads)
        writes = list(writes)
        deps = self._deps(reads, writes)
        waits = self._waits(eng, deps, eng, same_sync)
        sem = self._sem(eng)
        self.cnt[eng] += 1
        tok = (eng, self.cnt[eng])
        self.ops[eng].append((waits, fn, (sem, 1)))
        self._commit(tok, reads, writes)
        self.nops += 1
        return tok

    def dma(self, queue, ch, fn, reads=(), writes=()):
        key = ("dma", ch)
        reads = list(reads)
        writes = list(writes)
        deps = self._deps(reads, writes)
        sem = self._sem(key)
        if self.cnt[key] > 0:
            v = 16 * self.cnt[key]
            if deps.get(key, 0) < v:
                deps[key] = v
        waits = self._waits(queue, deps, None, True)
        self.cnt[key] += 1
        tok = (key, 16 * self.cnt[key])
        self.ops[queue].append((waits, fn, (sem, 16)))
        self._commit(tok, reads, writes)
        self.nops += 1
        return tok

    def barrier(self, keep_prefix=("slab",), skip_ch_prefix=("w",)):
        toks = {}
        for k, c in self.cnt.items():
            if c == 0:
                continue
            if isinstance(k, tuple):
                if str(k[1]).startswith(skip_ch_prefix):
                    continue
                toks[k] = 16 * c
            else:
                toks[k] = c
        for e in self.eng:
            waits = self._waits(e, dict(toks), None, True)
            if waits:
                self.ops[e].append((waits, None, None))

        def keep(r):
            name = r[0] if isinstance(r, tuple) else r
            return str(name).startswith(keep_prefix)
        self.last_w = {r: t for r, t in self.last_w.items() if keep(r)}
        self.readers = {r: t for r, t in self.readers.items() if keep(r)}

    def finish(self):
        toks = {}
        for k, c in self.cnt.items():
            if c == 0:
                continue
            toks[k] = 16 * c if isinstance(k, tuple) else c
        for e in self.eng:
            waits = self._waits(e, dict(toks), None, True)
            if waits:
                self.ops[e].append((waits, None, None))

    def emit(self):
        nc = self.nc
        with nc.Block() as block:
            def mk(e):
                def body(engine):
                    for waits, fn, inc in self.ops[e]:
                        for sem, v in waits:
                            engine.wait_ge(sem, v)
                        if fn is not None:
                            ins = fn(engine)
                            if inc is not None:
                                ins.then_inc(inc[0], inc[1])
                return body
            block.sync(mk(SP))
            block.tensor(mk(PE))
            block.scalar(mk(ACT))
            block.vector(mk(DVE))
            block.gpsimd(mk(POOL))


def build(NB, dbg=False, layers=2, stop_after=None):
    T = NB * 256
    NT = T // 128
    NG = T // 512
    NC = T // 32
    NQB = NT
    nc = bass.Bass("TRN2", target_bir_lowering=False)
    P = Prog(nc)

    def din(name, shape, dt=F32):
        return nc.dram_tensor(name, list(shape), dt, kind="ExternalInput").ap()

    def dout(name, shape, dt=F32, internal=False):
        return nc.dram_tensor(name, list(shape), dt, kind=("Internal" if internal else "ExternalOutput")).ap()

    x_d = din("x", [T, D])
    cond_d = din("cond", [128, 16])
    adaw_d = din("ada_w", [2, D, 6144])
    adabfm_d = din("adab_fm", [128, 2, 32])
    adabg_d = din("adab_g", [2, 2048])
    win_d = din("w_in", [2, D, DIN])
    wout_d = din("w_out", [2, D, D])
    ck_d = din("ck", [2, 256, 256])
    cv_d = din("cv", [2, 256, 256])
    s0_d = din("s0", [2, 2, 4, 128, 128])
    sink_d = din("sinkrep", [128, 2, 8])
    lbraw_d = din("lbraw", [128, 2, 2, 4])
    hgnw_d = din("hgnw", [128, 2, 4])
    convw_d = din("convw", [128, 2, 4, 31])
    convp_d = din("convp", [128, 3, 2, 4])
    lnw_d = din("lnw", [2, 2048])
    lnb_d = din("lnb", [2, 2048])
    cos_d = din("cosT", [128, T])
    sin_d = din("sinT", [128, T])
    rm_d = din("Rm", [128, 128])
    amask_d = din("amask", [128, NQB, 2, 128])
    ctxones_d = din("ctxones", [128, 128])
    flags_d = din("flags", [128, 4, NB])
    tri_d = din("tri", [128, 2, 32])

    y_d = dout("y", [T, D])
    ok_d = dout("ok", [2, T, 256])
    ov_d = dout("ov", [2, T, 256])
    ost_d = dout("ost", [2, NB, 2, 4, 128, 128])
    x1_d = dout("x1s", [T, D], internal=not dbg)
    mix_d = dout("mixs", [16, 128, T], BF16, internal=not dbg)

    ARENA = 176 * 1024
    arena = nc.alloc_sbuf_tensor("arena", [128, ARENA // 4], F32)
    top = [0]
    maxtop = [0]

    def alloc(shape, dt=F32, parts=None):
        esz = 4 if dt == F32 else 2
        n = 1
        for s in shape[1:]:
            n *= s
        nb = (n * esz + 31) // 32 * 32
        off = top[0]
        top[0] += nb
        maxtop[0] = max(maxtop[0], top[0])
        assert top[0] <= ARENA, f"SBUF arena overflow {top[0]}"
        p = shape[0]
        ap = arena[0:p, off // 4:(off + nb) // 4]
        if dt != F32:
            ap = ap.bitcast(dt)
        ap = ap[:, 0:n]
        if len(shape) == 3:
            ap = ap.rearrange("p (a b) -> p a b", a=shape[1])
        elif len(shape) == 4:
            ap = ap.rearrange("p (a b c) -> p a b c", a=shape[1], b=shape[2])
        return ap

    ps = [nc.alloc_psum_tensor(f"ps{i}", [128, 512], F32) for i in range(8)]

    def PSR(i):
        return ("ps", i)

    def mm(out, lhsT, rhs, start, stop, reads, writes):
        P.op(PE, lambda e: e.matmul(out, lhsT, rhs, start=start, stop=stop), reads, writes)

    def tr(out, in_, ident, reads, writes):
        P.op(PE, lambda e: e.transpose(out, in_, ident), reads, writes)

    def act(out, in_, func, reads, writes, bias=None, scale=None):
        kw = {}
        if bias is not None:
            kw["bias"] = bias
        if scale is not None:
            kw["scale"] = scale
        P.op(ACT, lambda e: e.activation(out=out, in_=in_, func=func, **kw), reads, writes)

    def tt(out, in0, in1, op, reads, writes, eng=DVE):
        P.op(eng, lambda e: e.tensor_tensor(out=out, in0=in0, in1=in1, op=op), reads, writes)

    def ts(out, in0, s1, s2, op0, op1, reads, writes, eng=DVE):
        if op1 is None:
            P.op(eng, lambda e: e.tensor_scalar(out=out, in0=in0, scalar1=s1, scalar2=None, op0=op0), reads, writes)
        else:
            P.op(eng, lambda e: e.tensor_scalar(out=out, in0=in0, scalar1=s1, scalar2=s2, op0=op0, op1=op1), reads, writes)

    def stt(out, in0, scalar, in1, op0, op1, reads, writes):
        P.op(DVE, lambda e: e.scalar_tensor_tensor(out=out, in0=in0, scalar=scalar, in1=in1, op0=op0, op1=op1), reads, writes)

    def cp(out, in_, reads, writes, eng=DVE):
        P.op(eng, lambda e: e.tensor_copy(out, in_), reads, writes)

    def recip(out, in_, reads, writes):
        P.op(DVE, lambda e: e.reciprocal(out, in_), reads, writes)

    def memset(ap, val, writes, eng=DVE):
        P.op(eng, lambda e: e.memset(ap, val), (), writes)

    def ld(ch, out, in_, writes, reads=()):
        return P.dma(SP, ch, lambda e: e.dma_start(out=out, in_=in_), reads, writes)

    def st(ch, out, in_, reads, writes=()):
        return P.dma(SP, ch, lambda e: e.dma_start(out=out, in_=in_), reads, writes)

    identf = alloc([128, 128])
    identb = alloc([128, 128], BF16)
    ones512 = alloc([128, 128])
    ones1 = alloc([128, 128])
    onesb = alloc([128, 128], BF16)
    ctxones = alloc([128, 128], BF16)
    ctxof = alloc([128, 128])
    tri = alloc([128, 2, 32])
    rm = alloc([128, 128])
    zc = alloc([128, 1])
    flags = alloc([128, 4, NB])
    modfm = alloc([128, 2, 32])
    lbraw = alloc([128, 2, 2, 4])
    lbv = alloc([128, 2, 2, 4])
    hgnw = alloc([128, 2, 4])
    convw = alloc([128, 2, 4, 31])
    convp = alloc([128, 3, 2, 4])
    cond = alloc([128, 16])
    sc_bf = alloc([128, 16], BF16)
    adabfm = alloc([128, 2, 32])
    const_top = top[0]

    memset(identf, 0.0, ["identf"])
    P.op(POOL, lambda e: e.affine_select(out=identf, in_=identf, pattern=[[-1, 128]], compare_op=ALU.not_equal,
                                         fill=1.0, base=0, channel_multiplier=1), ["identf"], ["identf"])
    cp(identb, identf, ["identf"], ["identb"])
    memset(ones512, 1.0 / 512.0, ["ones512"])
    memset(ones1, 1.0, ["ones1"])
    memset(onesb, 1.0, ["onesb"])
    memset(zc, 0.0, ["zc"])
    ld("c0", ctxof, ctxones_d[:, :], ["ctxof"])
    cp(ctxones, ctxof, ["ctxof"], ["ctxones"])
    ld("c1", tri, tri_d[:, :, :], ["tri"])
    ld("c2", rm, rm_d[:, :], ["rm"])
    ld("c3", flags, flags_d[:, :, :], ["flags"])
    ld("c4", lbraw, lbraw_d[:, :, :, :], ["lbraw"])
    ld("c5", hgnw, hgnw_d[:, :, :], ["hgnw"])
    ld("c6", convw, convw_d[:, :, :, :], ["convw"])
    ld("c7", convp, convp_d[:, :, :, :], ["convp"])
    ld("c0", cond, cond_d[:, :], ["cond"])
    ld("c1", adabfm, adabfm_d[:, :, :], ["adabfm"])
    act(sc_bf, cond, AF.Silu, ["cond"], ["sc_bf"])

    slab = [alloc([128, 16, SLABW], BF16) for _ in range(2)]
    slab_specs = []

    def recorded_dma(i):
        b = i % 2
        for (src, off, n) in slab_specs[i]:
            dst = slab[b][:, :, off:off + n]
            P.dma(POOL, f"w{b}", lambda e, dst=dst, src=src: e.dma_start(
                out=dst, in_=src.rearrange("(kc p) n -> p kc n", p=128)), (), [("slab", b)])

    slab_state = {"next": 0}

    def acquire():
        i = slab_state["next"]
        if i == 0:
            recorded_dma(0)
        if i + 1 < len(slab_specs):
            recorded_dma(i + 1)
        slab_state["next"] = i + 1
        return slab[i % 2], ("slab", i % 2)

    def seg(w, c0, n, off):
        return (w[:, c0:c0 + n], off, n)

    for s in range(8):
        slab_specs.append([seg(adaw_d[0], s * 512, 512, 0)])
    for l in range(layers):
        w = win_d[l]
        for j in range(4):
            slab_specs.append([seg(w, C_CA + j * 128, 128, 0), seg(w, C_CB + j * 128, 128, 128)])
        slab_specs.append([seg(w, C_CG, 512, 0)])
        for h in range(4):
            slab_specs.append([seg(w, C_HI + h * 128, 128, 0)])
            slab_specs.append([seg(w, C_HQ + h * 128, 128, 0), seg(w, C_HFF + h * 128, 128, 128), seg(w, C_HFB + h * 128, 128, 256),
                               seg(w, C_HG + h * 128, 128, 384)])
        slab_specs.append([seg(w, C_AK, 512, 0)])
        for g in range(2):
            slab_specs.append([seg(w, C_AQ + g * 512, 512, 0)])
            slab_specs.append([seg(w, C_AG + g * 512, 512, 0)])
        for s in range(8, 12):
            slab_specs.append([seg(adaw_d[l], s * 512, 512, 0)])
        if l + 1 < layers:
            for s in range(8):
                slab_specs.append([seg(adaw_d[l + 1], s * 512, 512, 0)])

    bank_rr = [0]

    def nextbank(lo=0, hi=8):
        b = lo + bank_rr[0] % (hi - lo)
        bank_rr[0] += 1
        return b

    for l in range(1):
        for s in range(8):
            sb, sres = acquire()
            for j in range(4):
                cb = s * 4 + j
                for kc in range(16):
                    mm(ps[7][:, cb:cb + 1], sb[:, kc, j * 128:(j + 1) * 128], sc_bf[:, kc:kc + 1], kc == 0, kc == 15,
                       [sres, "sc_bf"], [PSR(7)])
        tt(modfm[:, l, :], ps[7][:, 0:32], adabfm[:, l, :], ALU.add, [PSR(7), "adabfm"], [("modfm", l)])
        ts(modfm[:, l, 16:32], modfm[:, l, 16:32], 1.0, None, ALU.add, None, [("modfm", l)], [("modfm", l)])

    def mod_gen(l2):
        for s in range(8):
            sb, sres = acquire()
            for j in range(4):
                for kc in range(16):
                    mm(ps[7][:, j:j + 1], sb[:, kc, j * 128:(j + 1) * 128], sc_bf[:, kc:kc + 1], kc == 0, kc == 15,
                       [sres, "sc_bf"], [PSR(7)])
            tt(modfm[:, l2, s * 4:(s + 1) * 4], ps[7][:, 0:4], adabfm[:, l2, s * 4:(s + 1) * 4], ALU.add, [PSR(7), "adabfm"], [("modfm", l2)])
            if s == 7:
                ts(modfm[:, l2, 16:32], modfm[:, l2, 16:32], 1.0, None, ALU.add, None, [("modfm", l2)], [("modfm", l2)])
            yield
    P.barrier()
    top[0] = const_top + 2 * (16 * SLABW * 2)

    hT = alloc([128, 16, T], BF16)
    layer_top = top[0]

    def chk(name):
        if stop_after == name:
            raise StopBuild()

    def layer_body(l):
        xin_d = x_d if l == 0 else x1_d
        xout_d = y_d if l == layers - 1 else x1_d
        top[0] = layer_top
        xb = [alloc([128, 2048]) for _ in range(3)]
        for tile in range(NT):
            xt = xb[tile % 3]
            xres = ("xb", tile % 3)
            ld(f"x{tile % 3}", xt, xin_d[tile * 128:(tile + 1) * 128, :], [xres])
            for j in range(4):
                bk = nextbank()
                for q in range(4):
                    kc = 4 * j + q
                    tr(ps[bk][:, q * 128:(q + 1) * 128], xt[:, kc * 128:(kc + 1) * 128], identf, [xres, "identf"], [PSR(bk)])
                for q in range(4):
                    kc = 4 * j + q
                    dst = hT[:, kc, tile * 128:(tile + 1) * 128]
                    src = ps[bk][:, q * 128:(q + 1) * 128]
                    if j % 2 == 0:
                        act(dst, src, AF.Identity, [PSR(bk), ("modfm", l)], [("hT", kc, tile)],
                            bias=modfm[:, l, kc:kc + 1], scale=modfm[:, l, 16 + kc:17 + kc])
                    else:
                        ts(dst, src, modfm[:, l, 16 + kc:17 + kc], modfm[:, l, kc:kc + 1], ALU.mult, ALU.add,
                           [PSR(bk), ("modfm", l)], [("hT", kc, tile)])
        P.barrier()
        top[0] = layer_top
        if stop_after == "H":
            raise StopBuild()

        def inproj_fm(sb, sres, c0, tg, bk):
            for kc in range(16):
                mm(ps[bk][:, :], sb[:, kc, c0:c0 + 128], hT[:, kc, tg * 512:(tg + 1) * 512], kc == 0, kc == 15,
                   [sres], [PSR(bk)])

        y_all = alloc([128, 4, T])
        conv_top = top[0]
        ca_t = [alloc([128, 512]) for _ in range(2)]
        u_pad = alloc([128, NB, 286], BF16)
        diagw = alloc([128, 31, 128], BF16)
        sigt = [alloc([128, 512]) for _ in range(2)]
        for j in range(4):
            sb, sres = acquire()
            tt(diagw, identb.unsqueeze(1).broadcast_to([128, 31, 128]),
               convw[:, l, j, :].unsqueeze(2).broadcast_to([128, 31, 128]), ALU.mult, ["identb", "convw"], ["diagw"])
            for tg in range(NG):
                bk = nextbank(0, 4)
                inproj_fm(sb, sres, 0, tg, bk)
                ca_ = ca_t[tg % 2]
                act(ca_, ps[bk][:, :], AF.Copy, [PSR(bk)], [("ca", tg % 2)])
                bk = nextbank(0, 4)
                inproj_fm(sb, sres, 128, tg, bk)
                sg = sigt[tg % 2]
                act(sg, ps[bk][:, :], AF.Sigmoid, [PSR(bk)], [("sig", tg % 2)])
                tt(u_pad[:, 2 * tg:2 * tg + 2, 15:271], ca_.rearrange("p (b t) -> p b t", b=2),
                   sg.rearrange("p (b t) -> p b t", b=2), ALU.mult, [("ca", tg % 2), ("sig", tg % 2)], ["u_pad"])
            memset(u_pad[:, 0, 0:15], 0.0, ["u_pad"])
            memset(u_pad[:, NB - 1, 271:286], 0.0, ["u_pad"])
            if NB > 1:
                tt(u_pad[:, 1:NB, 0:15], u_pad[:, 0:NB - 1, 256:271], flags[:, 2, 1:NB].unsqueeze(2).broadcast_to([128, NB - 1, 15]),
                   ALU.mult, ["u_pad", "flags"], ["u_pad"])
                tt(u_pad[:, 0:NB - 1, 271:286], u_pad[:, 1:NB, 15:30], flags[:, 3, 0:NB - 1].unsqueeze(2).broadcast_to([128, NB - 1, 15]),
                   ALU.mult, ["u_pad", "flags"], ["u_pad"])
            for tg in range(NG):
                bk = nextbank(4, 8)
                for tap in range(31):
                    mm(ps[bk][:, :], diagw[:, tap, :], u_pad[:, 2 * tg:2 * tg + 2, tap:tap + 256], tap == 0, tap == 30,
                       ["diagw", "u_pad"], [PSR(bk)])
                act(y_all[:, j, tg * 512:(tg + 1) * 512], ps[bk][:, :], AF.Identity, [PSR(bk), "convp"], [("y_all", j, tg)],
                    bias=convp[:, 0, l, j:j + 1])
        P.barrier()
        top[0] = conv_top
        ysq = [alloc([128, 512]) for _ in range(2)]
        mean_sb = alloc([128, 512])
        m2 = alloc([128, 512])
        rstd = alloc([128, 512])
        t1 = [alloc([128, 512]) for _ in range(2)]
        cgt = [alloc([128, 512], BF16) for _ in range(2)]
        mixst = [alloc([128, 4, 512], BF16) for _ in range(2)]
        sb, sres = acquire()
        for tg in range(NG):
            for j in range(4):
                q = ysq[j % 2]
                act(q, y_all[:, j, tg * 512:(tg + 1) * 512], AF.Square, [("y_all", j, tg)], [("ysq", j % 2)])
                mm(ps[4][:, :], ones512, y_all[:, j, tg * 512:(tg + 1) * 512], j == 0, j == 3, [("y_all", j, tg), "ones512"], [PSR(4)])
                mm(ps[5][:, :], ones512, q, j == 0, j == 3, [("ysq", j % 2), "ones512"], [PSR(5)])
            act(mean_sb, ps[4][:, :], AF.Copy, [PSR(4)], ["mean_sb"])
            tt(m2, mean_sb, mean_sb, ALU.mult, ["mean_sb"], ["m2"])
            tt(m2, ps[5][:, :], m2, ALU.subtract, [PSR(5), "m2"], ["m2"])
            act(rstd, m2, AF.Sqrt, ["m2"], ["rstd"], bias=1e-5)
            recip(rstd, rstd, ["rstd"], ["rstd"])
            ms = mixst[tg % 2]
            for j in range(4):
                bk = nextbank(0, 4)
                inproj_fm(sb, sres, j * 128, tg, bk)
                cg_ = cgt[j % 2]
                act(cg_, ps[bk][:, :], AF.Silu, [PSR(bk)], [("cgt", j % 2)])
                t = t1[j % 2]
                tt(t, y_all[:, j, tg * 512:(tg + 1) * 512], mean_sb, ALU.subtract, [("y_all", j, tg), "mean_sb"], [("t1", j % 2)])
                tt(t, t, rstd, ALU.mult, [("t1", j % 2), "rstd"], [("t1", j % 2)])
                act(t, t, AF.Silu, [("t1", j % 2), "convp"], [("t1", j % 2)], bias=convp[:, 2, l, j:j + 1], scale=convp[:, 1, l, j:j + 1])
                tt(ms[:, j, :], t, cg_, ALU.mult, [("t1", j % 2), ("cgt", j % 2)], [("mixst", tg % 2)])
            st(f"mx{tg % 2}", mix_d[12:16, :, tg * 512:(tg + 1) * 512].rearrange("k p t -> p k t"), ms, [("mixst", tg % 2)])
        P.barrier()
        top[0] = layer_top
        if stop_after == "conv":
            raise StopBuild()

        lbe = alloc([128, 2, 2, 4])
        lbs = alloc([128, 2, 4])
        act(lbe, lbraw, AF.Exp, ["lbraw"], ["lbe"])
        tt(lbs, lbe[:, :, 0, :], lbe[:, :, 1, :], ALU.add, ["lbe"], ["lbs"])
        recip(lbs, lbs, ["lbs"], ["lbs"])
        tt(lbs, lbs, lbe[:, :, 1, :], ALU.mult, ["lbs", "lbe"], ["lbs"])
        if l == 0:
            ts(lbs, lbs, 0.0, None, ALU.mult, None, ["lbs"], ["lbs"])
        ts(lbv[:, :, 0, :], lbs, -1.0, 1.0, ALU.mult, ALU.add, ["lbs"], ["lbv"])
        hg_top = top[0]
        LNCQ = math.log(128 ** -0.5)
        for h in range(4):
            top[0] = hg_top
            qsil = alloc([128, T], BF16)
            gsil = alloc([128, T], BF16)
            Vp = alloc([128, NB * 3, 128], BF16)
            o_acc = alloc([128, T])
            vtmp = [alloc([128, 512], BF16) for _ in range(2)]
            scr_top = top[0]
            KVsb = [[alloc([128, 8, 128]) for _ in range(2)] for _ in range(2)]
            scr_end = top[0]
            sbB, sresB = acquire()
            for tg in range(NG):
                bk = nextbank(0, 4)
                inproj_fm(sbB, sresB, 0, tg, bk)
                vt_ = vtmp[tg % 2]
                act(vt_, ps[bk][:, :], AF.Copy, [PSR(bk)], [("vtmp", tg % 2)])
                bk2 = nextbank(4, 8)
                pb = ps[bk2][:, :].bitcast(BF16)
                for blk in range(2):
                    for cl in range(8):
                        cc = blk * 8 + cl
                        q_, slot_ = cl % 3, blk * 3 + cl // 3
                        tr(pb[32 * q_:32 * q_ + 32, slot_ * 128:(slot_ + 1) * 128], vt_[:, cc * 32:(cc + 1) * 32], identb,
                           [("vtmp", tg % 2), "identb"], [PSR(bk2)])
                for blk in range(2):
                    b_ = 2 * tg + blk
                    act(Vp[0:96, b_ * 3:b_ * 3 + 2, :], pb[0:96, blk * 384:blk * 384 + 256].rearrange("p (a b) -> p a b", a=2),
                        AF.Copy, [PSR(bk2)], ["Vp"])
                    act(Vp[0:64, b_ * 3 + 2, :], pb[0:64, blk * 384 + 256:blk * 384 + 384], AF.Copy, [PSR(bk2)], ["Vp"])
            sb, sres = acquire()
            for tg in range(NG):
                bk = nextbank(0, 4)
                inproj_fm(sb, sres, 0, tg, bk)
                act(qsil[:, tg * 512:(tg + 1) * 512], ps[bk][:, :], AF.Silu, [PSR(bk)], [("qsil", tg)])
                bk = nextbank(0, 4)
                inproj_fm(sb, sres, 384, tg, bk)
                act(gsil[:, tg * 512:(tg + 1) * 512], ps[bk][:, :], AF.Silu, [PSR(bk)], [("gsil", tg)])
            def mk():
                d_ = dict(kraw=alloc([128, 256]), G=alloc([128, 256]), tA=alloc([128, 256]),
                          tB=alloc([128, 256]), kt=alloc([128, 256], BF16),
                          kh=alloc([128, 256], BF16), Khp=alloc([128, 3, 128], BF16), mref=alloc([128, 8]),
                          refn=alloc([128, 8]), bend=alloc([128, 1]))
                d_["Lg"] = d_["tB"]
                return d_

            def mkp():
                return dict(qt=alloc([128, 256], BF16), Dn=alloc([128, 8]), tin=alloc([128, 1]))
            BUF = [[mk() for _ in range(2)] for _ in range(2)]
            PERS = [[mkp() for _ in range(3)] for _ in range(2)]
            Sp = [[alloc([128, 128]) for _ in range(2)] for _ in range(2)]
            Spb = [[alloc([128, 128], BF16) for _ in range(4)] for _ in range(2)]
            send = [[alloc([128, 128]) for _ in range(2)] for _ in range(2)]

            def prep(d, step):
                b = step if d == 0 else NB - 1 - step
                B_ = BUF[d][step % 2]
                tag = (d, step % 2)
                R = lambda nm: ((nm, d, step % 3) if nm in ("qt", "Dn", "tin") else ((("tB",) + tag) if nm == "Lg" else (nm,) + tag))
                Pp = PERS[d][step % 3]
                tk = slice(b * 256, (b + 1) * 256)
                pbk = 7 if d == 0 else 3
                for kc in range(16):
                    mm(ps[pbk][:, 0:256], sb[:, kc, 128 + 128 * d:256 + 128 * d], hT[:, kc, tk], kc == 0, kc == 15, [sres], [PSR(pbk)])
                sg = B_["tA"]
                act(sg, ps[pbk][:, 0:256], AF.Sigmoid, [PSR(pbk)], [R("tA")], scale=-1.0)
                yield
                ts(B_["kraw"], sg, lbv[:, d, 0, h:h + 1], None, ALU.mult, None, [R("tA"), "lbv"], [R("kraw")])
                ts(sg, B_["kraw"], -1.0, 1.0, ALU.mult, ALU.add, [R("kraw")], [R("tA")])
                ts(sg, sg, 1e-6, None, ALU.max, None, [R("tA")], [R("tA")])
                yield
                act(B_["Lg"], sg, AF.Ln, [R("tA")], [R("Lg")])
                yield
                G = B_["G"]
                P.op(DVE, lambda e, G=G, Lg=B_["Lg"]: e.tensor_tensor_scan(out=G, data0=Lg, data1=zc.broadcast_to([128, 256]), initial=0.0,
                                                                            op0=ALU.add, op1=ALU.add), [R("Lg"), "zc"], [R("G")])
                mref, refn = B_["mref"], B_["refn"]
                if d == 0:
                    cp(mref, G[:, 15:256:32], [R("G")], [R("mref")])
                    cp(refn[:, 0:7], mref[:, 1:8], [R("mref")], [R("refn")])
                    cp(refn[:, 7:8], G[:, 255:256], [R("G"), R("refn")], [R("refn")])
                    tt(Pp["tin"], mref[:, 0:1], flags[:, 0, b:b + 1], ALU.add, [R("mref"), "flags"], [R("tin")])
                else:
                    ts(B_["bend"], G[:, 255:256], -1.0, None, ALU.mult, None, [R("G")], [R("bend")])
                    tt(G, B_["Lg"], G, ALU.subtract, [R("Lg"), R("G")], [R("G")])
                    cp(mref, G[:, 16:256:32], [R("G")], [R("mref")])
                    cp(refn[:, 1:8], mref[:, 0:7], [R("mref")], [R("refn")])
                    cp(refn[:, 0:1], G[:, 0:1], [R("G"), R("refn")], [R("refn")])
                    tt(Pp["tin"], mref[:, 7:8], B_["bend"], ALU.subtract, [R("mref"), R("bend")], [R("tin")])
                    tt(Pp["tin"], Pp["tin"], flags[:, 1, b:b + 1], ALU.add, [R("tin"), "flags"], [R("tin")])
                tt(Pp["Dn"], refn, mref, ALU.subtract, [R("refn"), R("mref")], [R("Dn")])
                a_, b_ = B_["tA"], B_["tB"]
                g3 = G.rearrange("p (c j) -> p c j", j=32)
                tt(a_.rearrange("p (c j) -> p c j", j=32), g3, mref.unsqueeze(2).broadcast_to([128, 8, 32]),
                   ALU.subtract, [R("G"), R("mref")], [R("tA")])
                ts(a_, a_, 41.0, -41.0, ALU.min, ALU.max, [R("tA")], [R("tA")])
                yield
                act(Pp["tin"], Pp["tin"], AF.Exp, [R("tin")], [R("tin")])
                act(Pp["Dn"], Pp["Dn"], AF.Exp, [R("Dn")], [R("Dn")])
                act(b_, a_, AF.Exp, [R("tA")], [R("tB")], bias=LNCQ)
                yield
                tt(Pp["qt"], qsil[:, tk], b_, ALU.mult, [R("tB"), ("qsil", b // 2)], [R("qt")])
                yield
                act(b_, a_, AF.Exp, [R("tA")], [R("tB")], scale=-1.0)
                yield
                tt(B_["kt"], B_["kraw"], b_, ALU.mult, [R("kraw"), R("tB")], [R("kt")])
                tt(a_.rearrange("p (c j) -> p c j", j=32), g3, refn.unsqueeze(2).broadcast_to([128, 8, 32]),
                   ALU.subtract, [R("G"), R("refn")], [R("tA")])
                yield
                act(b_, a_, AF.Exp, [R("tA")], [R("tB")], scale=-1.0)
                yield
                tt(B_["kh"], B_["kraw"], b_, ALU.mult, [R("kraw"), R("tB")], [R("kh")])
                yield
                pb = ps[pbk][:, :].bitcast(BF16)
                for cc in range(8):
                    q_, slot_ = cc % 3, cc // 3
                    tr(pb[32 * q_:32 * q_ + 32, slot_ * 128:(slot_ + 1) * 128], B_["kh"][:, cc * 32:(cc + 1) * 32], identb,
                       [R("kh"), "identb"], [PSR(pbk)])
                act(B_["Khp"][0:96, 0:2, :], pb[0:96, 0:256].rearrange("p (a b) -> p a b", a=2), AF.Copy, [PSR(pbk)], [R("Khp")])
                act(B_["Khp"][0:64, 2, :], pb[0:64, 256:384], AF.Copy, [PSR(pbk)], [R("Khp")])
                yield

            for d in range(2):
                ld(f"s0{d}", send[d][0], s0_d[l, d, h, :, :], [("send", d, 0)])
            nsend = [1, 1]
            NI = NB * 8

            Asb = [[alloc([128, 3, 32], BF16) for _ in range(2)] for _ in range(2)]

            def batch(step):
                for d in range(2):
                    b = step if d == 0 else NB - 1 - step
                    B_ = BUF[d][step % 2]
                    tag = (d, step % 2)
                    for cl in range(8):
                        q_ = cl % 3
                        pr = slice(32 * q_, 32 * q_ + 32)
                        mm(ps[q_][:, (cl // 3) * 128:(cl // 3 + 1) * 128], B_["Khp"][pr, cl // 3, :], Vp[pr, b * 3 + cl // 3, :], True, True,
                           [("Khp",) + tag, "Vp"], [PSR(q_)])
                    for q_ in range(3):
                        n_ = 3 if q_ < 2 else 2
                        act(KVsb[d][step % 2][:, q_:8:3, :], ps[q_][:, 0:n_ * 128].rearrange("p (a b) -> p a b", a=n_), AF.Copy,
                            [PSR(q_)], [("KVsb",) + tag, ("mxs", 0), ("mxs", 1), ("rA", 0), ("rA", 1), ("rB", 0), ("rB", 1)])
                    for cl in range(8):
                        pr = slice(32 * (cl % 3), 32 * (cl % 3) + 32)
                        cs = slice(cl * 32, (cl + 1) * 32)
                        aslot = ps[4][pr, (d * 3 + cl // 3) * 32:(d * 3 + cl // 3 + 1) * 32]
                        mm(aslot, B_["kt"][:, cs], PERS[d][step % 3]["qt"][:, cs], True, True, [("kt",) + tag, ("qt", d, step % 3)], [PSR(4)])
                for d in range(2):
                    tag = (d, step % 2)
                    A_ = Asb[d][step % 2]
                    tt(A_[0:64, 0:3, :], ps[4][0:64, d * 96:d * 96 + 96].rearrange("p (a b) -> p a b", a=3),
                       tri[0:64, d, :].unsqueeze(1).broadcast_to([64, 3, 32]), ALU.mult, [PSR(4), "tri"], [("Asb",) + tag])
                    tt(A_[64:96, 0:2, :], ps[4][64:96, d * 96:d * 96 + 64].rearrange("p (a b) -> p a b", a=2),
                       tri[64:96, d, :].unsqueeze(1).broadcast_to([32, 2, 32]), ALU.mult, [PSR(4), "tri"], [("Asb",) + tag])

            def run_interleaved(gens):
                gens = list(gens)
                waiting = []
                while gens or waiting:
                    if not gens:
                        gens, waiting = waiting, []
                    for g in list(gens):
                        try:
                            r = next(g)
                            if r == "WAITPREP":
                                gens.remove(g)
                                waiting.append(g)
                        except StopIteration:
                            gens.remove(g)

            NI = NB * 8

            def chain_gen(step, d):
                for ci in range(8):
                    i = step * 8 + ci
                    if True:
                        cl = ci if d == 0 else 7 - ci
                        b = step if d == 0 else NB - 1 - step
                        c = b * 8 + cl
                        B_ = BUF[d][step % 2]
                        Pp = PERS[d][step % 3]
                        ptag = (d, step % 3)
                        tag = (d, step % 2)
                        pr = slice(32 * (cl % 3), 32 * (cl % 3) + 32)
                        cs = slice(cl * 32, (cl + 1) * 32)
                        A = Asb[d][step % 2][pr, cl // 3, :]
                        kvs = KVsb[d][step % 2][:, cl, :]
                        grp = c // 16
                        ob = 5 + d
                        oslice = ps[ob][:, (c % 16) * 32:(c % 16 + 1) * 32]
                        mm(oslice, Spb[d][i % 4], Pp["qt"][:, cs], True, False, [("Spb", d, i % 4), ("qt",) + ptag], [PSR(ob)])
                        mm(oslice, Vp[pr, b * 3 + cl // 3, :], A, False, True, ["Vp", ("Asb",) + tag], [PSR(ob)])
                        boundary = (ci == 7)
                        last = (i == NI - 1)
                        nxt = (i + 1) % 2
                        dn = Pp["Dn"][:, cl:cl + 1]
                        if boundary:
                            sd_ = send[d][nsend[d] % 2]
                            sres_ = ("send", d, nsend[d] % 2)
                            nsend[d] += 1
                            stt(sd_, Sp[d][i % 2], dn, kvs, ALU.mult, ALU.add, [("Sp", d, i % 2), ("Dn",) + ptag, ("KVsb",) + tag], [sres_])
                            st(f"so{d}", ost_d[l, b, d, h, :, :], sd_, [sres_])
                            if not last:
                                ntag = (d, (step + 1) % 3)
                                ts(Sp[d][nxt], sd_, PERS[d][(step + 1) % 3]["tin"], None, ALU.mult, None, [sres_, ("tin",) + ntag], [("Sp", d, nxt)])
                        else:
                            stt(Sp[d][nxt], Sp[d][i % 2], dn, kvs, ALU.mult, ALU.add, [("Sp", d, i % 2), ("Dn",) + ptag, ("KVsb",) + tag], [("Sp", d, nxt)])
                        if not last:
                            act(Spb[d][(i + 1) % 4], Sp[d][nxt], AF.Copy, [("Sp", d, nxt)], [("Spb", d, (i + 1) % 4)])
                        gdone = (c % 16 == 15) if d == 0 else (c % 16 == 0)
                        if gdone:
                            fstep = grp * 16 + 15
                            bstep = NI - 1 - grp * 16
                            mine = fstep if d == 0 else bstep
                            other = bstep if d == 0 else fstep
                            first = mine < other or (mine == other and d == 0)
                            osl = o_acc[:, grp * 512:(grp + 1) * 512]
                            if first:
                                act(osl, ps[ob][:, :], AF.Copy, [PSR(ob)], [("oacc", grp)])
                            else:
                                tt(osl, ps[ob][:, :], osl, ALU.add, [PSR(ob), ("oacc", grp)], [("oacc", grp)])
                        yield

            g0 = [prep(0, 0), prep(1, 0)]
            if NB > 1:
                g0 += [prep(0, 1), prep(1, 1)]
            run_interleaved(g0)
            for d in range(2):
                ts(Sp[d][0], send[d][0], PERS[d][0]["tin"], None, ALU.mult, None, [("send", d, 0), ("tin", d, 0)], [("Sp", d, 0)])
                act(Spb[d][0], Sp[d][0], AF.Copy, [("Sp", d, 0)], [("Spb", d, 0)])
            batch(0)
            for step in range(NB):
                if step + 1 < NB:
                    batch(step + 1)
                gens = [chain_gen(step, 0), chain_gen(step, 1)]
                if step + 2 < NB:
                    gens += [prep(0, step + 2), prep(1, step + 2)]
                run_interleaved(gens)
            P.barrier()
            save_top = top[0]
            top[0] = scr_top
            mxs = [alloc([128, 512], BF16) for _ in range(2)]
            rA = [alloc([128, 512]) for _ in range(2)]
            rB = [alloc([128, 512]) for _ in range(2)]
            assert top[0] <= scr_end
            top[0] = save_top
            for tg in range(NG):
                sl = slice(tg * 512, (tg + 1) * 512)
                a = rA[tg % 2]
                b = rB[tg % 2]
                act(a, o_acc[:, sl], AF.Square, [("oacc", tg)], [("rA", tg % 2)])
                bk = 7
                mm(ps[bk][:, :], ones1, a, True, True, [("rA", tg % 2), "ones1"], [PSR(bk)])
                act(b, ps[bk][:, :], AF.Sqrt, [PSR(bk)], [("rB", tg % 2)], bias=1e-6, scale=1.0 / 128.0)
                recip(b, b, [("rB", tg % 2)], [("rB", tg % 2)])
                tt(a, o_acc[:, sl], b, ALU.mult, [("oacc", tg), ("rB", tg % 2)], [("rA", tg % 2)])
                stt(mxs[tg % 2], a, hgnw[:, l, h:h + 1], gsil[:, sl], ALU.mult, ALU.mult,
                    [("rA", tg % 2), "hgnw", ("gsil", tg)], [("mxs", tg % 2)])
                st(f"mx{tg % 2}", mix_d[8 + h, :, sl], mxs[tg % 2], [("mxs", tg % 2)])
        P.barrier()
        top[0] = layer_top
        if stop_after == "hgrn":
            raise StopBuild()

        kr = alloc([128, 2, T], BF16)
        Vt = alloc([128, NT, 256], BF16)
        ckT = alloc([128, 2, 256], BF16)
        cvb = alloc([128, 2, 256], BF16)
        amask = alloc([128, NQB, 2, 128], BF16)
        esink = alloc([128, 8])
        qr = alloc([128, 4, T], BF16)
        att_top = top[0]
        ckf = alloc([128, 2, 256])
        cvf = alloc([128, 2, 256])
        cst = [alloc([128, 512]) for _ in range(2)]
        snt = [alloc([128, 512]) for _ in range(2)]
        zq = [alloc([128, 512]) for _ in range(2)]
        r1 = [alloc([128, 512]) for _ in range(2)]
        r2 = [alloc([128, 512]) for _ in range(2)]
        ktok = [alloc([128, 4, 128]) for _ in range(2)]
        vtok = [alloc([128, 256]) for _ in range(2)]
        top[0] = att_top
        PT = [alloc([128, 5, 512], BF16) for _ in range(2)]
        rec = [alloc([128, 512]) for _ in range(2)]
        gst = [alloc([128, 512], BF16) for _ in range(2)]

        for half in range(0, NQB, 8):
            hi_ = min(half + 8, NQB)
            dst = amask[:, half:hi_, :, :]
            src = amask_d[:, half:hi_, :, :]
            P.dma(POOL, "am", lambda e, dst=dst, src=src: e.dma_start(out=dst, in_=src), (), ["amask"])
        ld("c1", ckf, ck_d[l].rearrange("(kb p) c -> p kb c", p=128), ["ckf"])
        ld("c2", cvf, cv_d[l].rearrange("(kb p) c -> p kb c", p=128), ["cvf"])
        cp(cvb, cvf, ["cvf"], ["cvb"])
        ld("c3", esink, sink_d[:, l, :], ["esink"])
        act(esink, esink, AF.Exp, ["esink"], ["esink"])
        for kb in range(2):
            for g in range(2):
                bk = nextbank()
                tr(ps[bk][:, 0:128], ckf[:, kb, g * 128:(g + 1) * 128], identf, ["ckf", "identf"], [PSR(bk)])
                act(ckT[:, g, kb * 128:(kb + 1) * 128], ps[bk][:, 0:128], AF.Copy, [PSR(bk)], ["ckT"])

        chk('attn_a')
        ropecnt = [0]

        def rope(src_bank, dst, tg):
            i = ropecnt[0] % 2
            ropecnt[0] += 1
            sl = slice(tg * 512, (tg + 1) * 512)
            ld(f"cs{i}", cst[i], cos_d[:, sl], [("cst", i)])
            ld(f"sn{i}", snt[i], sin_d[:, sl], [("snt", i)])
            act(zq[i], ps[src_bank][:, :], AF.Copy, [PSR(src_bank)], [("zq", i)])
            bk2 = nextbank(4, 8)
            mm(ps[bk2][:, :], rm, zq[i], True, True, ["rm", ("zq", i)], [PSR(bk2)])
            tt(r1[i], zq[i], cst[i], ALU.mult, [("zq", i), ("cst", i)], [("r1", i)])
            tt(r2[i], ps[bk2][:, :], snt[i], ALU.mult, [PSR(bk2), ("snt", i)], [("r2", i)])
            tt(dst, r1[i], r2[i], ALU.add, [("r1", i), ("r2", i)], ["ropeout"])
            return i

        sb, sres = acquire()
        for kvh in range(2):
            for tg in range(NG):
                bk = nextbank(0, 4)
                inproj_fm(sb, sres, kvh * 128, tg, bk)
                i = rope(bk, kr[:, kvh, tg * 512:(tg + 1) * 512], tg)
                bk3 = nextbank(4, 8)
                for a in range(4):
                    tr(ps[bk3][:, a * 128:(a + 1) * 128], zq[i][:, a * 128:(a + 1) * 128], identf, [("zq", i), "identf"], [PSR(bk3)])
                kt_ = ktok[(kvh * NG + tg) % 2]
                kres = ("ktok", (kvh * NG + tg) % 2)
                act(kt_, ps[bk3][:, :].rearrange("p (a b) -> p a b", a=4), AF.Copy, [PSR(bk3)], [kres])
                st(f"ko{(kvh * NG + tg) % 2}", ok_d[l, tg * 512:(tg + 1) * 512, kvh * 128:(kvh + 1) * 128].rearrange("(a p) d -> p a d", p=128),
                   kt_, [kres])
        chk('attn_b')
        for tt_ in range(NT):
            bk = nextbank(0, 4)
            for kc in range(16):
                mm(ps[bk][:, 0:256], hT[:, kc, tt_ * 128:(tt_ + 1) * 128], sb[:, kc, 256:512], kc == 0, kc == 15, [sres], [PSR(bk)])
            vt_ = vtok[tt_ % 2]
            act(vt_, ps[bk][:, 0:256], AF.Copy, [PSR(bk)], [("vtok", tt_ % 2)])
            cp(Vt[:, tt_, :], vt_, [("vtok", tt_ % 2)], ["Vt"])
            st(f"vo{tt_ % 2}", ov_d[l, tt_ * 128:(tt_ + 1) * 128, :], vt_, [("vtok", tt_ % 2)])

        chk('attn_kv')
        SCALE = 128 ** -0.5
        P.barrier()
        for g in range(2):
            sb, sres = acquire()
            for hh in range(4):
                for tg in range(NG):
                    bk = nextbank(0, 4)
                    inproj_fm(sb, sres, hh * 128, tg, bk)
                    rope(bk, qr[:, hh, tg * 512:(tg + 1) * 512], tg)
            P.barrier()
            chk('attn_q')
            def slots_of(n):
                kbL = max(n - 1, 0)
                kbR = min(n + 1, NQB - 1)
                return [(kr[:, g, kbL * 128:(kbL + 1) * 128], Vt[:, kbL, g * 128:(g + 1) * 128], 0, onesb),
                        (kr[:, g, n * 128:(n + 1) * 128], Vt[:, n, g * 128:(g + 1) * 128], None, onesb),
                        (kr[:, g, kbR * 128:(kbR + 1) * 128], Vt[:, kbR, g * 128:(g + 1) * 128], 1, onesb),
                        (ckT[:, g, 0:128], cvb[:, 0, g * 128:(g + 1) * 128], None, ctxones),
                        (ckT[:, g, 128:256], cvb[:, 1, g * 128:(g + 1) * 128], None, ctxones)]

            def S_part(n):
                pt = PT[n % 2]
                pres = ("PT", n % 2)
                qsl = qr[:, :, n * 128:(n + 1) * 128]
                for si, (kT, vv, mside, onesl) in enumerate(slots_of(n)):
                    bk = nextbank(0, 4)
                    mm(ps[bk][:, :], kT, qsl, True, True, ["kr", "ckT", ("qrn", n)], [PSR(bk)])
                    act(pt[:, si, :], ps[bk][:, :], AF.Exp, [PSR(bk)], [pres], scale=SCALE)
                    if mside is not None:
                        v4 = pt[:, si, :].rearrange("p (h q) -> p h q", h=4)
                        tt(v4, v4, amask[:, n, mside, :].unsqueeze(1).broadcast_to([128, 4, 128]), ALU.mult, [pres, "amask"], [pres])

            def F_part(n):
                pt = PT[n % 2]
                pres = ("PT", n % 2)
                qsl = qr[:, :, n * 128:(n + 1) * 128]
                slots = slots_of(n)
                pvb = 4 + 2 * (n % 2)
                smb = pvb + 1
                for si, (kT, vv, mside, onesl) in enumerate(slots):
                    mm(ps[pvb][:, :], vv, pt[:, si, :], si == 0, si == 4, [pres, "Vt", "cvb"], [PSR(pvb)])
                for si, (kT, vv, mside, onesl) in enumerate(slots):
                    mm(ps[smb][:, :], onesl, pt[:, si, :], si == 0, si == 4, [pres, "onesb", "ctxones"], [PSR(smb)])
                rc = rec[n % 2]
                tt(rc.rearrange("p (h q) -> p h q", h=4), ps[smb][:, :].rearrange("p (h q) -> p h q", h=4),
                   esink[:, g * 4:(g + 1) * 4].unsqueeze(2).broadcast_to([128, 4, 128]), ALU.add, [PSR(smb), "esink"], [("rec", n % 2)])
                recip(rc, rc, [("rec", n % 2)], [("rec", n % 2)])
                tt(qsl, ps[pvb][:, :].rearrange("p (h q) -> p h q", h=4), rc.rearrange("p (h q) -> p h q", h=4), ALU.mult,
                   [PSR(pvb), ("rec", n % 2)], [("qrn", n)])

            S_part(0)
            for n in range(NQB):
                if n + 1 < NQB:
                    S_part(n + 1)
                F_part(n)
            P.barrier()
            chk('attn_s')
            sb, sres = acquire()
            for hh in range(4):
                for tg in range(NG):
                    bk = nextbank(0, 4)
                    inproj_fm(sb, sres, hh * 128, tg, bk)
                    gs_ = gst[tg % 2]
                    act(gs_, ps[bk][:, :], AF.Silu, [PSR(bk)], [("gst", tg % 2)])
                    qs_ = qr[:, hh, tg * 512:(tg + 1) * 512]
                    tt(qs_, qs_, gs_, ALU.mult, [("gst", tg % 2), "qr"], ["qr"])
            st("mx0", mix_d[g * 4:(g + 1) * 4, :, :].rearrange("k p t -> p k t"), qr, ["qr"])
            P.barrier()
        top[0] = layer_top
        if stop_after == "attn":
            raise StopBuild()

        top[0] = const_top + 2 * (16 * SLABW * 2)
        wo = alloc([128, 16, 2048], BF16)
        lnw_b = alloc([128, 2048])
        lnb_b = alloc([128, 2048])
        gate_b = alloc([128, 2048])
        screp = alloc([128, 16, 128], BF16)
        cp(screp, sc_bf.unsqueeze(2).broadcast_to([128, 16, 128]), ["sc_bf"], ["screp"])
        ld("c2", lnb_b, adabg_d[l:l + 1, :].partition_broadcast(128), ["lnb_b"])
        for s_ in range(4):
            dst = wo[:, :, s_ * 512:(s_ + 1) * 512]
            src = wout_d[l][:, s_ * 512:(s_ + 1) * 512]
            P.dma(POOL, f"wo{s_ % 2}", lambda e, dst=dst, src=src: e.dma_start(out=dst, in_=src.rearrange("(kc p) n -> p kc n", p=128)),
                  (), [("wo", s_)])
            sb, sres = acquire()
            bk = nextbank(0, 4)
            for kc in range(16):
                mm(ps[bk][:, :], screp[:, kc, :], sb[:, kc, 0:512], kc == 0, kc == 15, [sres, "screp"], [PSR(bk)])
            tt(gate_b[:, s_ * 512:(s_ + 1) * 512], ps[bk][:, :], lnb_b[:, s_ * 512:(s_ + 1) * 512], ALU.add,
               [PSR(bk), "lnb_b"], [("gate_b", s_)])
        ld("c0", lnw_b, lnw_d[l:l + 1, :].partition_broadcast(128), ["lnw_b"])
        ld("c1", lnb_b, lnb_d[l:l + 1, :].partition_broadcast(128), ["lnb_b"])
        mixt = [alloc([128, 16, 128], BF16) for _ in range(2)]
        xt2 = [alloc([128, 2048]) for _ in range(2)]
        tm = [alloc([128, 512]) for _ in range(2)]
        stats = [alloc([128, 24]) for _ in range(2)]
        mv = [alloc([128, 2]) for _ in range(2)]
        sdv = [alloc([128, 2]) for _ in range(2)]

        def tile_gen(tile_i):
            p_ = tile_i % 2
            mt = mixt[p_]
            mres = ("mixt", p_)
            ld(f"mt{p_}", mt, mix_d[:, :, tile_i * 128:(tile_i + 1) * 128].rearrange("k p t -> p k t"), [mres])
            xt = xt2[p_]
            yres = ("xt2", p_)
            ld(f"x{p_}", xt, xin_d[tile_i * 128:(tile_i + 1) * 128, :], [yres])
            yield
            st_, mv_, sd_ = stats[p_], mv[p_], sdv[p_]
            for cg in range(4):
                bk = cg + 4 * p_
                for kc in range(16):
                    mm(ps[bk][:, :], mt[:, kc, :], wo[:, kc, cg * 512:(cg + 1) * 512], kc == 0, kc == 15,
                       [mres, ("wo", cg)], [PSR(bk)])
                csl = slice(cg * 512, (cg + 1) * 512)
                tm_ = tm[cg % 2]
                tt(tm_, ps[bk][:, :], gate_b[:, csl], ALU.mult, [PSR(bk), ("gate_b", cg)], [("tm", cg % 2)])
                yield
                stt(xt[:, csl], xt[:, csl], ALPHA, tm_, ALU.mult, ALU.add, [yres, ("tm", cg % 2)], [yres])
                yield
                P.op(DVE, lambda e, o=st_[:, cg * 6:(cg + 1) * 6], i_=xt[:, csl]: e.bn_stats(out=o, in_=i_), [yres], [("stats", p_)])
                yield
            yield "EPI"
            P.op(DVE, lambda e: e.bn_aggr(out=mv_, in_=st_), [("stats", p_)], [("mv", p_)])
            yield
            act(sd_[:, 0:1], mv_[:, 1:2], AF.Sqrt, [("mv", p_)], [("sdv", p_)], bias=1e-5)
            yield
            recip(sd_[:, 0:1], sd_[:, 0:1], [("sdv", p_)], [("sdv", p_)])
            yield
            ts(sd_[:, 1:2], mv_[:, 0:1], sd_[:, 0:1], -1.0, ALU.mult, ALU.mult, [("mv", p_), ("sdv", p_)], [("sdv", p_)])
            yield
            act(xt, xt, AF.Identity, [yres, ("sdv", p_)], [yres], bias=sd_[:, 1:2], scale=sd_[:, 0:1])
            yield
            tt(xt, xt, lnw_b, ALU.mult, [yres, "lnw_b"], [yres])
            yield
            tt(xt, xt, lnb_b, ALU.add, [yres, "lnb_b"], [yres])
            yield
            st(f"yo{p_}", xout_d[tile_i * 128:(tile_i + 1) * 128, :], xt, [yres])
            yield

        active = [tile_gen(0)]
        nxt_tile = 1
        mg = mod_gen(l + 1) if l + 1 < layers else None
        while active:
            for g_ in list(active):
                try:
                    r_ = next(g_)
                    if r_ == "EPI":
                        if mg is not None:
                            try:
                                next(mg)
                            except StopIteration:
                                mg = None
                        if nxt_tile < NT:
                            active.append(tile_gen(nxt_tile))
                            nxt_tile += 1
                except StopIteration:
                    active.remove(g_)
        if mg is not None:
            for _ in mg:
                pass
        P.barrier()

    try:
        for l in range(layers):
            layer_body(l)
    except StopBuild:
        pass
    P.finish()
    P.emit()
    nc._maxtop = maxtop[0]
    return nc


def _fm(v):
    v = np.asarray(v, np.float32)
    lead = v.shape[:-1]
    n = v.shape[-1] // 128
    v = v.reshape(lead + (n, 128))
    return np.ascontiguousarray(np.moveaxis(v, -1, 0))


def rope_tables(T, sample):
    d = np.arange(128)
    if not sample:
        return np.ones((128, T), np.float32), np.zeros((128, T), np.float32)
    t = np.arange(T)
    row = (t // 64).astype(np.float32)
    col = (t % 64).astype(np.float32)
    nf = 32
    inv = (np.float32(10000.0) ** (-np.arange(nf, dtype=np.float32) / np.float32(nf))).astype(np.float32)
    pos = np.where((d < 64)[:, None], row[None, :], col[None, :]).astype(np.float32)
    ang = (pos * inv[d % 32][:, None]).astype(np.float32)
    return np.cos(ang).astype(np.float32), np.sin(ang).astype(np.float32)


def rot_matrix():
    R = np.zeros((128, 128), np.float32)
    for m in range(128):
        q = (m % 64) // 32
        if q == 0:
            R[m, m + 32] = -1.0
        else:
            R[m, m - 32] = 1.0
    return np.ascontiguousarray(R.T)


def core_consts(NB, sample):
    T = NB * 256
    NQB = T // 128
    cosT, sinT = rope_tables(T, sample)
    amask = np.zeros((128, NQB, 2, 128), np.float32)
    j = np.arange(128)[:, None]
    i = np.arange(128)[None, :]
    low = (j >= i).astype(np.float32)
    up = (j <= i).astype(np.float32)
    for n in range(NQB):
        if sample:
            if n >= 1:
                amask[:, n, 0, :] = low
            if n <= NQB - 2:
                amask[:, n, 1, :] = up
        else:
            if n % 2 == 0:
                amask[:, n, 1, :] = 1.0
            else:
                amask[:, n, 0, :] = 1.0
    ctxones = np.full((128, 128), 1.0 if sample else 0.0, np.float32)
    flags = np.zeros((128, 4, NB), np.float32)
    if sample:
        flags[:, 2, 1:] = 1.0
        flags[:, 3, :NB - 1] = 1.0
    else:
        flags[:, 0, 1:] = -1e4
        flags[:, 1, :NB - 1] = -1e4
    tri = np.zeros((128, 2, 32), np.float32)
    s = (np.arange(128) % 32)[:, None]
    t = np.arange(32)[None, :]
    tri[:, 0, :] = (s <= t)
    tri[:, 1, :] = (s >= t)
    return dict(cosT=cosT, sinT=sinT, Rm=rot_matrix(), amask=amask, ctxones=ctxones, flags=flags, tri=tri)


def shared_inputs(ada_w, ada_b, w_in, w_out, attn_sink, hg_lower_bounds, hg_norm_w, conv_w, conv_b, conv_ln_w, conv_ln_b, ln_w, ln_b):
    f = lambda a: np.ascontiguousarray(np.asarray(a, np.float32))
    ada_b = f(ada_b)
    d = {}
    d["ada_w"] = f(ada_w)
    d["adab_fm"] = np.ascontiguousarray(_fm(ada_b[:, :4096]))
    d["adab_g"] = np.ascontiguousarray(ada_b[:, 4096:])
    d["w_in"] = f(w_in)
    d["w_out"] = f(w_out)
    d["sinkrep"] = np.ascontiguousarray(np.broadcast_to(f(attn_sink).reshape(1, 2, 8), (128, 2, 8)))
    d["lbraw"] = _fm(f(hg_lower_bounds).reshape(2, 2, 512))
    d["hgnw"] = _fm(hg_norm_w)
    d["convw"] = np.ascontiguousarray(np.transpose(_fm(conv_w), (0, 1, 3, 2)))
    d["convp"] = np.ascontiguousarray(np.stack([_fm(conv_b), _fm(conv_ln_w), _fm(conv_ln_b)], axis=1))
    d["lnw"] = f(ln_w)
    d["lnb"] = f(ln_b)
    return d


NBK = 8
_CACHE = {}


def kernel(x_prompt, x_sample, cache_k, cache_v, state_hgrn, c, c_ctx, ada_w, ada_b, w_in, w_out,
           attn_sink, hg_lower_bounds, hg_norm_w, conv_w, conv_b, conv_ln_w, conv_ln_b, ln_w, ln_b):
    f = lambda a: np.ascontiguousarray(np.asarray(a, np.float32))
    x_prompt = f(x_prompt)
    x_sample = f(x_sample)
    NB = NBK
    T = NB * 256
    shared = shared_inputs(ada_w, ada_b, w_in, w_out, attn_sink, hg_lower_bounds, hg_norm_w, conv_w, conv_b,
                           conv_ln_w, conv_ln_b, ln_w, ln_b)
    cs = core_consts(NB, True)
    cp_ = core_consts(NB, False)
    assign = {2: list(range(0, 6)), 3: list(range(6, 12)), 4: list(range(12, 17)), 5: list(range(17, 22)),
              6: list(range(22, 27)), 7: list(range(27, 32))}
    in_maps = []
    for core in range(8):
        m = dict(shared)
        if core < 2:
            m.update(cs)
            m["x"] = np.ascontiguousarray(x_sample[core])
            m["cond"] = np.ascontiguousarray(f(c)[core].reshape(16, 128).T)
            m["ck"] = np.ascontiguousarray(f(cache_k)[core].reshape(2, 256, 256))
            m["cv"] = np.ascontiguousarray(f(cache_v)[core].reshape(2, 256, 256))
            m["s0"] = np.ascontiguousarray(f(state_hgrn)[core])
        else:
            m.update(cp_)
            xs = np.zeros((T, D), np.float32)
            for i, s in enumerate(assign[core]):
                xs[i * 256:(i + 1) * 256] = x_prompt[s]
            m["x"] = xs
            m["cond"] = np.ascontiguousarray(f(c_ctx).reshape(16, 128).T)
            m["ck"] = np.zeros((2, 256, 256), np.float32)
            m["cv"] = np.zeros((2, 256, 256), np.float32)
            m["s0"] = np.zeros((2, 2, 4, 128, 128), np.float32)
        in_maps.append(m)
    if "nc" not in _CACHE:
        _CACHE["nc"] = build(NB)
    res = run_bass_kernel_spmd(_CACHE["nc"], in_maps, core_ids=list(range(8))).results
    y_prompt = np.zeros((32, 256, D), np.float32)
    y_sample = np.zeros((2, 2048, D), np.float32)
    nk = np.zeros((32, 2, 256, 2, 128), np.float32)
    nv = np.zeros((32, 2, 256, 2, 128), np.float32)
    nst = np.zeros((32, 2, 2, 4, 128, 128), np.float32)
    for core in range(8):
        r = res[core]
        if core < 2:
            y_sample[core] = r["y"]
        else:
            for i, s in enumerate(assign[core]):
                y_prompt[s] = r["y"][i * 256:(i + 1) * 256]
                nk[s] = r["ok"][:, i * 256:(i + 1) * 256, :].reshape(2, 256, 2, 128)
                nv[s] = r["ov"][:, i * 256:(i + 1) * 256, :].reshape(2, 256, 2, 128)
                nst[s] = r["ost"][:, i]
    return (y_prompt, y_sample, nk, nv, nst)
```
